# Optimizing a Trainium2 kernel written in Bass

```python
import math
import functools
import jax
import jax.numpy as jnp
from jax import lax
import numpy as np

D_MODEL = 1024
BATCH = 4
SEQ = 4096
DEPTH = 1
DEC_BATCH = 128
DEC_SEQ = 4
PAST_LEN = 2048
PAGE_SIZE = 128

HEAD_DIM = 64
ATT_BRANCHES = ((128, 1), (512, 4), (2048, 16))
N_BRANCH = len(ATT_BRANCHES)
HEADS_PER_BRANCH = 4
ATT_HEADS = N_BRANCH * HEADS_PER_BRANCH
D_ATT = ATT_HEADS * HEAD_DIM
SSD_HEADS = 12
SSD_HEAD_DIM = 64
D_INNER = SSD_HEADS * SSD_HEAD_DIM
SSD_GROUPS = 4
SSD_HEADS_PER_GROUP = SSD_HEADS // SSD_GROUPS
SSD_STATE = 128
CONV_WIDTH = 4
CONV_DIM = D_INNER + 2 * SSD_GROUPS * SSD_STATE
SSD_CHUNK = 128
D_MIX = D_ATT + D_INNER
D_IN_TOTAL = 3 * D_ATT + D_INNER + CONV_DIM + SSD_HEADS
D_FF = ((8 * D_MODEL + 3 * 256 - 1) // (3 * 256)) * 256
RMS_EPS = 1e-5
NEG_INF = -1e30
ATT_SCALE = HEAD_DIM ** -0.5

kernel_name = 'hybrid_dilated_attn_ssd_step'


def rmsnorm(x, g):
    xf = x.astype(jnp.float32)
    inv = lax.rsqrt(jnp.mean(xf * xf, axis=-1, keepdims=True) + RMS_EPS)
    return (xf * inv * g.astype(jnp.float32)).astype(x.dtype)


def project(x, norm_mix, w_in):
    b, t = x.shape[:2]
    u = rmsnorm(x, norm_mix) @ w_in
    q, k, v = (u[..., i * D_ATT:(i + 1) * D_ATT].reshape(b, t, N_BRANCH, HEADS_PER_BRANCH, HEAD_DIM)
               for i in range(3))
    o = 3 * D_ATT
    z = u[..., o:o + D_INNER]
    o += D_INNER
    xbc = u[..., o:o + CONV_DIM]
    o += CONV_DIM
    dt_raw = u[..., o:o + SSD_HEADS]
    return q, k, v, z, xbc, dt_raw


def dilated_branch_prompt(q, k, v, window, dil):
    b, t, h, d = q.shape
    band = window // dil
    unit = band * dil
    tp = -(-t // unit) * unit
    nblk = tp // unit

    def to_blocks(a):
        a = jnp.pad(a.astype(jnp.float32), ((0, 0), (0, tp - t), (0, 0), (0, 0)))
        a = a.reshape(b, tp // dil, dil, h, d).transpose(0, 2, 1, 3, 4)
        return a.reshape(b, dil, nblk, band, h, d)

    def with_prev(a):
        prev = jnp.pad(a[:, :, :-1], ((0, 0), (0, 0), (1, 0), (0, 0), (0, 0), (0, 0)))
        return jnp.concatenate([prev, a], axis=3)

    qb = to_blocks(q)
    kk = with_prev(to_blocks(k))
    vv = with_prev(to_blocks(v))
    s = jnp.einsum('brnqhd,brnkhd->brnhqk', qb, kk) * ATT_SCALE
    qi = jnp.arange(band)[:, None]
    kj = jnp.arange(2 * band)[None, :]
    dist = band + qi - kj
    in_band = (dist >= 0) & (dist <= band)
    has_prev = (jnp.arange(nblk)[:, None, None] > 0) | (kj[None] >= band)
    valid = in_band[None] & has_prev
    s = jnp.where(valid[None, None, :, None], s, NEG_INF)
    m = jnp.max(s, axis=-1, keepdims=True)
    p = jnp.exp(s - m)
    den = jnp.sum(p, axis=-1)
    o = jnp.einsum('brnhqk,brnkhd->brnqhd', p, vv) / den.transpose(0, 1, 2, 4, 3)[..., None]
    lse = (m[..., 0] + jnp.log(den)).transpose(0, 1, 2, 4, 3)
    o = o.reshape(b, dil, tp // dil, h, d).transpose(0, 2, 1, 3, 4).reshape(b, tp, h, d)[:, :t]
    lse = lse.reshape(b, dil, tp // dil, h).transpose(0, 2, 1, 3).reshape(b, tp, h)[:, :t]
    return o, lse


def dilated_branch_sample(q, k_all, v_all, window, dil, lb):
    s_len = q.shape[1]
    band = window // dil
    idx = lb + jnp.arange(s_len)[:, None] - dil * jnp.arange(band + 1)[None, :]
    valid = idx >= 0
    idx = jnp.maximum(idx, 0)
    kg = k_all[:, idx].astype(jnp.float32)
    vg = v_all[:, idx].astype(jnp.float32)
    sc = jnp.einsum('bshd,bsjhd->bhsj', q.astype(jnp.float32), kg) * ATT_SCALE
    sc = jnp.where(valid[None, None], sc, NEG_INF)
    m = jnp.max(sc, axis=-1, keepdims=True)
    p = jnp.exp(sc - m)
    den = jnp.sum(p, axis=-1)
    o = jnp.einsum('bhsj,bsjhd->bshd', p, vg) / den.transpose(0, 2, 1)[..., None]
    lse = (m[..., 0] + jnp.log(den)).transpose(0, 2, 1)
    return o, lse


def merge_branches(outs, lses):
    o = jnp.stack(outs, axis=2)
    alpha = jax.nn.softmax(jnp.stack(lses, axis=2), axis=2)
    b, t = o.shape[:2]
    return (o * alpha[..., None]).reshape(b, t, D_ATT)


def attend_prompt(q, k, v):
    t = q.shape[1]
    outs, lses, kv_rows = [], [], []
    for g, (win, dil) in enumerate(ATT_BRANCHES):
        o, l = dilated_branch_prompt(q[:, :, g], k[:, :, g], v[:, :, g], win, dil)
        outs.append(o)
        lses.append(l)
        kv_rows.append(jnp.stack([k[:, :, g], v[:, :, g]], axis=2)[:, t - min(win, t):])
    return merge_branches(outs, lses), kv_rows


def attend_sample(q, k, v, kv_caches):
    s_len = q.shape[1]
    outs, lses, kv_rows = [], [], []
    for g, (win, dil) in enumerate(ATT_BRANCHES):
        cache = kv_caches[g]
        lb = cache.shape[1]
        new_rows = jnp.stack([k[:, :, g], v[:, :, g]], axis=2).astype(cache.dtype)
        kv_all = jnp.concatenate([cache, new_rows], axis=1)
        o, l = dilated_branch_sample(q[:, :, g], kv_all[:, :, 0], kv_all[:, :, 1], win, dil, lb)
        outs.append(o)
        lses.append(l)
        keep = min(win, lb + s_len)
        kv_rows.append(kv_all[:, lb + s_len - keep:])
    return merge_branches(outs, lses), kv_rows


def ssd_scan(xs, dt, a, bm, cm, h0):
    b, t = xs.shape[:2]
    cl = math.gcd(SSD_CHUNK, t)
    nc = t // cl
    G, R, P, N = SSD_GROUPS, SSD_HEADS_PER_GROUP, SSD_HEAD_DIM, SSD_STATE
    x = xs.reshape(b, nc, cl, G, R, P)
    dtc = dt.reshape(b, nc, cl, G, R)
    bc = bm.reshape(b, nc, cl, G, N)
    cc = cm.reshape(b, nc, cl, G, N)
    acum = jnp.cumsum(dtc * a.reshape(G, R), axis=2)
    seg = acum[:, :, :, None] - acum[:, :, None]
    causal = jnp.tril(jnp.ones((cl, cl), dtype=bool))[:, :, None, None]
    lmat = jnp.exp(jnp.where(causal, seg, -jnp.inf))
    cb = jnp.einsum('bclgn,bcsgn->bclsg', cc, bc)
    y_diag = jnp.einsum('bclsgr,bcsgrp->bclgrp', cb[..., None] * lmat * dtc[:, :, None], x)
    decay = jnp.exp(acum[:, :, -1:] - acum)
    states = jnp.einsum('bclgn,bclgr,bclgrp->bcgrpn', bc, decay * dtc, x)
    chunk_decay = jnp.exp(acum[:, :, -1])

    def step(h, inp):
        st, dec = inp
        return dec[..., None, None] * h + st, h

    h_last, h_prev = lax.scan(step, h0.reshape(b, G, R, P, N),
                              (states.transpose(1, 0, 2, 3, 4, 5), chunk_decay.transpose(1, 0, 2, 3)))
    h_prev = h_prev.transpose(1, 0, 2, 3, 4, 5)
    y_off = jnp.einsum('bclgn,bcgrpn,bclgr->bclgrp', cc, h_prev, jnp.exp(acum))
    y = (y_diag + y_off).reshape(b, t, SSD_HEADS, P)
    return y, h_last.reshape(b, SSD_HEADS, P, N)


def ssd_mixer(z, xbc, dt_raw, conv_prev, ssm_prev, conv_w, conv_b, dt_bias, a_log, d_skip, norm_ssd):
    b, t = xbc.shape[:2]
    f32 = jnp.float32
    xpad = jnp.concatenate([conv_prev, xbc.astype(conv_prev.dtype)], axis=1)
    new_conv = xpad[:, -(CONV_WIDTH - 1):]
    xc = lax.conv_general_dilated(xpad.astype(f32), conv_w.astype(f32)[:, None, :], (1,), 'VALID',
                                  dimension_numbers=('NWC', 'WIO', 'NWC'),
                                  feature_group_count=CONV_DIM) + conv_b.astype(f32)
    xc = jax.nn.silu(xc)
    gn = SSD_GROUPS * SSD_STATE
    xs = xc[..., :D_INNER].reshape(b, t, SSD_HEADS, SSD_HEAD_DIM)
    bm = xc[..., D_INNER:D_INNER + gn].reshape(b, t, SSD_GROUPS, SSD_STATE)
    cm = xc[..., D_INNER + gn:].reshape(b, t, SSD_GROUPS, SSD_STATE)
    dt = jax.nn.softplus(dt_raw.astype(f32) + dt_bias.astype(f32))
    a = -jnp.exp(a_log.astype(f32))
    y, h_last = ssd_scan(xs, dt, a, bm, cm, ssm_prev.astype(f32))
    y = y + d_skip.astype(f32)[:, None] * xs
    y = y.reshape(b, t, D_INNER) * jax.nn.silu(z.astype(f32))
    y = rmsnorm(y, norm_ssd)
    return y, new_conv, h_last.astype(ssm_prev.dtype)


def decoder_layer(x, attend, conv_prev, ssm_prev, norm_mix, w_in, conv_w, conv_b, dt_bias, a_log,
                  d_skip, norm_ssd, w_out, norm_ffn, w_gate_up, w_down):
    q, k, v, z, xbc, dt_raw = project(x, norm_mix, w_in)
    att, kv_rows = attend(q, k, v)
    ssd, conv_new, ssm_new = ssd_mixer(z, xbc, dt_raw, conv_prev, ssm_prev, conv_w, conv_b,
                                       dt_bias, a_log, d_skip, norm_ssd)
    mixed = jnp.concatenate([att, ssd], axis=-1).astype(x.dtype)
    h = x + mixed @ w_out
    gu = rmsnorm(h, norm_ffn) @ w_gate_up
    y = h + (jax.nn.silu(gu[..., :D_FF]) * gu[..., D_FF:]) @ w_down
    return y, kv_rows, conv_new, ssm_new


def setup_inputs(seed: int = 0) -> dict:
    key = jax.random.key(seed)
    ks = jax.random.split(key, 20)
    f32 = jnp.float32
    nrm = lambda k, shp, sc=1.0: (jax.random.normal(k, shp, f32) * sc).astype(f32)
    caches = {}
    for i, (win, dil) in enumerate(ATT_BRANCHES):
        caches['cache_kv_d%d' % dil] = nrm(ks[2 + i], (DEPTH, DEC_BATCH, min(win, PAST_LEN), 2,
                                                     HEADS_PER_BRANCH, HEAD_DIM))
    dt0 = jnp.exp(jax.random.uniform(ks[8], (DEPTH, SSD_HEADS), f32,
                                     minval=math.log(1e-3), maxval=math.log(1e-1)))
    return {
        'x_prompt': nrm(ks[0], (BATCH, SEQ, D_MODEL)),
        'x_sample': nrm(ks[1], (DEC_BATCH, DEC_SEQ, D_MODEL)),
        'cache_kv_d1': caches['cache_kv_d1'],
        'cache_kv_d4': caches['cache_kv_d4'],
        'cache_kv_d16': caches['cache_kv_d16'],
        'state_conv': nrm(ks[5], (DEPTH, DEC_BATCH, CONV_WIDTH - 1, CONV_DIM)),
        'state_ssm': nrm(ks[6], (DEPTH, DEC_BATCH, SSD_HEADS, SSD_HEAD_DIM, SSD_STATE), 0.1),
        'norm_mix': 1.0 + nrm(ks[7], (DEPTH, D_MODEL), 0.02),
        'w_in': nrm(ks[9], (DEPTH, D_MODEL, D_IN_TOTAL), D_MODEL ** -0.5),
        'conv_w': nrm(ks[10], (DEPTH, CONV_WIDTH, CONV_DIM), CONV_WIDTH ** -0.5),
        'conv_b': nrm(ks[11], (DEPTH, CONV_DIM), 0.02),
        'dt_bias': dt0 + jnp.log(-jnp.expm1(-dt0)),
        'a_log': jnp.log(jax.random.uniform(ks[12], (DEPTH, SSD_HEADS), f32, minval=1.0, maxval=16.0)),
        'd_skip': 1.0 + nrm(ks[13], (DEPTH, SSD_HEADS), 0.1),
        'norm_ssd': 1.0 + nrm(ks[14], (DEPTH, D_INNER), 0.02),
        'w_out': nrm(ks[15], (DEPTH, D_MIX, D_MODEL), D_MIX ** -0.5),
        'norm_ffn': 1.0 + nrm(ks[16], (DEPTH, D_MODEL), 0.02),
        'w_gate_up': nrm(ks[17], (DEPTH, D_MODEL, 2 * D_FF), D_MODEL ** -0.5),
        'w_down': nrm(ks[18], (DEPTH, D_FF, D_MODEL), D_FF ** -0.5),
        'norm_final': 1.0 + nrm(ks[19], (D_MODEL,), 0.02),
    }


def reference(x_prompt, x_sample, cache_kv_d1, cache_kv_d4, cache_kv_d16, state_conv, state_ssm,
              norm_mix, w_in, conv_w, conv_b, dt_bias, a_log, d_skip, norm_ssd, w_out, norm_ffn,
              w_gate_up, w_down, norm_final):
    hp, hs = x_prompt, x_sample
    p_new = [[] for _ in range(5)]
    s_new = [[] for _ in range(5)]
    for l in range(DEPTH):
        w = (norm_mix[l], w_in[l], conv_w[l], conv_b[l], dt_bias[l], a_log[l], d_skip[l],
             norm_ssd[l], w_out[l], norm_ffn[l], w_gate_up[l], w_down[l])
        conv0 = jnp.zeros((hp.shape[0], CONV_WIDTH - 1, CONV_DIM), hp.dtype)
        ssm0 = jnp.zeros((hp.shape[0], SSD_HEADS, SSD_HEAD_DIM, SSD_STATE), hp.dtype)
        hp, kv_p, conv_p, ssm_p = decoder_layer(hp, attend_prompt, conv0, ssm0, *w)
        caches_l = (cache_kv_d1[l], cache_kv_d4[l], cache_kv_d16[l])
        hs, kv_s, conv_s, ssm_s = decoder_layer(hs, functools.partial(attend_sample, kv_caches=caches_l),
                                                state_conv[l], state_ssm[l], *w)
        for i, arr in enumerate((kv_p[0], kv_p[1], kv_p[2], conv_p, ssm_p)):
            p_new[i].append(arr)
        for i, arr in enumerate((kv_s[0], kv_s[1], kv_s[2], conv_s, ssm_s)):
            s_new[i].append(arr)
    y_prompt = rmsnorm(hp, norm_final)
    y_sample = rmsnorm(hs, norm_final)
    p_kv_d1, p_kv_d4, p_kv_d16, p_conv, p_ssm = [jnp.stack(a) for a in p_new]
    s_kv_d1, s_kv_d4, s_kv_d16, s_conv, s_ssm = [jnp.stack(a) for a in s_new]
    return (y_prompt, y_sample, p_kv_d1, p_kv_d4, p_kv_d16, p_conv, p_ssm,
            s_kv_d1, s_kv_d4, s_kv_d16, s_conv, s_ssm)
```

```python
import numpy as np
from contextlib import ExitStack
import concourse.bass as bass
import concourse.mybir as mybir
from concourse.bass_utils import run_bass_kernel_spmd

F32 = mybir.dt.float32
BF16 = mybir.dt.bfloat16
AF = mybir.ActivationFunctionType
ALU = mybir.AluOpType
AX = mybir.AxisListType

ENGS = ("pe", "act", "dve", "pool", "sp")
NEG = -1.0e30
DIL = (1, 4, 16)
Q0, K0, V0, Z0, X0, DT0 = 0, 768, 1536, 2304, 3072, 4864
TO, TS = 2048, 64
TT = TO + TS
SCALE = 0.125
EPS = 1e-5


class Prog:
    def __init__(self):
        self.ops = {e: [] for e in ENGS}
        self.cnt = {e: 0 for e in ENGS}
        self.waited = {e: {} for e in ENGS}
        self.dcnt = {}
        self.lastw = {}
        self.readers = {}

    def _emit(self, eng, fn, tok, hasinc, reads, writes, isdma):
        deps = {}

        def add(t):
            if t is not None and deps.get(t[0], 0) < t[1]:
                deps[t[0]] = t[1]
        for r in reads:
            add(self.lastw.get(r))
            if isinstance(r, str) and r.startswith("ps"):
                for s, v in self.readers.get(r, {}).items():
                    add((s, v))
        for w in writes:
            add(self.lastw.get(w))
            for s, v in self.readers.get(w, {}).items():
                add((s, v))
        waits = []
        own = "E" + eng
        for s, v in deps.items():
            if s == own and eng == "pe" and not isdma:
                continue
            if self.waited[eng].get(s, 0) >= v:
                continue
            self.waited[eng][s] = v
            waits.append((s, v))
        self.ops[eng].append((waits, fn, tok if hasinc else None, isdma))
        for r in reads:
            d = self.readers.setdefault(r, {})
            if d.get(tok[0], 0) < tok[1]:
                d[tok[0]] = tok[1]
        for w in writes:
            self.lastw[w] = tok
            self.readers[w] = {}
        return tok

    def op(self, eng, fn, reads=(), writes=(), inc=True):
        if inc:
            self.cnt[eng] += 1
            tok = ("E" + eng, self.cnt[eng])
        else:
            tok = ("E" + eng, self.cnt[eng] + 1)
        return self._emit(eng, fn, tok, inc, reads, writes, False)

    def dma(self, eng, out, in_, stream, reads=(), writes=()):
        self.dcnt[stream] = self.dcnt.get(stream, 0) + 16
        tok = ("D" + stream, self.dcnt[stream])
        return self._emit(eng, lambda e: e.dma_start(out=out, in_=in_), tok, True, reads, writes, True)

    def barrier(self):
        allw = [("E" + e, self.cnt[e]) for e in ENGS if self.cnt[e] > 0]
        allw += [("D" + s, v) for s, v in self.dcnt.items()]
        for eng in ENGS:
            waits = []
            for s, v in allw:
                if self.waited[eng].get(s, 0) >= v:
                    continue
                self.waited[eng][s] = v
                waits.append((s, v))
            self.ops[eng].append((waits, None, None, False))

    def replay(self, nc, stack, pfx="a", semstack=None):
        sems = {}
        for n in ["E" + e for e in ENGS] + ["D" + s for s in self.dcnt]:
            sems[n] = (semstack or stack).enter_context(nc.semaphore(pfx + n))
        block = stack.enter_context(nc.Block())
        handles = {"pe": block.tensor, "act": block.scalar, "dve": block.vector,
                   "pool": block.gpsimd, "sp": block.sync}
        for eng in ENGS:
            ops = self.ops[eng]

            def body(e, ops=ops):
                for (waits, fn, tok, isdma) in ops:
                    for (s, v) in waits:
                        e.wait_ge(sems[s], v)
                    if fn is None:
                        continue
                    ins = fn(e)
                    if tok is not None:
                        ins.then_inc(sems[tok[0]], 16 if isdma else 1)
            handles[eng](body)


class Rot:
    def __init__(self, items):
        self.items = list(items)
        self.i = 0

    def next(self):
        v = self.items[self.i % len(self.items)]
        self.i += 1
        return v


def run_pipelined(gens, depth):
    active = []
    gens = list(gens)
    gi = 0
    while gi < len(gens) or active:
        if gi < len(gens) and len(active) < depth:
            active.append(gens[gi])
            gi += 1
        nxt = []
        for g in active:
            try:
                next(g)
                nxt.append(g)
            except StopIteration:
                pass
        active = nxt


def build(stage=99):
    nc = bass.Bass("TRN2", target_bir_lowering=False)
    dram = lambda n, s, k="ExternalInput": nc.dram_tensor(n, s, F32, kind=k).ap()
    xo = dram("xo", [TO, 1024])
    xp = dram("xp", [TO, 1024])
    xs = dram("xs", [TS, 1024])
    ck = [dram("ck1", [16, 128, 512]), dram("ck4", [16, 512, 512]), dram("ck16", [16, 2048, 512])]
    sconv_in = dram("sconv", [48, 1792])
    sssm_in = dram("sssm", [16, 768, 128])
    w_in = dram("w_in", [1024, 4876])
    w_out = dram("w_out", [1536, 1024])
    w_gu = dram("w_gu", [1024, 5632])
    w_down = dram("w_down", [2816, 1024])
    gmix_d = dram("gmix", [128, 8])
    gffn_d = dram("gffn", [128, 8])
    gfin_d = dram("gfin", [128, 1024])
    gssd_d = dram("gssd", [128, 768])
    convw_d = dram("convw", [128, 14 * 4])
    convb_d = dram("convb", [128, 14])
    dtb_d = dram("dtb", [128, 12])
    alog_d = dram("alog", [128, 12])
    dsk_d = dram("dsk", [128, 12])
    maskA_d = dram("maskA", [128, 256])
    maskF_d = dram("maskF", [128, 256])
    maskS_d = dram("maskS", [1, 4 * 256])
    hp_d = dram("hp", [128, 1])
    cmats_d = dram("cmats", [128, 4 * 128])
    y_o = dram("y", [TT, 1024], "ExternalOutput")
    pkv_o = [dram("pkv1", [128, 512], "ExternalOutput"), dram("pkv4", [512, 512], "ExternalOutput"),
             dram("pkv16", [2048, 512], "ExternalOutput")]
    pconv_o = dram("pconv", [3, 1792], "ExternalOutput")
    pssm_o = dram("pssm", [768, 128], "ExternalOutput")
    skv_o = [dram("skv1", [16, 128, 512], "ExternalOutput"), dram("skv4", [16, 512, 512], "ExternalOutput"),
             dram("skv16", [16, 2048, 512], "ExternalOutput")]
    oconv_o = dram("oconv", [16, 3, 1792], "ExternalOutput")
    ossm_o = dram("ossm", [16, 768, 128], "ExternalOutput")
    stats_d = dram("stats", [TT, 12, 2], "Internal")
    mixd = nc.dram_tensor("mixd", [12, 128, TT], BF16, kind="Internal").ap()
    w_in_v = w_in.rearrange("(k p) n -> p k n", p=128)

    with ExitStack() as st:
        cur = [st]
        sbt = lambda n, s, d: cur[0].enter_context(nc.sbuf_tensor("sb_" + n, s, d))
        psum = st.enter_context(nc.psum_tensor("psum", [128, 4096], F32))
        P = Prog()

        def PS(b, nb=1):
            return psum[:, b * 512:(b + nb) * 512]

        def PSK(b, nb=1):
            return ["ps%d" % i for i in range(b, b + nb)]

        cm = sbt("cm", [128, 4, 128], F32)
        identb = sbt("identb", [128, 128], BF16)
        gmix = sbt("gmix", [128, 8], F32)
        gffn = sbt("gffn", [128, 8], F32)
        gfin = sbt("gfin", [128, 1024], F32)
        gssd = sbt("gssd", [128, 768], F32)
        convw = sbt("convw", [128, 14, 4], F32)
        convb = sbt("convb", [128, 14], F32)
        dtb = sbt("dtb", [128, 12], F32)
        abc = sbt("abc", [128, 12], F32)
        dsk = sbt("dsk", [128, 12], F32)
        maskA = sbt("maskA", [128, 256], F32)
        maskF = sbt("maskF", [128, 256], F32)
        maskS = sbt("maskS", [128, 4, 256], F32)
        hp = sbt("hp", [128, 1], F32)
        cl = [("cm", cm[:].rearrange("p a b -> p (a b)"), cmats_d), ("gmix", gmix[:], gmix_d),
              ("gffn", gffn[:], gffn_d), ("gfin", gfin[:], gfin_d), ("gssd", gssd[:], gssd_d),
              ("convw", convw[:].rearrange("p a b -> p (a b)"), convw_d), ("convb", convb[:], convb_d),
              ("dtb", dtb[:], dtb_d), ("abc", abc[:], alog_d), ("dsk", dsk[:], dsk_d),
              ("maskA", maskA[:], maskA_d), ("maskF", maskF[:], maskF_d),
              ("maskS", maskS[0:1].rearrange("p a b -> p (a b)"), maskS_d), ("hp", hp[:], hp_d)]
        for (k, t, d) in cl:
            P.dma("sp", t, d[:, :], "c_" + k, writes=[k])
        identf = cm[:, 0, :]
        triu = cm[:, 1, :]
        slm = cm[:, 2, :]
        onesm = cm[:, 3, :]
        P.op("dve", lambda e: e.tensor_copy(out=identb[:], in_=identf), reads=["cm"], writes=["identb"])
        P.op("act", lambda e: e.activation(out=abc[:], in_=abc[:], func=AF.Exp), reads=["abc"], writes=["abc"])
        P.op("dve", lambda e: e.tensor_scalar(out=abc[:], in0=abc[:], scalar1=-1.0, scalar2=None, op0=ALU.mult),
             reads=["abc"], writes=["abc"])

        xin = [sbt("xin%d" % i, [128, 1024], F32) for i in range(2)]
        xbb = [sbt("xbb%d" % i, [128, 1024], BF16) for i in range(2)]
        junk = sbt("junk", [128, 1024], BF16)
        nstat = [sbt("nstat%d" % i, [128, 4], F32) for i in range(2)]
        wb = [sbt("wb%d" % i, [128, 8, 512], BF16) for i in range(2)]
        wrot = Rot(range(2))
        nrot = Rot(range(2))
        st1 = ExitStack()
        cur[0] = st1
        xnT = sbt("xnT", [128, 8, TT], BF16)
        kvf = [sbt("kvf%d" % i, [128, 512], F32) for i in range(2)]
        kvrot = Rot(range(2))

        def load_w(wv, c0, ncols, nk=8):
            s = wrot.next()
            key = "wb%d" % s
            P.dma("pool", wb[s][:, 0:nk, 0:ncols], wv[:, 0:nk, c0:c0 + ncols], key, writes=[key])
            return wb[s], key

        def xkeys(t0, tn):
            return [("xnT", i) for i in range(t0 // 128, (t0 + tn - 1) // 128 + 1)]

        def norm_T(src, src_key, T, gcol, gkey, dst, dst_keys, psb, from_dram=True):
            s = nrot.next()
            if from_dram:
                xt = xin[s]
                P.dma("sp", xt[0:T, :], src, "xin%d" % s, writes=["xin%d" % s])
                xa, xk = xt[0:T, :], "xin%d" % s
            else:
                xa, xk = src, src_key
            ns = nstat[s]
            nk = "nstat%d" % s
            P.op("act", lambda e: e.activation(out=junk[0:T, :], in_=xa, func=AF.Square, scale=1.0 / 32,
                                               accum_out=ns[0:T, 0:1]), reads=[xk], writes=["junk", nk])
            P.op("dve", lambda e: e.tensor_scalar(out=ns[0:T, 1:2], in0=ns[0:T, 0:1], scalar1=EPS, scalar2=None,
                                                  op0=ALU.add), reads=[nk], writes=[nk])
            P.op("act", lambda e: e.activation(out=ns[0:T, 2:3], in_=ns[0:T, 1:2], func=AF.Ln), reads=[nk], writes=[nk])
            P.op("act", lambda e: e.activation(out=ns[0:T, 3:4], in_=ns[0:T, 2:3], func=AF.Exp, scale=-0.5),
                 reads=[nk], writes=[nk])
            xb = xbb[s]
            bk = "xbb%d" % s
            P.op("dve", lambda e: e.tensor_scalar(out=xb[0:T, :], in0=xa, scalar1=ns[0:T, 3:4], scalar2=None,
                                                  op0=ALU.mult), reads=[xk, nk], writes=[bk])
            pt = PS(psb).bitcast(BF16).rearrange("p (k t) -> p k t", k=8)
            for k in range(8):
                P.op("pe", lambda e, k=k: e.transpose(out=pt[:, k, 0:T], in_=xb[0:T, k * 128:(k + 1) * 128],
                                                      identity=identb[0:T, 0:T]),
                     reads=[bk, "identb"], writes=PSK(psb), inc=(k == 7))
            P.op("dve", lambda e: e.tensor_tensor(out=dst, in0=pt[:, :, 0:T],
                                                  in1=gcol.unsqueeze(2).broadcast_to([128, 8, T]), op=ALU.mult),
                 reads=PSK(psb) + [gkey], writes=dst_keys)

        prot = Rot([0, 1, 2, 3])

        def norm_tokens(srcd, ntok, base):
            for i in range(0, ntok, 128):
                T = min(128, ntok - i)
                norm_T(srcd[i:i + T, :], None, T, gmix[:, :], "gmix", xnT[:, :, base + i:base + i + T],
                       [("xnT", (base + i) // 128)], prot.next())

        def tokslice(start, d, n=128):
            return slice(start, start + d * (n - 1) + 1, d)

        Vtok = [sbt("Vtok%d" % g, [128, 16, 256], BF16) for g in range(3)]
        Vprev = [sbt("Vprev0", [128, 1, 256], BF16), sbt("Vprev1", [128, 4, 256], BF16),
                 sbt("Vprev2", [128, 16, 256], BF16)]
        Vnew = [sbt("Vnew%d" % g, [128, 256], BF16) for g in range(3)]
        kTp = [sbt("kTp0", [128, 2, 128], BF16), sbt("kTp1", [128, 2, 512], BF16), sbt("kTp2", [128, 2, 2048], BF16)]

        def block_tokens(g, blk):
            d = DIL[g]
            u, r = blk // d, blk % d
            return tokslice(u * 128 * d + r, d)

        def kv_block(g, wt, wkey, cols, T, want_k, vdst, vkey, out_dma):
            b = prot.next()
            c0 = 0 if want_k else 256
            for kc in range(8):
                P.op("pe", lambda e, kc=kc: e.matmul(out=PS(b)[0:T, c0:512], lhsT=xnT[:, kc, cols],
                                                     rhs=wt[:, kc, c0:512], start=(kc == 0), stop=(kc == 7)),
                     reads=[wkey] + [("xnT", i) for i in range(17)], writes=PSK(b), inc=(kc == 7))
            if out_dma is not None:
                s = kvrot.next()
                P.op("dve", lambda e: e.tensor_copy(out=kvf[s][0:T, :], in_=PS(b)[0:T, :]),
                     reads=PSK(b), writes=["kvf%d" % s])
                if vdst is not None:
                    P.op("act", lambda e: e.activation(out=vdst, in_=kvf[s][0:T, 256:512], func=AF.Copy),
                         reads=["kvf%d" % s], writes=[vkey])
                out_dma(kvf[s], "kvf%d" % s)
            elif vdst is not None:
                P.op("act", lambda e: e.activation(out=vdst, in_=PS(b)[0:T, 256:512], func=AF.Copy),
                     reads=PSK(b), writes=[vkey])

        def load_wkv(g):
            s = wrot.next()
            key = "wb%d" % s
            P.dma("pool", wb[s][:, :, 0:256], w_in_v[:, :, K0 + g * 256:K0 + (g + 1) * 256], key, writes=[key])
            P.dma("pool", wb[s][:, :, 256:512], w_in_v[:, :, V0 + g * 256:V0 + (g + 1) * 256], key, writes=[key])
            return wb[s], key

        import os
        for g in range(3 if not os.environ.get("KDBG_NOCC") else 0):
            lb = 128 * DIL[g]
            for b0 in range(0, 16, 4):
                P.dma("act", skv_o[g][b0:b0 + 4, 0:lb - 4, :], ck[g][b0:b0 + 4, 4:lb, :], "cc%d_%d" % (g, b0))

        SSD_ON = stage >= 3
        st1a = ExitStack()
        cur[0] = st1a
        if SSD_ON:
            GT = 512
            wz = sbt("wz", [128, 8, 768], BF16)
            wdt = sbt("wdt", [128, 8, 12], BF16)
            P.dma("pool", wz[:, :, :], w_in_v[:, :, Z0:Z0 + 768], "wz", writes=["wz"])
            P.dma("pool", wdt[:, :, :], w_in_v[:, :, DT0:DT0 + 12], "wdt", writes=["wdt"])
            stg = sbt("stg", [128, 14, 3 + GT], BF16)
            carry = sbt("carry", [128, 14, 3], BF16)
            xc = sbt("xc", [128, 14, GT], BF16)
            cvt = [sbt("cvt%d" % i, [128, GT], F32) for i in range(2)]
            cvrot = Rot(range(2))
            hT = sbt("hT", [128, 768], F32)
            hTb = sbt("hTb", [128, 768], BF16)
            dv = sbt("dv", [128, 128], F32)
            cdt = sbt("cdt", [128, 12], F32)
            xw = sbt("xw", [128, 768], BF16)
            xtok = sbt("xtok", [128, 768], BF16)
            Btok = sbt("Btok", [128, 512], BF16)
            sz = sbt("sz", [128, 768], F32)
            t1 = sbt("t1", [128, 768], F32)
            yv = sbt("yv", [128, 768], F32)
            CBm = sbt("CBm", [128, 4, 128], F32)
            Rt2 = [sbt("Rt%d" % i, [128, 3, 128], F32) for i in range(2)]
            Lh2 = [sbt("Lh%d" % i, [128, 3, 128], F32) for i in range(2)]
            Gh2 = [sbt("Gh%d" % i, [128, 3, 128], BF16) for i in range(2)]
            ynb = sbt("ynb", [128, 768], BF16)
            ych = sbt("ych", [128, 6, 128], BF16)
            sst = sbt("sst", [128, 6, 128], F32)
            gcol = lambda g: (g // 2) * 512 + (g % 2) * 192
            hcol = lambda h: gcol(h // 3) + (h % 3) * 64

            def v2(ap2d):
                return ap2d.rearrange("p (a c) -> p a c", a=2)

            def pv2(b, T):
                return PS(b, 2)[0:T, :].rearrange("p (a c) -> p a c", a=2)[:, :, 0:384]

            def ssd_chunk(T, xcf, xck, xnf, xnk, full, out_cols):
                D = lambda a, n=12: dv[0:T, a:a + n]
                if full:
                    for (c0, c1, b) in ((0, 512, 0), (512, 768, 1)):
                        for kc in range(8):
                            P.op("pe", lambda e, kc=kc, c0=c0, c1=c1, b=b: e.matmul(
                                out=PS(b)[0:T, 0:c1 - c0], lhsT=xnf(kc), rhs=wz[:, kc, c0:c1],
                                start=(kc == 0), stop=(kc == 7)), reads=["wz"] + xnk, writes=PSK(b), inc=(kc == 7))
                    P.op("act", lambda e: e.activation(out=sz[0:T, 0:512], in_=PS(0)[0:T, :], func=AF.Silu),
                         reads=PSK(0), writes=["sz"])
                    P.op("act", lambda e: e.activation(out=sz[0:T, 512:768], in_=PS(1)[0:T, 0:256], func=AF.Silu),
                         reads=PSK(1) + ["sz"], writes=["sz"])
                for kc in range(8):
                    P.op("pe", lambda e, kc=kc: e.matmul(out=PS(2)[0:T, 0:12], lhsT=xnf(kc), rhs=wdt[:, kc, :],
                                                         start=(kc == 0), stop=(kc == 7)),
                         reads=["wdt"] + xnk, writes=PSK(2), inc=(kc == 7))
                dvo = lambda fn, rd=(): P.op("dve", fn, reads=["dv"] + list(rd), writes=["dv"])
                aco = lambda fn, rd=(): P.op("act", fn, reads=["dv"] + list(rd), writes=["dv"])
                dvo(lambda e: e.tensor_tensor(out=D(0), in0=PS(2)[0:T, 0:12], in1=dtb[0:T, :], op=ALU.add),
                    PSK(2) + ["dtb"])
                dvo(lambda e: e.tensor_scalar(out=D(12), in0=D(0), scalar1=-1.0, scalar2=None, op0=ALU.mult))
                dvo(lambda e: e.tensor_tensor(out=D(12), in0=D(0), in1=D(12), op=ALU.max))
                aco(lambda e: e.activation(out=D(24), in_=D(12), func=AF.Exp, scale=-1.0))
                dvo(lambda e: e.tensor_scalar(out=D(24), in0=D(24), scalar1=1.0, scalar2=None, op0=ALU.add))
                aco(lambda e: e.activation(out=D(36), in_=D(24), func=AF.Ln))
                dvo(lambda e: e.scalar_tensor_tensor(out=D(48), in0=D(0), scalar=0.0, in1=D(36), op0=ALU.max,
                                                     op1=ALU.add))
                dvo(lambda e: e.tensor_tensor(out=D(60), in0=D(48), in1=abc[0:T, :], op=ALU.mult), ["abc"])
                P.op("pe", lambda e: e.matmul(out=PS(2)[0:T, 16:28], lhsT=triu[0:T, 0:T], rhs=D(60), start=True,
                                              stop=True), reads=["dv", "cm"], writes=PSK(2), inc=False)
                P.op("pe", lambda e: e.matmul(out=PS(2)[:, 32:44], lhsT=onesm[0:T, :], rhs=D(60), start=True,
                                              stop=True), reads=["dv", "cm"], writes=PSK(2))
                dvo(lambda e: e.tensor_copy(out=D(72), in_=PS(2)[0:T, 16:28]), PSK(2))
                P.op("act", lambda e: e.activation(out=cdt[:, :], in_=PS(2)[:, 32:44], func=AF.Exp),
                     reads=PSK(2), writes=["cdt"])
                dvo(lambda e: e.tensor_tensor(out=D(84), in0=PS(2)[0:T, 32:44], in1=D(72), op=ALU.subtract), PSK(2))
                aco(lambda e: e.activation(out=D(96), in_=D(84), func=AF.Exp))
                dvo(lambda e: e.tensor_tensor(out=D(96), in0=D(96), in1=D(48), op=ALU.mult))
                if full:
                    aco(lambda e: e.activation(out=D(108), in_=D(72), func=AF.Exp))
                ptx = PS(3).bitcast(BF16)
                ptb = PS(4).bitcast(BF16)
                for j in range(6):
                    P.op("pe", lambda e, j=j: e.transpose(out=ptx[0:T, j * 128:(j + 1) * 128], in_=xcf(j),
                                                          identity=identb[:, :]),
                         reads=xck + ["identb"], writes=PSK(3), inc=(j == 5))
                for g in range(4):
                    P.op("pe", lambda e, g=g: e.transpose(out=ptb[0:T, g * 128:(g + 1) * 128], in_=xcf(6 + g),
                                                          identity=identb[:, :]),
                         reads=xck + ["identb"], writes=PSK(4), inc=(g == 3))
                P.op("dve", lambda e: e.tensor_tensor(
                    out=xw[0:T, :].rearrange("p (h d) -> p h d", h=12),
                    in0=ptx[0:T, 0:768].rearrange("p (h d) -> p h d", h=12),
                    in1=D(96).unsqueeze(2).broadcast_to([T, 12, 64]), op=ALU.mult),
                    reads=PSK(3) + ["dv"], writes=["xw"])
                if full:
                    P.op("act", lambda e: e.activation(out=xtok[0:T, :], in_=ptx[0:T, 0:768], func=AF.Copy),
                         reads=PSK(3), writes=["xtok"])
                P.op("act", lambda e: e.activation(out=Btok[0:T, :], in_=ptb[0:T, 0:512], func=AF.Copy),
                     reads=PSK(4), writes=["Btok"])
                for g in range(4):
                    P.op("pe", lambda e, g=g: e.matmul(out=PS(6, 2)[:, gcol(g):gcol(g) + 192],
                                                       lhsT=Btok[0:T, g * 128:(g + 1) * 128],
                                                       rhs=xw[0:T, g * 192:(g + 1) * 192], start=True, stop=True),
                         reads=["Btok", "xw"], writes=PSK(6, 2), inc=(g == 3))
                if full:
                    for g in range(4):
                        P.op("pe", lambda e, g=g: e.matmul(out=PS(0, 2)[0:T, gcol(g):gcol(g) + 192],
                                                           lhsT=xcf(10 + g), rhs=hTb[:, g * 192:(g + 1) * 192],
                                                           start=True, stop=True),
                             reads=xck + ["hTb"], writes=PSK(0, 2), inc=(g == 3))
                    P.op("dve", lambda e: e.tensor_tensor(
                        out=v2(t1[0:T, :]).rearrange("p a (h d) -> p a h d", h=6),
                        in0=pv2(0, T).rearrange("p a (h d) -> p a h d", h=6),
                        in1=D(108).rearrange("p (a h) -> p a h", a=2).unsqueeze(3).broadcast_to([T, 2, 6, 64]),
                        op=ALU.mult), reads=PSK(0, 2) + ["dv"], writes=["t1"])
                    for g in range(4):
                        P.op("pe", lambda e, g=g: e.matmul(out=PS(4)[0:T, g * 128:g * 128 + T], lhsT=xcf(6 + g),
                                                           rhs=xcf(10 + g), start=True, stop=True),
                             reads=xck, writes=PSK(4), inc=(g == 3))
                    P.op("dve", lambda e: e.tensor_tensor(
                        out=CBm[0:T, :, 0:T], in0=PS(4)[0:T, :].rearrange("p (g l) -> p g l", g=4)[:, :, 0:T],
                        in1=triu[0:T, 0:T].unsqueeze(1).broadcast_to([T, 4, T]), op=ALU.mult),
                        reads=PSK(4) + ["cm"], writes=["CBm"])
                    for g in range(4):
                        q2 = g % 2
                        Rt, Lh, Gh = Rt2[q2], Lh2[q2], Gh2[q2]
                        rk_, lk_, gk_ = "Rt%d" % q2, "Lh%d" % q2, "Gh%d" % q2
                        sb_ = 5 if q2 == 0 else 3
                        P.op("dve", lambda e, g=g, Rt=Rt: e.tensor_tensor(
                            out=Rt[0:T, :, 0:T], in0=triu[0:T, 0:T].unsqueeze(1).broadcast_to([T, 3, T]),
                            in1=dv[0:T, 60 + 3 * g:63 + 3 * g].unsqueeze(2).broadcast_to([T, 3, T]), op=ALU.mult),
                            reads=["dv", "cm"], writes=[rk_])
                        P.op("pe", lambda e, Rt=Rt, sb_=sb_: e.matmul(
                            out=PS(sb_)[0:T, 0:384].rearrange("p (r l) -> p r l", r=3)[:, :, 0:T], lhsT=slm[0:T, 0:T],
                            rhs=Rt[0:T, :, 0:T], start=True, stop=True), reads=[rk_, "cm"], writes=PSK(sb_))
                        P.op("act", lambda e, Lh=Lh, sb_=sb_: e.activation(
                            out=Lh[0:T, :, 0:T], in_=PS(sb_)[0:T, 0:384].rearrange("p (r l) -> p r l", r=3)[:, :, 0:T],
                            func=AF.Exp), reads=PSK(sb_), writes=[lk_])
                        P.op("dve", lambda e, g=g, Lh=Lh: e.tensor_tensor(
                            out=Lh[0:T, :, 0:T], in0=Lh[0:T, :, 0:T],
                            in1=dv[0:T, 48 + 3 * g:51 + 3 * g].unsqueeze(2).broadcast_to([T, 3, T]), op=ALU.mult),
                            reads=[lk_, "dv"], writes=[lk_])
                        P.op("dve", lambda e, g=g, Lh=Lh, Gh=Gh: e.tensor_tensor(
                            out=Gh[0:T, :, 0:T], in0=Lh[0:T, :, 0:T],
                            in1=CBm[0:T, g, 0:T].unsqueeze(1).broadcast_to([T, 3, T]), op=ALU.mult),
                            reads=[lk_, "CBm"], writes=[gk_])
                        for r in range(3):
                            h = 3 * g + r
                            P.op("pe", lambda e, r=r, h=h, Gh=Gh: e.matmul(
                                out=PS(0, 2)[0:T, hcol(h):hcol(h) + 64], lhsT=Gh[0:T, r, 0:T],
                                rhs=xtok[0:T, h * 64:(h + 1) * 64], start=True, stop=True),
                                reads=[gk_, "xtok"], writes=PSK(0, 2), inc=(r == 2))
                    P.op("dve", lambda e: e.tensor_tensor(out=v2(yv[0:T, :]), in0=pv2(0, T), in1=v2(t1[0:T, :]),
                                                          op=ALU.add), reads=PSK(0, 2) + ["t1"], writes=["yv"])
                    P.op("dve", lambda e: e.tensor_tensor(
                        out=t1[0:T, :].rearrange("p (h d) -> p h d", h=12),
                        in0=xtok[0:T, :].rearrange("p (h d) -> p h d", h=12),
                        in1=dsk[0:T, :].unsqueeze(2).broadcast_to([T, 12, 64]), op=ALU.mult),
                        reads=["xtok", "dsk", "yv"], writes=["t1"])
                    P.op("dve", lambda e: e.tensor_tensor(out=yv[0:T, :], in0=yv[0:T, :], in1=t1[0:T, :], op=ALU.add),
                         reads=["yv", "t1"], writes=["yv"])
                    P.op("dve", lambda e: e.tensor_tensor(out=yv[0:T, :], in0=yv[0:T, :], in1=sz[0:T, :], op=ALU.mult),
                         reads=["yv", "sz"], writes=["yv"])
                    P.op("act", lambda e: e.activation(out=t1[0:T, :], in_=yv[0:T, :], func=AF.Square,
                                                       scale=float(768 ** -0.5), accum_out=D(120, 1)),
                         reads=["yv", "dv"], writes=["t1", "dv"])
                    dvo(lambda e: e.tensor_scalar(out=D(121, 1), in0=D(120, 1), scalar1=EPS, scalar2=None, op0=ALU.add))
                    aco(lambda e: e.activation(out=D(122, 1), in_=D(121, 1), func=AF.Ln))
                    aco(lambda e: e.activation(out=D(123, 1), in_=D(122, 1), func=AF.Exp, scale=-0.5))
                    P.op("dve", lambda e: e.scalar_tensor_tensor(out=ynb[0:T, :], in0=yv[0:T, :], scalar=D(123, 1),
                                                                 in1=gssd[0:T, :], op0=ALU.mult, op1=ALU.mult),
                         reads=["yv", "dv", "gssd"], writes=["ynb"])
                    for j in range(6):
                        P.op("pe", lambda e, j=j: e.transpose(out=ptx[:, j * 128:j * 128 + T],
                                                              in_=ynb[0:T, j * 128:(j + 1) * 128],
                                                              identity=identb[0:T, 0:T]),
                             reads=["ynb", "identb"], writes=PSK(3), inc=(j == 5))
                    P.op("act", lambda e: e.activation(
                        out=ych[:, :, 0:T], in_=ptx[:, 0:768].rearrange("p (j t) -> p j t", j=6)[:, :, 0:T],
                        func=AF.Copy), reads=PSK(3), writes=["ych"])
                    P.dma("sp", mixd[6:12, :, out_cols:out_cols + T].rearrange("j p t -> p j t"), ych[:, :, 0:T],
                          "ycho", reads=["ych"])
                P.op("dve", lambda e: e.tensor_tensor(
                    out=hT[:, :].rearrange("p (h d) -> p h d", h=12), in0=hT[:, :].rearrange("p (h d) -> p h d", h=12),
                    in1=cdt[:, :].unsqueeze(2).broadcast_to([128, 12, 64]), op=ALU.mult),
                    reads=["hT", "cdt"], writes=["hT"])
                P.op("dve", lambda e: e.tensor_tensor(out=v2(hT[:, :]), in0=pv2(6, 128), in1=v2(hT[:, :]), op=ALU.add),
                     reads=PSK(6, 2) + ["hT"], writes=["hT"])
                P.op("act", lambda e: e.activation(out=hTb[:, :], in_=hT[:, :], func=AF.Copy), reads=["hT"],
                     writes=["hTb"])

            def conv_silu(j, ins, outc, cview):
                cs = cvrot.next()
                cv = cview(cvt[cs])
                ck_ = "cvt%d" % cs
                P.op("dve", lambda e: e.tensor_scalar(out=cv, in0=ins[0], scalar1=convw[:, j, 0:1],
                                                      scalar2=convb[:, j:j + 1], op0=ALU.mult, op1=ALU.add),
                     reads=["stg", "convw", "convb"], writes=[ck_])
                for w in range(1, 4):
                    P.op("dve", lambda e, w=w: e.scalar_tensor_tensor(out=cv, in0=ins[w], scalar=convw[:, j, w:w + 1],
                                                                      in1=cv, op0=ALU.mult, op1=ALU.add),
                         reads=["stg", "convw", ck_], writes=[ck_])
                P.op("act", lambda e: e.activation(out=outc, in_=cv, func=AF.Silu), reads=[ck_], writes=["xc"])

            def xbc_proj(tok0, ntok, evac):
                for cg in range(4):
                    ncol = 512 if cg < 3 else 256
                    wt, wkey = load_w(w_in_v, X0 + cg * 512, ncol)
                    for jj in range(ncol // 128):
                        j = cg * 4 + jj
                        b = prot.next()
                        for kc in range(8):
                            P.op("pe", lambda e, kc=kc, b=b, jj=jj, wt=wt: e.matmul(
                                out=PS(b)[:, 0:ntok], lhsT=wt[:, kc, jj * 128:(jj + 1) * 128],
                                rhs=xnT[:, kc, tok0:tok0 + ntok], start=(kc == 0), stop=(kc == 7)),
                                reads=[wkey] + xkeys(tok0, ntok), writes=PSK(b), inc=(kc == 7))
                        evac(j, b)

            def ssd_group(tok0, ntok, full):
                P.op("dve", lambda e: e.tensor_copy(out=stg[:, :, 0:3], in_=carry[:, :, :]), reads=["carry", "xc"],
                     writes=["stg"])
                xbc_proj(tok0, ntok, lambda j, b: P.op("act", lambda e: e.activation(
                    out=stg[:, j, 3:3 + ntok], in_=PS(b)[:, 0:ntok], func=AF.Copy), reads=PSK(b), writes=["stg"]))
                P.op("dve", lambda e: e.tensor_copy(out=carry[:, :, :], in_=stg[:, :, ntok:ntok + 3]), reads=["stg"],
                     writes=["carry"])
                for j in range(14):
                    conv_silu(j, [stg[:, j, w:w + ntok] for w in range(4)], xc[:, j, 0:ntok],
                              lambda t: t[:, 0:ntok])
                for ci in range(ntok // 128):
                    ssd_chunk(128, lambda j, ci=ci: xc[:, j, ci * 128:(ci + 1) * 128], ["xc"],
                              lambda kc, ci=ci: xnT[:, kc, tok0 + ci * 128:tok0 + (ci + 1) * 128],
                              xkeys(tok0 + ci * 128, 128), full, tok0 + ci * 128)

            def state_out(dst):
                for a in range(6):
                    P.op("pe", lambda e, a=a: e.transpose(out=PS(0, 2)[:, a * 128:(a + 1) * 128],
                                                          in_=hT[:, a * 128:(a + 1) * 128], identity=identf),
                         reads=["hT", "cm"], writes=PSK(0, 2), inc=(a == 5))
                P.op("dve", lambda e: e.tensor_copy(out=sst[:, :, :].rearrange("p a n -> p (a n)"),
                                                    in_=PS(0, 2)[:, 0:768]), reads=PSK(0, 2), writes=["sst"])
                P.dma("sp", dst.rearrange("(a p) n -> p a n", p=128), sst[:, :, :], "ssto", reads=["sst"])
        LV = int(os.environ.get("KDBG_LV", "9"))
        if LV >= 1:
            norm_tokens(xp, TO, 0)
        if LV >= 2:
            for g in range(3):
                d = DIL[g]
                wt, wkey = load_wkv(g)
                nb = d
                for r in range(d):
                    blk = (16 // d - 1) * d + r if d < 16 else r
                    kv_block(g, wt, wkey, block_tokens(g, blk), 128, False, Vprev[g][:, r, :],
                             ("Vprev", g, r), None)
        if stage >= 4:
            for g in range(3):
                npv = 128 * DIL[g]
                wt, wkey = load_w(w_in_v, K0 + g * 256, 256)
                for c in range(2):
                    for a0 in range(0, npv, 512):
                        an = min(512, npv - a0)
                        b = prot.next()
                        for kc in range(8):
                            P.op("pe", lambda e, kc=kc, b=b, c=c, a0=a0, an=an, wt=wt, npv=npv: e.matmul(
                                out=PS(b)[:, 0:an], lhsT=wt[:, kc, c * 128:(c + 1) * 128],
                                rhs=xnT[:, kc, TO - npv + a0:TO - npv + a0 + an], start=(kc == 0), stop=(kc == 7)),
                                reads=[wkey] + [("xnT", i) for i in range(17)], writes=PSK(b), inc=(kc == 7))
                        P.op("act", lambda e, b=b, c=c, a0=a0, an=an, g=g: e.activation(
                            out=kTp[g][:, c, a0:a0 + an], in_=PS(b)[:, 0:an], func=AF.Copy),
                            reads=PSK(b), writes=["kTp"])
        if SSD_ON:
            P.op("dve", lambda e: e.memset(hT[:, :], 0.0), writes=["hT"])
            P.op("dve", lambda e: e.memset(hTb[:, :], 0.0), writes=["hTb"])
            P.op("dve", lambda e: e.memset(carry[:, :, :], 0.0), writes=["carry"])
            for gi in range(TO // GT):
                ssd_group(gi * GT, GT, False)
            P.op("dve", lambda e: e.tensor_scalar(out=hT[:, :], in0=hT[:, :], scalar1=hp[:, 0:1], scalar2=None,
                                                  op0=ALU.mult), reads=["hT", "hp"], writes=["hT"])
            P.op("act", lambda e: e.activation(out=hTb[:, :], in_=hT[:, :], func=AF.Copy), reads=["hT"], writes=["hTb"])
            P.op("dve", lambda e: e.tensor_scalar(out=carry[:, :, :].rearrange("p a b -> p (a b)"),
                                                  in0=carry[:, :, :].rearrange("p a b -> p (a b)"),
                                                  scalar1=hp[:, 0:1], scalar2=None, op0=ALU.mult),
                 reads=["carry", "hp"], writes=["carry"])
        SUB = int(os.environ.get("KDBG_SUB", "9"))
        if LV >= 3:
            norm_tokens(xo, TO, 0)
            if SUB >= 2:
                norm_tokens(xs, TS, TO)
        for g in range(3 if (LV >= 3 and SUB >= 1) else 0):
            d = DIL[g]
            lb = 128 * d
            wt, wkey = load_wkv(g)
            for blk in range(16):
                u, r = blk // d, blk % d
                is_out = (u == 16 // d - 1)

                def od(t, tk, g=g, r=r, d=d):
                    if os.environ.get("KDBG_CONT"):
                        P.dma("sp", pkv_o[g][r * 128:(r + 1) * 128, :], t[:, :], "pkvo", reads=[tk])
                    else:
                        P.dma("sp", pkv_o[g][r:128 * d:d, :], t[:, :], "pkvo", reads=[tk])
                kv_block(g, wt, wkey, block_tokens(g, blk), 128, is_out, Vtok[g][:, blk, :],
                         ("Vtok", g, blk), od if (is_out and LV >= 4) else None)
            def ods(t, tk, g=g, lb=lb):
                for s_ in range(4):
                    P.dma("sp", skv_o[g][:, lb - 4 + s_, :], t[s_:64:4, :], "skvo", reads=[tk])
            if SUB >= 3:
                kv_block(g, wt, wkey, slice(TO, TT), 64, True, Vnew[g][0:64, :], ("Vnew", g), ods if LV >= 5 else None)

        for cg in range(4):
            ncol = 512 if cg < 3 else 256
            wt, wkey = load_w(w_in_v, X0 + cg * 512, ncol)
            for (which, cols, T) in (("p", slice(TO - 3, TO), 3), ("s", slice(TO, TT), 64)):
                b = prot.next()
                for kc in range(8):
                    P.op("pe", lambda e, kc=kc, b=b, cols=cols, T=T, wt=wt, ncol=ncol: e.matmul(
                        out=PS(b)[0:T, 0:ncol], lhsT=xnT[:, kc, cols], rhs=wt[:, kc, 0:ncol],
                        start=(kc == 0), stop=(kc == 7)),
                        reads=[wkey, ("xnT", 15), ("xnT", 16)], writes=PSK(b), inc=(kc == 7))
                ks = kvrot.next()
                P.op("dve", lambda e, b=b, ks=ks, T=T, ncol=ncol: e.tensor_copy(
                    out=kvf[ks][0:T, 0:ncol], in_=PS(b)[0:T, 0:ncol]), reads=PSK(b), writes=["kvf%d" % ks])
                if which == "p":
                    P.dma("sp", pconv_o[:, cg * 512:cg * 512 + ncol], kvf[ks][0:3, 0:ncol], "cvo", reads=["kvf%d" % ks])
                else:
                    for s_ in range(1, 4):
                        P.dma("sp", oconv_o[:, s_ - 1, cg * 512:cg * 512 + ncol], kvf[ks][s_:64:4, 0:ncol], "cvo",
                              reads=["kvf%d" % ks])
        if SSD_ON:
            for gi in range(TO // GT):
                ssd_group(gi * GT, GT, True)
            state_out(pssm_o)
            stg_s = stg[:, :, 0:112].rearrange("p j (b w) -> p j b w", w=7)
            for j in range(14):
                if j % 6 == 0:
                    ncs = min(768, 1792 - j * 128)
                    P.dma("sp", sz[0:48, 0:ncs], sconv_in[:, j * 128:j * 128 + ncs], "sct", writes=["sz"])
                jl = j % 6
                b = prot.next()
                P.op("pe", lambda e, jl=jl, b=b: e.transpose(out=PS(b)[:, 0:48], in_=sz[0:48, jl * 128:(jl + 1) * 128],
                                                            identity=identf[0:48, 0:48]),
                     reads=["sz", "cm"], writes=PSK(b))
                P.op("dve", lambda e, j=j, b=b: e.tensor_copy(
                    out=stg_s[:, j, :, 0:3], in_=PS(b)[:, 0:48].rearrange("p (b r) -> p b r", r=3)),
                    reads=PSK(b) + ["xc"], writes=["stg"])
            xbc_proj(TO, 64, lambda j, b: P.op("act", lambda e: e.activation(
                out=stg_s[:, j, :, 3:7], in_=PS(b)[:, 0:64].rearrange("p (b s) -> p b s", s=4), func=AF.Copy),
                reads=PSK(b), writes=["stg"]))
            for j in range(14):
                conv_silu(j, [stg_s[:, j, :, w:w + 4] for w in range(4)],
                          xc[:, j, 0:64].rearrange("p (b s) -> p b s", s=4),
                          lambda t: t[:, 0:64].rearrange("p (b s) -> p b s", s=4))
            for b in range(16):
                P.dma("sp", sst[:, :, :], sssm_in[b].rearrange("(a p) n -> p a n", p=128), "ssti", writes=["sst"])
                for a in range(6):
                    P.op("pe", lambda e, a=a: e.transpose(out=PS(0, 2)[:, a * 128:(a + 1) * 128], in_=sst[:, a, :],
                                                          identity=identf), reads=["sst", "cm"], writes=PSK(0, 2),
                         inc=(a == 5))
                P.op("dve", lambda e: e.tensor_copy(out=hT[:, :], in_=PS(0, 2)[:, 0:768]), reads=PSK(0, 2),
                     writes=["hT"])
                P.op("act", lambda e: e.activation(out=hTb[:, :], in_=hT[:, :], func=AF.Copy), reads=["hT"],
                     writes=["hTb"])
                ssd_chunk(4, lambda j, b=b: xc[:, j, 4 * b:4 * b + 4], ["xc"],
                          lambda kc, b=b: xnT[:, kc, TO + 4 * b:TO + 4 * b + 4], [("xnT", 16)], True, TO + 4 * b)
                state_out(ossm_o[b])
        if os.environ.get("KDBG_NOATT"):
            P.op("dve", lambda e: e.memset(junk[:, :], 0.0), writes=["junk"])
            for j in range(6):
                for (a0, a1) in ((0, 1024), (1024, 2048), (2048, TT)):
                    P.dma("sp", mixd[j, :, a0:a1], junk[:, 0:a1 - a0], "zbo", reads=["junk"])
        NOSSD = stage < 3
        if NOSSD:
            zt = sbt("zt", [128, 768], F32)
            P.op("dve", lambda e: e.memset(zt[:], 0.0), writes=["zt"])
            for i in range(6):
                P.dma("sp", pssm_o[i * 128:(i + 1) * 128, :], zt[:, 0:128], "zo", reads=["zt"])
            for b in range(16):
                P.dma("sp", ossm_o[b].rearrange("(a p) n -> p a n", p=128),
                      zt[:, 0:768].rearrange("p (a n) -> p a n", a=6), "zo", reads=["zt"])
        P.barrier()
        P.replay(nc, st1a, "a", st)
        st1a.close()
        if stage >= 4 and not os.environ.get("KDBG_NOATT"):
            P = Prog()
            st1b = ExitStack()
            cur[0] = st1b
            qT = sbt("qTz", [128, 2, 2, TT], BF16)
            kT = sbt("kT", [128, 2, TT], BF16)
            UTg = sbt("UTg", [128, 2, TT], BF16)
            Sm = [sbt("Sm%d" % i, [128, 2, 2, 256], F32) for i in range(3)]
            Pb = [sbt("Pb%d" % i, [128, 2, 2, 256], BF16) for i in range(3)]
            PT = [sbt("PT%d" % i, [128, 8, 128], BF16) for i in range(3)]
            mxt = [sbt("mxt%d" % i, [128, 4], F32) for i in range(3)]
            ngt = [sbt("ngt%d" % i, [128, 4], F32) for i in range(3)]
            dnt = [sbt("dnt%d" % i, [128, 4], F32) for i in range(3)]
            stt = [sbt("stt%d" % i, [128, 4, 2], F32) for i in range(3)]
            ckt = [sbt("ckt%d" % i, [128, 4, 512], BF16) for i in range(2)]
            kTc = [sbt("kTc%d" % i, [128, 4, 2, 128], BF16) for i in range(2)]
            Vnb = [sbt("Vnb%d" % i, [128, 256], BF16) for i in range(2)]
            for hh_ in range(2):
                for c_ in range(2):
                    P.op("pool", lambda e, hh_=hh_, c_=c_: e.memset(qT[:, hh_, c_, :], 0.0), writes=["qT"])
            arot = Rot(range(3))
            crot = Rot(range(2))
            pjrot = Rot([0, 1, 2, 3])

            def attn_block(nq, Wo, q_ap, kp_ap, ko_ap, vp_ap, vo_ap, mask_ap, ut_ap, stat_dst, rkeys):
                W = 128 + Wo
                sl = arot.next()
                bS = (2 * sl, 2 * sl + 1)
                bT, bO = 6, 7
                smk, pbk, ptk, stk = "Sm%d" % sl, "Pb%d" % sl, "PT%d" % sl, "stt%d" % sl
                for c in range(2):
                    for hh in range(2):
                        P.op("pe", lambda e, c=c, hh=hh: e.matmul(
                            out=PS(bS[c])[0:nq, hh * 256:hh * 256 + 128], lhsT=q_ap(c, hh), rhs=kp_ap(c, hh),
                            start=True, stop=True), reads=rkeys, writes=PSK(bS[c]), inc=False)
                        P.op("pe", lambda e, c=c, hh=hh: e.matmul(
                            out=PS(bS[c])[0:nq, hh * 256 + 128:hh * 256 + W], lhsT=q_ap(c, hh), rhs=ko_ap(c, hh),
                            start=True, stop=True), reads=rkeys, writes=PSK(bS[c]), inc=(hh == 1))
                yield
                for c in range(2):
                    Sv = PS(bS[c])[0:nq, :].rearrange("p (h w) -> p h w", h=2)[:, :, 0:W]
                    P.op("dve", lambda e, c=c, Sv=Sv: e.scalar_tensor_tensor(
                        out=Sm[sl][0:nq, c, :, 0:W], in0=Sv, scalar=SCALE,
                        in1=mask_ap.unsqueeze(1).broadcast_to([nq, 2, W]), op0=ALU.mult, op1=ALU.add),
                        reads=PSK(bS[c]) + ["maskA", "maskF", "maskS"], writes=[(smk, c)])
                    P.op("dve", lambda e, c=c: e.tensor_reduce(out=mxt[sl][0:nq, 2 * c:2 * c + 2],
                                                               in_=Sm[sl][0:nq, c, :, 0:W], axis=AX.X, op=ALU.max),
                         reads=[(smk, c)], writes=[(stk, "mx", c)])
                    P.op("dve", lambda e, c=c: e.tensor_scalar(out=ngt[sl][0:nq, 2 * c:2 * c + 2],
                                                               in0=mxt[sl][0:nq, 2 * c:2 * c + 2], scalar1=-1.0,
                                                               scalar2=None, op0=ALU.mult),
                         reads=[(stk, "mx", c)], writes=[(stk, "ng", c)])
                    for hh in range(2):
                        P.op("act", lambda e, c=c, hh=hh: e.activation(
                            out=Pb[sl][0:nq, c, hh, 0:W], in_=Sm[sl][0:nq, c, hh, 0:W], func=AF.Exp,
                            bias=ngt[sl][0:nq, 2 * c + hh:2 * c + hh + 1], scale=1.0,
                            accum_out=dnt[sl][0:nq, 2 * c + hh:2 * c + hh + 1]),
                            reads=[(smk, c), (stk, "ng", c)], writes=[(pbk, c, hh), (stk, "dn", c, hh)])
                yield
                ptv = PS(bT).bitcast(BF16).rearrange("p (a q) -> p a q", a=8)
                for c in range(2):
                    for hh in range(2):
                        a = (c * 2 + hh) * 2
                        P.op("pe", lambda e, c=c, hh=hh, a=a: e.transpose(
                            out=ptv[:, a, 0:nq], in_=Pb[sl][0:nq, c, hh, 0:128], identity=identb[0:nq, 0:nq]),
                            reads=[(pbk, c, hh), "identb"], writes=PSK(bT), inc=False)
                        P.op("pe", lambda e, c=c, hh=hh, a=a: e.transpose(
                            out=ptv[0:Wo, a + 1, 0:nq], in_=Pb[sl][0:nq, c, hh, 128:W], identity=identb[0:nq, 0:nq]),
                            reads=[(pbk, c, hh), "identb"], writes=PSK(bT), inc=(c == 1 and hh == 1))
                yield
                ptv4 = ptv.rearrange("p (x j) q -> p x j q", j=2)
                pt4 = PT[sl][:, :, :].rearrange("p (x j) q -> p x j q", j=2)
                P.op("dve", lambda e: e.tensor_copy(out=pt4[:, :, 0, 0:nq], in_=ptv4[:, :, 0, 0:nq]),
                     reads=PSK(bT), writes=[ptk])
                P.op("dve", lambda e: e.tensor_copy(out=pt4[0:Wo, :, 1, 0:nq], in_=ptv4[0:Wo, :, 1, 0:nq]),
                     reads=PSK(bT) + [ptk], writes=[ptk])
                yield
                pov = PS(bO)[:, :].rearrange("p (a q) -> p a q", a=4)
                for c in range(2):
                    for hh in range(2):
                        a = c * 2 + hh
                        P.op("pe", lambda e, c=c, a=a: e.matmul(out=pov[:, a, 0:nq], lhsT=vp_ap(c),
                                                                rhs=PT[sl][:, 2 * a, 0:nq], start=True, stop=False),
                             reads=rkeys + [ptk], writes=PSK(bO), inc=False)
                        P.op("pe", lambda e, c=c, a=a: e.matmul(out=pov[:, a, 0:nq], lhsT=vo_ap(c),
                                                                rhs=PT[sl][0:Wo, 2 * a + 1, 0:nq], start=False,
                                                                stop=True),
                             reads=rkeys + [ptk], writes=PSK(bO), inc=(a == 3))
                yield
                for c in range(2):
                    for hh in range(2):
                        P.op("act", lambda e, c=c, hh=hh: e.activation(
                            out=ut_ap(c, hh), in_=pov[hh * 64:(hh + 1) * 64, c * 2 + hh, 0:nq], func=AF.Copy),
                            reads=PSK(bO), writes=[("UTg", c, hh)])
                P.op("dve", lambda e: e.tensor_copy(out=stt[sl][0:nq, :, 0], in_=mxt[sl][0:nq, :]),
                     reads=[(stk, "mx", 0), (stk, "mx", 1)], writes=[stk])
                P.op("dve", lambda e: e.tensor_copy(out=stt[sl][0:nq, :, 1], in_=dnt[sl][0:nq, :]),
                     reads=[(stk, "dn", c, hh) for c in range(2) for hh in range(2)] + [stk], writes=[stk])
                P.dma("sp", stat_dst, stt[sl][0:nq, :, :], "stato%d" % sl, reads=[stk])
                yield

            utk = [("UTg", c, hh) for c in range(2) for hh in range(2)]
            KATT = int(os.environ.get("KATT", "9"))
            for g in range(3 if KATT >= 2 else 1):
                d = DIL[g]
                swq = wrot.next()
                wqk = "wb%d" % swq
                P.dma("pool", wb[swq][:, :, 0:256], w_in_v[:, :, Q0 + g * 256:Q0 + (g + 1) * 256], wqk, writes=[wqk])
                P.dma("pool", wb[swq][:, :, 256:512], w_in_v[:, :, K0 + g * 256:K0 + (g + 1) * 256], wqk, writes=[wqk])
                for (dst, dkey, co) in ((qT, "qT", 0), (kT, "kT", 256)):
                    for c in range(2):
                        for (a0, an) in ((0, 512), (512, 512), (1024, 512), (1536, 512), (2048, 64)):
                            b = pjrot.next()
                            for kc in range(8):
                                P.op("pe", lambda e, kc=kc, b=b, c=c, co=co, a0=a0, an=an, swq=swq: e.matmul(
                                    out=PS(b)[:, 0:an], lhsT=wb[swq][:, kc, co + c * 128:co + (c + 1) * 128],
                                    rhs=xnT[:, kc, a0:a0 + an], start=(kc == 0), stop=(kc == 7)),
                                    reads=[wqk], writes=PSK(b), inc=(kc == 7))
                            if dkey == "qT":
                                for hh in range(2):
                                    P.op("act", lambda e, b=b, c=c, hh=hh, a0=a0, an=an: e.activation(
                                        out=qT[hh * 64:(hh + 1) * 64, hh, c, a0:a0 + an],
                                        in_=PS(b)[hh * 64:(hh + 1) * 64, 0:an], func=AF.Copy),
                                        reads=PSK(b), writes=[dkey])
                            else:
                                P.op("act", lambda e, b=b, c=c, dst=dst, a0=a0, an=an: e.activation(
                                    out=dst[:, c, a0:a0 + an], in_=PS(b)[:, 0:an], func=AF.Copy),
                                    reads=PSK(b), writes=[dkey])
                swv = wrot.next()
                wvk = "wb%d" % swv
                P.dma("pool", wb[swv][:, :, 0:256], w_in_v[:, :, V0 + g * 256:V0 + (g + 1) * 256], wvk, writes=[wvk])
                rk = ["qT", "kT", "kTp"]
                gens = []
                for blk in range(16):
                    u, r = blk // d, blk % d
                    tk = block_tokens(g, blk)
                    if u == 0:
                        pi = r
                        ptk_ = slice(r, r + d * 127 + 1, d)
                        kp = (lambda c, hh, ptk_=ptk_, g=g: kTp[g][:, c, ptk_])
                        vp = (lambda c, pi=pi, g=g: Vprev[g][:, pi, c * 128:(c + 1) * 128])
                        mk = maskF[:, :]
                    else:
                        ptok = block_tokens(g, blk - d)
                        kp = (lambda c, hh, ptok=ptok: kT[:, c, ptok])
                        vp = (lambda c, g=g, pb=blk - d: Vtok[g][:, pb, c * 128:(c + 1) * 128])
                        mk = maskA[:, :]
                    gens.append(attn_block(
                        128, 128,
                        (lambda c, hh, tk=tk: qT[:, hh, c, tk]),
                        kp,
                        (lambda c, hh, tk=tk: kT[:, c, tk]),
                        vp,
                        (lambda c, g=g, blk=blk: Vtok[g][:, blk, c * 128:(c + 1) * 128]),
                        mk,
                        (lambda c, hh, tk=tk: UTg[hh * 64:(hh + 1) * 64, c, tk]),
                        stats_d[tk, 4 * g:4 * g + 4, :], rk + utk))
                run_pipelined(gens, 3)
                ns = 1 if d == 1 else 4
                for b in range(16 if KATT >= 3 else 0):
                    cs = crot.next()
                    ckk, kck, vnk = "ckt%d" % cs, "kTc%d" % cs, "Vnb%d" % cs
                    if d == 1:
                        srcv = ck[0][b].rearrange("(m s) c -> m s c", s=1)
                    elif d == 4:
                        srcv = ck[1][b].rearrange("(m s) c -> m s c", s=4)
                    else:
                        srcv = ck[2][b].rearrange("(m s) c -> m s c", s=16)[:, 0:4, :]
                    P.dma("pool", ckt[cs][:, 0:ns, :], srcv, ckk, writes=[ckk])
                    bq = pjrot.next()
                    kv8 = PS(bq).bitcast(BF16).rearrange("p (a q) -> p a q", a=8)
                    for si in range(ns):
                        for c in range(2):
                            P.op("pe", lambda e, si=si, c=c, cs=cs, kv8=kv8: e.transpose(
                                out=kv8[:, si * 2 + c, :], in_=ckt[cs][:, si, c * 128:(c + 1) * 128],
                                identity=identb[:, :]), reads=[ckk, "identb"], writes=PSK(bq),
                                inc=(si == ns - 1 and c == 1))
                    P.op("dve", lambda e, cs=cs, kv8=kv8, ns=ns: e.tensor_copy(
                        out=kTc[cs][:, 0:ns, :, :].rearrange("p s c q -> p (s c) q"), in_=kv8[:, 0:2 * ns, :]),
                        reads=PSK(bq), writes=[kck])
                    bv = pjrot.next()
                    for kc in range(8):
                        P.op("pe", lambda e, kc=kc, b=b, bv=bv, swv=swv: e.matmul(
                            out=PS(bv)[0:4, 0:256], lhsT=xnT[:, kc, TO + 4 * b:TO + 4 * b + 4],
                            rhs=wb[swv][:, kc, 0:256], start=(kc == 0), stop=(kc == 7)),
                            reads=[wvk], writes=PSK(bv), inc=(kc == 7))
                    P.op("act", lambda e, cs=cs, bv=bv: e.activation(out=Vnb[cs][0:4, :], in_=PS(bv)[0:4, 0:256], func=AF.Copy),
                         reads=PSK(bv), writes=[vnk])
                    gens = []
                    for si in range(ns):
                        if d == 1:
                            nq, t0q, mk = 4, TO + 4 * b, maskA[0:4, 0:132]
                        else:
                            nq, t0q, mk = 1, TO + 4 * b + si, maskS[0:1, si, 0:132]
                        gens.append(attn_block(
                            nq, 4,
                            (lambda c, hh, t0q=t0q, nq=nq: qT[:, hh, c, t0q:t0q + nq]),
                            (lambda c, hh, si=si, cs=cs: kTc[cs][:, si, c, :]),
                            (lambda c, hh, b=b: kT[:, c, TO + 4 * b:TO + 4 * b + 4]),
                            (lambda c, si=si, cs=cs: ckt[cs][:, si, 256 + c * 128:256 + (c + 1) * 128]),
                            (lambda c, cs=cs: Vnb[cs][0:4, c * 128:(c + 1) * 128]),
                            mk,
                            (lambda c, hh, t0q=t0q, nq=nq: UTg[hh * 64:(hh + 1) * 64, c, t0q:t0q + nq]),
                            stats_d[t0q:t0q + nq, 4 * g:4 * g + 4, :], rk + utk + [ckk, kck, vnk]))
                    run_pipelined(gens, 3)
                for c in range(2):
                    P.dma("sp", mixd[2 * g + c, :, :], UTg[:, c, :], "uto", reads=utk)
            P.barrier()
            P.replay(nc, st1b, "c", st)
            st1b.close()
        st1.close()

        P = Prog()
        st2 = ExitStack()
        cur[0] = st2
        w_out_v = w_out.rearrange("(k p) n -> p k n", p=128)
        w_gu_v = w_gu.rearrange("(k p) n -> p k n", p=128)
        w_down_v = w_down.rearrange("(k p) n -> p k n", p=128)
        wbig = sbt("wbig", [128, 22, 512], BF16)
        mixT = sbt("mixg", [128, 12, 1088], BF16)
        hbuf = sbt("hbuf", [128, 9, 1024], F32)
        hnT = sbt("hnT", [128, 8, 1088], BF16)
        aT = sbt("aT", [128, 22, 1088], BF16)
        sgt = [sbt("sgt%d" % i, [128, 512], BF16) for i in range(2)]
        yo = [sbt("yo0", [128, 1024], F32)] * 2
        fst = [sbt("fst%d" % i, [128, 4], F32) for i in range(2)]
        stmt = [sbt("stm%d" % i, [128, 12, 2], F32) for i in range(2)]
        Wtt = [sbt("Wt%d" % i, [128, 32], F32) for i in range(2)]
        Wxt = [sbt("Wx%d" % i, [128, 768], BF16) for i in range(2)]
        srot = Rot(range(2))
        prot2 = Rot([0, 1, 2, 3])
        grot = Rot([4, 5, 6, 7])
        groups = [(0, 1024), (1024, 1088)]
        def ffn_group(t0, ntok):
            tiles = [(i // 128, t0 + i, min(128, ntok - i)) for i in range(0, ntok, 128)]
            for (ti, tb, T) in tiles:
                src = xo[tb:tb + T, :] if tb < TO else xs[tb - TO:tb - TO + T, :]
                P.dma("sp", hbuf[0:T, ti, :], src, "hb%d" % ti, writes=[("hbuf", ti)])
            if stage >= 4:
                P.dma("sp", mixT[:, :, 0:ntok], mixd[:, :, t0:t0 + ntok].rearrange("j p t -> p j t"), "mixg",
                      writes=["mixT"])
            else:
                P.op("dve", lambda e: e.memset(mixT[:, :, :], 0.0), writes=["mixT"])
            if stage >= 4 and not os.environ.get("KDBG_NOATT") and int(os.environ.get("KATT", "9")) >= 4:
                for (ti, tb, T) in tiles:
                    ms_ = srot.next()
                    stm, Wt, Wx = stmt[ms_], Wtt[ms_], Wxt[ms_]
                    sk = "stm%d" % ms_
                    P.dma("sp", stm[0:T, :, :], stats_d[tb:tb + T, :, :], sk, writes=[sk])
                    m3 = stm[0:T, :, :].rearrange("p (g h) t -> p g h t", g=3)
                    mo = lambda fn: P.op("dve", fn, reads=[sk], writes=[sk])
                    W3 = Wt[0:T, 0:12].rearrange("p (g h) -> p g h", g=3)
                    E3 = Wt[0:T, 12:24].rearrange("p (g h) -> p g h", g=3)
                    Mx, Dn = Wt[0:T, 24:28], Wt[0:T, 28:32]
                    mo(lambda e, m3=m3, Mx=Mx: e.tensor_tensor(out=Mx, in0=m3[:, 0, :, 0], in1=m3[:, 1, :, 0], op=ALU.max))
                    mo(lambda e, m3=m3, Mx=Mx: e.tensor_tensor(out=Mx, in0=Mx, in1=m3[:, 2, :, 0], op=ALU.max))
                    mo(lambda e, m3=m3, Mx=Mx, E3=E3, T=T: e.tensor_tensor(
                        out=E3, in0=m3[:, :, :, 0], in1=Mx.unsqueeze(1).broadcast_to([T, 3, 4]), op=ALU.subtract))
                    P.op("act", lambda e, E3=E3: e.activation(out=E3, in_=E3, func=AF.Exp), reads=[sk], writes=[sk])
                    mo(lambda e, m3=m3, E3=E3, W3=W3: e.tensor_tensor(out=W3, in0=E3, in1=m3[:, :, :, 1], op=ALU.mult))
                    mo(lambda e, W3=W3, Dn=Dn: e.tensor_tensor(out=Dn, in0=W3[:, 0, :], in1=W3[:, 1, :], op=ALU.add))
                    mo(lambda e, W3=W3, Dn=Dn: e.tensor_tensor(out=Dn, in0=Dn, in1=W3[:, 2, :], op=ALU.add))
                    mo(lambda e, Dn=Dn: e.reciprocal(out=Dn, in_=Dn))
                    mo(lambda e, E3=E3, W3=W3, Dn=Dn, T=T: e.tensor_tensor(
                        out=W3, in0=E3, in1=Dn.unsqueeze(1).broadcast_to([T, 3, 4]), op=ALU.mult))
                    P.op("dve", lambda e, Wt=Wt, Wx=Wx, T=T: e.tensor_copy(
                        out=Wx[0:T, :].rearrange("p (a d) -> p a d", a=12),
                        in_=Wt[0:T, 0:12].unsqueeze(2).broadcast_to([T, 12, 64])), reads=[sk], writes=["Wx%d" % ms_])
                    bm = prot2.next()
                    pw = PS(bm).bitcast(BF16).rearrange("p (a q) -> p a q", a=8)
                    for j in range(6):
                        P.op("pe", lambda e, j=j, pw=pw, Wx=Wx, T=T: e.transpose(
                            out=pw[:, j, 0:T], in_=Wx[0:T, j * 128:(j + 1) * 128], identity=identb[0:T, 0:T]),
                            reads=["Wx%d" % ms_, "identb"], writes=PSK(bm), inc=(j == 5))
                    P.op("dve", lambda e, pw=pw, tb=tb, T=T: e.tensor_tensor(
                        out=mixT[:, 0:6, tb - t0:tb - t0 + T], in0=mixT[:, 0:6, tb - t0:tb - t0 + T],
                        in1=pw[:, 0:6, 0:T], op=ALU.mult), reads=PSK(bm) + ["mixT"], writes=["mixT"])
            for half in range(2):
                P.dma("pool", wbig[:, 0:6, :], w_out_v[:, 0:6, half * 512:(half + 1) * 512], "wbigA", writes=["wbigA"])
                P.dma("pool", wbig[:, 6:12, :], w_out_v[:, 6:12, half * 512:(half + 1) * 512], "wbigB", writes=["wbigB"])
                for (ti, tb, T) in tiles:
                    b = prot2.next()
                    for kc in range(12):
                        P.op("pe", lambda e, kc=kc, b=b, tb=tb, T=T: e.matmul(
                            out=PS(b)[0:T, :], lhsT=mixT[:, kc, tb - t0:tb - t0 + T], rhs=wbig[:, kc, :],
                            start=(kc == 0), stop=(kc == 11)), reads=["wbigA" if kc < 6 else "wbigB", "mixT"], writes=PSK(b), inc=(kc == 11))
                    P.op("dve", lambda e, b=b, ti=ti, T=T, half=half: e.tensor_tensor(
                        out=hbuf[0:T, ti, half * 512:(half + 1) * 512], in0=PS(b)[0:T, :],
                        in1=hbuf[0:T, ti, half * 512:(half + 1) * 512], op=ALU.add),
                        reads=PSK(b) + [("hbuf", ti)], writes=[("hbuf", ti)])
            for (ti, tb, T) in tiles:
                norm_T(hbuf[0:T, ti, :], ("hbuf", ti), T, gffn[:, :], "gffn", hnT[:, :, ti * 128:ti * 128 + T],
                       [("hnT", ti)], prot2.next(), from_dram=False)
            hk = [("hnT", ti) for (ti, _, _) in tiles]
            tranges = [(a0, min(512, ntok - a0)) for a0 in range(0, ntok, 512)]
            for j0 in range(0, 22, 2):
                sw = wrot.next()
                wkey = "wb%d" % sw
                P.dma("pool", wb[sw][:, :, 0:256], w_gu_v[:, :, j0 * 128:(j0 + 2) * 128], wkey, writes=[wkey])
                P.dma("pool", wb[sw][:, :, 256:512], w_gu_v[:, :, 2816 + j0 * 128:2816 + (j0 + 2) * 128], wkey,
                      writes=[wkey])
                for jj in range(2):
                    j = j0 + jj
                    for (a0, an) in tranges:
                        bg, bu = grot.next(), grot.next()
                        for (bb, c0) in ((bg, jj * 128), (bu, 256 + jj * 128)):
                            for kc in range(8):
                                P.op("pe", lambda e, kc=kc, bb=bb, c0=c0, sw=sw, a0=a0, an=an: e.matmul(
                                    out=PS(bb)[:, 0:an], lhsT=wb[sw][:, kc, c0:c0 + 128], rhs=hnT[:, kc, a0:a0 + an],
                                    start=(kc == 0), stop=(kc == 7)), reads=[wkey] + hk, writes=PSK(bb), inc=(kc == 7))
                        ss_ = srot.next()
                        P.op("act", lambda e, bg=bg, ss_=ss_, an=an: e.activation(
                            out=sgt[ss_][:, 0:an], in_=PS(bg)[:, 0:an], func=AF.Silu), reads=PSK(bg),
                            writes=["sgt%d" % ss_])
                        P.op("dve", lambda e, bu=bu, ss_=ss_, j=j, a0=a0, an=an: e.tensor_tensor(
                            out=aT[:, j, a0:a0 + an], in0=sgt[ss_][:, 0:an], in1=PS(bu)[:, 0:an], op=ALU.mult),
                            reads=PSK(bu) + ["sgt%d" % ss_], writes=[("aT", j)])
            ak = [("aT", j) for j in range(22)]
            for half in range(2):
                P.dma("pool", wbig[:, 0:6, :], w_down_v[:, 0:6, half * 512:(half + 1) * 512], "wbigA", writes=["wbigA"])
                P.dma("pool", wbig[:, 6:22, :], w_down_v[:, 6:22, half * 512:(half + 1) * 512], "wbigB", writes=["wbigB"])
                for (ti, tb, T) in tiles:
                    b = prot2.next()
                    for fc in range(22):
                        P.op("pe", lambda e, fc=fc, b=b, ti=ti, T=T: e.matmul(
                            out=PS(b)[0:T, :], lhsT=aT[:, fc, ti * 128:ti * 128 + T], rhs=wbig[:, fc, :],
                            start=(fc == 0), stop=(fc == 21)), reads=["wbigA" if fc < 6 else "wbigB"] + ak, writes=PSK(b), inc=(fc == 21))
                    P.op("dve", lambda e, b=b, ti=ti, T=T, half=half: e.tensor_tensor(
                        out=hbuf[0:T, ti, half * 512:(half + 1) * 512], in0=PS(b)[0:T, :],
                        in1=hbuf[0:T, ti, half * 512:(half + 1) * 512], op=ALU.add),
                        reads=PSK(b) + [("hbuf", ti)], writes=[("hbuf", ti)])
            for (ti, tb, T) in tiles:
                s_ = srot.next()
                fs, fk = fst[s_], "fst%d" % s_
                P.op("act", lambda e, ti=ti, T=T, fs=fs: e.activation(
                    out=junk[0:T, :], in_=hbuf[0:T, ti, :], func=AF.Square, scale=1.0 / 32, accum_out=fs[0:T, 0:1]),
                    reads=[("hbuf", ti)], writes=["junk", fk])
                P.op("dve", lambda e, T=T, fs=fs: e.tensor_scalar(out=fs[0:T, 1:2], in0=fs[0:T, 0:1], scalar1=EPS,
                                                                  scalar2=None, op0=ALU.add), reads=[fk], writes=[fk])
                P.op("act", lambda e, T=T, fs=fs: e.activation(out=fs[0:T, 2:3], in_=fs[0:T, 1:2], func=AF.Ln),
                     reads=[fk], writes=[fk])
                P.op("act", lambda e, T=T, fs=fs: e.activation(out=fs[0:T, 3:4], in_=fs[0:T, 2:3], func=AF.Exp,
                                                               scale=-0.5), reads=[fk], writes=[fk])
                P.op("dve", lambda e, ti=ti, T=T, fs=fs, s_=s_: e.scalar_tensor_tensor(
                    out=yo[s_][0:T, :], in0=hbuf[0:T, ti, :], scalar=fs[0:T, 3:4], in1=gfin[0:T, :],
                    op0=ALU.mult, op1=ALU.mult), reads=[("hbuf", ti), fk, "gfin"], writes=["yo0"])
                P.dma("sp", y_o[tb:tb + T, :], yo[s_][0:T, :], "yout0", reads=["yo0"])

        for (t0_, ntok_) in groups:
            ffn_group(t0_, ntok_)
        P.barrier()
        P.replay(nc, st2, "b", st)
        st2.close()
    return nc


_NC_CACHE = {}


def _consts():
    ident = np.eye(128, dtype=np.float32)
    i = np.arange(128)
    triu = (i[:, None] <= i[None, :]).astype(np.float32)
    sl = (i[:, None] > i[None, :]).astype(np.float32)
    ones = np.ones((128, 128), np.float32)
    cm = np.concatenate([ident, triu, sl, ones], axis=1)
    q = np.arange(128)[:, None]
    k = np.arange(256)[None, :]
    valid = (k >= q) & (k <= q + 128)
    maskA = np.where(valid, 0.0, NEG).astype(np.float32)
    maskS = np.full((4, 256), NEG, np.float32)
    for s in range(4):
        maskS[s, 0:128] = 0.0
        maskS[s, 128 + s] = 0.0
    return cm, maskA, maskS.reshape(1, 1024)


def kernel(x_prompt, x_sample, cache_kv_d1, cache_kv_d4, cache_kv_d16, state_conv, state_ssm,
           norm_mix, w_in, conv_w, conv_b, dt_bias, a_log, d_skip, norm_ssd, w_out, norm_ffn,
           w_gate_up, w_down, norm_final, _stage=99):
    f = lambda a: np.ascontiguousarray(np.asarray(a, dtype=np.float32))
    if _stage not in _NC_CACHE:
        _NC_CACHE[_stage] = build(_stage)
    nc = _NC_CACHE[_stage]
    cm, maskA, maskS = _consts()
    bc = lambda v, n: f(np.broadcast_to(np.asarray(v, np.float32).reshape(1, -1), (128, n)))
    col = lambda v, k: f(np.asarray(v, np.float32).reshape(k, 128).T)
    cw = np.asarray(conv_w, np.float32)[0]
    convw = f(cw.reshape(4, 14, 128).transpose(2, 1, 0).reshape(128, 56))
    shared = {
        "w_in": f(w_in[0]), "w_out": f(w_out[0]), "w_gu": f(w_gate_up[0]), "w_down": f(w_down[0]),
        "gmix": col(norm_mix[0], 8), "gffn": col(norm_ffn[0], 8), "gfin": bc(norm_final, 1024),
        "gssd": bc(norm_ssd[0], 768), "convw": convw, "convb": col(conv_b[0], 14),
        "dtb": bc(dt_bias[0], 12), "alog": bc(a_log[0], 12), "dsk": bc(d_skip[0], 12),
        "maskA": maskA, "maskS": maskS, "cmats": cm,
    }
    xpn = np.asarray(x_prompt, np.float32)
    xsn = np.asarray(x_sample, np.float32)
    caches = [np.asarray(cache_kv_d1, np.float32)[0], np.asarray(cache_kv_d4, np.float32)[0],
              np.asarray(cache_kv_d16, np.float32)[0]]
    sc = np.asarray(state_conv, np.float32)[0]
    ss = np.asarray(state_ssm, np.float32)[0]
    in_maps = []
    for c in range(8):
        b, half = c // 2, c % 2
        m = dict(shared)
        m["xo"] = f(xpn[b, half * TO:(half + 1) * TO])
        m["xp"] = f(xpn[b, 0:TO]) if half == 1 else np.zeros((TO, 1024), np.float32)
        m["xs"] = f(xsn[16 * c:16 * c + 16].reshape(64, 1024))
        for nm, ca, lb in zip(("ck1", "ck4", "ck16"), caches, (128, 512, 2048)):
            m[nm] = f(ca[16 * c:16 * c + 16].reshape(16, lb, 512))
        m["sconv"] = f(sc[16 * c:16 * c + 16].reshape(48, 1792))
        m["sssm"] = f(ss[16 * c:16 * c + 16].reshape(16, 768, 128))
        mf = maskA.copy()
        if half == 0:
            mf[:, 0:128] = NEG
        m["maskF"] = mf
        m["hp"] = np.full((128, 1), float(half), np.float32)
        in_maps.append(m)
    res = run_bass_kernel_spmd(nc, in_maps, core_ids=list(range(8))).results
    y_prompt = np.zeros((4, 4096, 1024), np.float32)
    y_sample = np.zeros((128, 4, 1024), np.float32)
    for c in range(8):
        b, half = c // 2, c % 2
        y_prompt[b, half * TO:(half + 1) * TO] = res[c]["y"][0:TO]
        y_sample[16 * c:16 * c + 16] = res[c]["y"][TO:TT].reshape(16, 4, 1024)
    odd = [1, 3, 5, 7]
    p_kv = [np.stack([res[c][n] for c in odd]).reshape(1, 4, lb, 2, 4, 64)
            for n, lb in (("pkv1", 128), ("pkv4", 512), ("pkv16", 2048))]
    p_conv = np.stack([res[c]["pconv"] for c in odd]).reshape(1, 4, 3, 1792)
    p_ssm = np.stack([res[c]["pssm"] for c in odd]).reshape(1, 4, 12, 64, 128)
    s_kv = [np.concatenate([res[c][n] for c in range(8)]).reshape(1, 128, lb, 2, 4, 64)
            for n, lb in (("skv1", 128), ("skv4", 512), ("skv16", 2048))]
    s_conv = np.concatenate([res[c]["oconv"] for c in range(8)]).reshape(1, 128, 3, 1792)
    s_ssm = np.concatenate([res[c]["ossm"] for c in range(8)]).reshape(1, 128, 12, 64, 128)
    return (y_prompt, y_sample, p_kv[0], p_kv[1], p_kv[2], p_conv, p_ssm,
            s_kv[0], s_kv[1], s_kv[2], s_conv, s_ssm)
```

```python
import numpy as np
from contextlib import ExitStack
import concourse.bass as bass
import concourse.mybir as mybir
from concourse.bass_utils import run_bass_kernel_spmd

F32 = mybir.dt.float32
BF16 = mybir.dt.bfloat16
AF = mybir.ActivationFunctionType
ALU = mybir.AluOpType
AX = mybir.AxisListType

ENGS = ("pe", "act", "dve", "pool", "sp")
NEG = -1.0e30
DIL = (1, 4, 16)
Q0, K0, V0, Z0, X0, DT0 = 0, 768, 1536, 2304, 3072, 4864
TO, TS = 2048, 64
TT = TO + TS
SCALE = 0.125
EPS = 1e-5


class Prog:
    def __init__(self):
        self.ops = {e: [] for e in ENGS}
        self.cnt = {e: 0 for e in ENGS}
        self.waited = {e: {} for e in ENGS}
        self.dcnt = {}
        self.lastw = {}
        self.readers = {}

    def _emit(self, eng, fn, tok, hasinc, reads, writes, isdma):
        deps = {}

        def add(t):
            if t is not None and deps.get(t[0], 0) < t[1]:
                deps[t[0]] = t[1]
        for r in reads:
            add(self.lastw.get(r))
            if isinstance(r, str) and r.startswith("ps"):
                for s, v in self.readers.get(r, {}).items():
                    add((s, v))
        for w in writes:
            add(self.lastw.get(w))
            for s, v in self.readers.get(w, {}).items():
                add((s, v))
        waits = []
        own = "E" + eng
        for s, v in deps.items():
            if s == own and eng == "pe" and not isdma:
                continue
            if self.waited[eng].get(s, 0) >= v:
                continue
            self.waited[eng][s] = v
            waits.append((s, v))
        self.ops[eng].append((waits, fn, tok if hasinc else None, isdma))
        for r in reads:
            d = self.readers.setdefault(r, {})
            if d.get(tok[0], 0) < tok[1]:
                d[tok[0]] = tok[1]
        for w in writes:
            self.lastw[w] = tok
            self.readers[w] = {}
        return tok

    def op(self, eng, fn, reads=(), writes=(), inc=True):
        if inc:
            self.cnt[eng] += 1
            tok = ("E" + eng, self.cnt[eng])
        else:
            tok = ("E" + eng, self.cnt[eng] + 1)
        return self._emit(eng, fn, tok, inc, reads, writes, False)

    def dma(self, eng, out, in_, stream, reads=(), writes=()):
        self.dcnt[stream] = self.dcnt.get(stream, 0) + 16
        tok = ("D" + stream, self.dcnt[stream])
        return self._emit(eng, lambda e: e.dma_start(out=out, in_=in_), tok, True, reads, writes, True)

    def barrier(self):
        allw = [("E" + e, self.cnt[e]) for e in ENGS if self.cnt[e] > 0]
        allw += [("D" + s, v) for s, v in self.dcnt.items()]
        for eng in ENGS:
            waits = []
            for s, v in allw:
                if self.waited[eng].get(s, 0) >= v:
                    continue
                self.waited[eng][s] = v
                waits.append((s, v))
            self.ops[eng].append((waits, None, None, False))

    def replay(self, nc, stack, pfx="a", semstack=None):
        sems = {}
        for n in ["E" + e for e in ENGS] + ["D" + s for s in self.dcnt]:
            sems[n] = (semstack or stack).enter_context(nc.semaphore(pfx + n))
        block = stack.enter_context(nc.Block())
        handles = {"pe": block.tensor, "act": block.scalar, "dve": block.vector,
                   "pool": block.gpsimd, "sp": block.sync}
        for eng in ENGS:
            ops = self.ops[eng]

            def body(e, ops=ops):
                for (waits, fn, tok, isdma) in ops:
                    for (s, v) in waits:
                        e.wait_ge(sems[s], v)
                    if fn is None:
                        continue
                    ins = fn(e)
                    if tok is not None:
                        ins.then_inc(sems[tok[0]], 16 if isdma else 1)
            handles[eng](body)


class Rot:
    def __init__(self, items):
        self.items = list(items)
        self.i = 0

    def next(self):
        v = self.items[self.i % len(self.items)]
        self.i += 1
        return v


def run_pipelined(gens, depth):
    active = []
    gens = list(gens)
    gi = 0
    while gi < len(gens) or active:
        if gi < len(gens) and len(active) < depth:
            active.append(gens[gi])
            gi += 1
        nxt = []
        for g in active:
            try:
                next(g)
                nxt.append(g)
            except StopIteration:
                pass
        active = nxt


def build(stage=99):
    nc = bass.Bass("TRN2", target_bir_lowering=False)
    dram = lambda n, s, k="ExternalInput": nc.dram_tensor(n, s, F32, kind=k).ap()
    xo = dram("xo", [TO, 1024])
    xp = dram("xp", [TO, 1024])
    xs = dram("xs", [TS, 1024])
    ck = [dram("ck1", [16, 128, 512]), dram("ck4", [16, 512, 512]), dram("ck16", [16, 2048, 512])]
    sconv_in = dram("sconv", [48, 1792])
    sssm_in = dram("sssm", [16, 768, 128])
    w_in = dram("w_in", [1024, 4876])
    w_out = dram("w_out", [1536, 1024])
    w_gu = dram("w_gu", [1024, 5632])
    w_down = dram("w_down", [2816, 1024])
    gmix_d = dram("gmix", [128, 8])
    gffn_d = dram("gffn", [128, 8])
    gfin_d = dram("gfin", [128, 1024])
    gssd_d = dram("gssd", [128, 768])
    convw_d = dram("convw", [128, 14 * 4])
    convb_d = dram("convb", [128, 14])
    dtb_d = dram("dtb", [128, 12])
    alog_d = dram("alog", [128, 12])
    dsk_d = dram("dsk", [128, 12])
    maskA_d = dram("maskA", [128, 256])
    maskF_d = dram("maskF", [128, 256])
    maskS_d = dram("maskS", [1, 4 * 256])
    hp_d = dram("hp", [128, 1])
    cmats_d = dram("cmats", [128, 4 * 128])
    y_o = dram("y", [TT, 1024], "ExternalOutput")
    pkv_o = [dram("pkv1", [128, 512], "ExternalOutput"), dram("pkv4", [512, 512], "ExternalOutput"),
             dram("pkv16", [2048, 512], "ExternalOutput")]
    pconv_o = dram("pconv", [3, 1792], "ExternalOutput")
    pssm_o = dram("pssm", [768, 128], "ExternalOutput")
    skv_o = [dram("skv1", [16, 128, 512], "ExternalOutput"), dram("skv4", [16, 512, 512], "ExternalOutput"),
             dram("skv16", [16, 2048, 512], "ExternalOutput")]
    oconv_o = dram("oconv", [16, 3, 1792], "ExternalOutput")
    ossm_o = dram("ossm", [16, 768, 128], "ExternalOutput")
    stats_d = dram("stats", [TT, 12, 2], "Internal")
    mixd = nc.dram_tensor("mixd", [12, 128, TT], BF16, kind="Internal").ap()
    w_in_v = w_in.rearrange("(k p) n -> p k n", p=128)

    with ExitStack() as st:
        cur = [st]
        sbt = lambda n, s, d: cur[0].enter_context(nc.sbuf_tensor("sb_" + n, s, d))
        psum = st.enter_context(nc.psum_tensor("psum", [128, 4096], F32))
        P = Prog()

        def PS(b, nb=1):
            return psum[:, b * 512:(b + nb) * 512]

        def PSK(b, nb=1):
            return ["ps%d" % i for i in range(b, b + nb)]

        cm = sbt("cm", [128, 4, 128], F32)
        identb = sbt("identb", [128, 128], BF16)
        gmix = sbt("gmix", [128, 8], F32)
        gffn = sbt("gffn", [128, 8], F32)
        gfin = sbt("gfin", [128, 1024], F32)
        gssd = sbt("gssd", [128, 768], F32)
        convw = sbt("convw", [128, 14, 4], F32)
        convb = sbt("convb", [128, 14], F32)
        dtb = sbt("dtb", [128, 12], F32)
        abc = sbt("abc", [128, 12], F32)
        dsk = sbt("dsk", [128, 12], F32)
        maskA = sbt("maskA", [128, 256], F32)
        maskF = sbt("maskF", [128, 256], F32)
        maskS = sbt("maskS", [128, 4, 256], F32)
        hp = sbt("hp", [128, 1], F32)
        cl = [("cm", cm[:].rearrange("p a b -> p (a b)"), cmats_d), ("gmix", gmix[:], gmix_d),
              ("gffn", gffn[:], gffn_d), ("gfin", gfin[:], gfin_d), ("gssd", gssd[:], gssd_d),
              ("convw", convw[:].rearrange("p a b -> p (a b)"), convw_d), ("convb", convb[:], convb_d),
              ("dtb", dtb[:], dtb_d), ("abc", abc[:], alog_d), ("dsk", dsk[:], dsk_d),
              ("maskA", maskA[:], maskA_d), ("maskF", maskF[:], maskF_d),
              ("maskS", maskS[0:1].rearrange("p a b -> p (a b)"), maskS_d), ("hp", hp[:], hp_d)]
        for (k, t, d) in cl:
            P.dma("sp", t, d[:, :], "c_" + k, writes=[k])
        identf = cm[:, 0, :]
        triu = cm[:, 1, :]
        slm = cm[:, 2, :]
        onesm = cm[:, 3, :]
        P.op("dve", lambda e: e.tensor_copy(out=identb[:], in_=identf), reads=["cm"], writes=["identb"])
        P.op("act", lambda e: e.activation(out=abc[:], in_=abc[:], func=AF.Exp), reads=["abc"], writes=["abc"])
        P.op("dve", lambda e: e.tensor_scalar(out=abc[:], in0=abc[:], scalar1=-1.0, scalar2=None, op0=ALU.mult),
             reads=["abc"], writes=["abc"])

        xin = [sbt("xin%d" % i, [128, 1024], F32) for i in range(2)]
        xbb = [sbt("xbb%d" % i, [128, 1024], BF16) for i in range(2)]
        junk = sbt("junk", [128, 1024], BF16)
        nstat = [sbt("nstat%d" % i, [128, 4], F32) for i in range(2)]
        wb = [sbt("wb%d" % i, [128, 8, 512], BF16) for i in range(2)]
        wrot = Rot(range(2))
        nrot = Rot(range(2))
        st1 = ExitStack()
        cur[0] = st1
        xnT = sbt("xnT", [128, 8, TT], BF16)
        kvf = [sbt("kvf%d" % i, [128, 512], F32) for i in range(2)]
        kvrot = Rot(range(2))

        def load_w(wv, c0, ncols, nk=8):
            s = wrot.next()
            key = "wb%d" % s
            P.dma("pool", wb[s][:, 0:nk, 0:ncols], wv[:, 0:nk, c0:c0 + ncols], key, writes=[key])
            return wb[s], key

        def xkeys(t0, tn):
            return [("xnT", i) for i in range(t0 // 128, (t0 + tn - 1) // 128 + 1)]

        def norm_T(src, src_key, T, gcol, gkey, dst, dst_keys, psb, from_dram=True):
            s = nrot.next()
            if from_dram:
                xt = xin[s]
                P.dma("sp", xt[0:T, :], src, "xin%d" % s, writes=["xin%d" % s])
                xa, xk = xt[0:T, :], "xin%d" % s
            else:
                xa, xk = src, src_key
            ns = nstat[s]
            nk = "nstat%d" % s
            P.op("act", lambda e: e.activation(out=junk[0:T, :], in_=xa, func=AF.Square, scale=1.0 / 32,
                                               accum_out=ns[0:T, 0:1]), reads=[xk], writes=["junk", nk])
            P.op("dve", lambda e: e.tensor_scalar(out=ns[0:T, 1:2], in0=ns[0:T, 0:1], scalar1=EPS, scalar2=None,
                                                  op0=ALU.add), reads=[nk], writes=[nk])
            P.op("act", lambda e: e.activation(out=ns[0:T, 2:3], in_=ns[0:T, 1:2], func=AF.Ln), reads=[nk], writes=[nk])
            P.op("act", lambda e: e.activation(out=ns[0:T, 3:4], in_=ns[0:T, 2:3], func=AF.Exp, scale=-0.5),
                 reads=[nk], writes=[nk])
            xb = xbb[s]
            bk = "xbb%d" % s
            P.op("dve", lambda e: e.tensor_scalar(out=xb[0:T, :], in0=xa, scalar1=ns[0:T, 3:4], scalar2=None,
                                                  op0=ALU.mult), reads=[xk, nk], writes=[bk])
            pt = PS(psb).bitcast(BF16).rearrange("p (k t) -> p k t", k=8)
            for k in range(8):
                P.op("pe", lambda e, k=k: e.transpose(out=pt[:, k, 0:T], in_=xb[0:T, k * 128:(k + 1) * 128],
                                                      identity=identb[0:T, 0:T]),
                     reads=[bk, "identb"], writes=PSK(psb), inc=(k == 7))
            P.op("dve", lambda e: e.tensor_tensor(out=dst, in0=pt[:, :, 0:T],
                                                  in1=gcol.unsqueeze(2).broadcast_to([128, 8, T]), op=ALU.mult),
                 reads=PSK(psb) + [gkey], writes=dst_keys)

        prot = Rot([0, 1, 2, 3])

        def norm_tokens(srcd, ntok, base):
            for i in range(0, ntok, 128):
                T = min(128, ntok - i)
                norm_T(srcd[i:i + T, :], None, T, gmix[:, :], "gmix", xnT[:, :, base + i:base + i + T],
                       [("xnT", (base + i) // 128)], prot.next())

        def tokslice(start, d, n=128):
            return slice(start, start + d * (n - 1) + 1, d)

        Vtok = [sbt("Vtok%d" % g, [128, 16, 256], BF16) for g in range(3)]
        Vprev = [sbt("Vprev0", [128, 1, 256], BF16), sbt("Vprev1", [128, 4, 256], BF16),
                 sbt("Vprev2", [128, 16, 256], BF16)]
        Vnew = [sbt("Vnew%d" % g, [128, 256], BF16) for g in range(3)]
        kTp = [sbt("kTp0", [128, 2, 128], BF16), sbt("kTp1", [128, 2, 512], BF16), sbt("kTp2", [128, 2, 2048], BF16)]

        def block_tokens(g, blk):
            d = DIL[g]
            u, r = blk // d, blk % d
            return tokslice(u * 128 * d + r, d)

        def kv_block(g, wt, wkey, cols, T, want_k, vdst, vkey, out_dma):
            b = prot.next()
            c0 = 0 if want_k else 256
            for kc in range(8):
                P.op("pe", lambda e, kc=kc: e.matmul(out=PS(b)[0:T, c0:512], lhsT=xnT[:, kc, cols],
                                                     rhs=wt[:, kc, c0:512], start=(kc == 0), stop=(kc == 7)),
                     reads=[wkey] + [("xnT", i) for i in range(17)], writes=PSK(b), inc=(kc == 7))
            if out_dma is not None:
                s = kvrot.next()
                P.op("dve", lambda e: e.tensor_copy(out=kvf[s][0:T, :], in_=PS(b)[0:T, :]),
                     reads=PSK(b), writes=["kvf%d" % s])
                if vdst is not None:
                    P.op("act", lambda e: e.activation(out=vdst, in_=kvf[s][0:T, 256:512], func=AF.Copy),
                         reads=["kvf%d" % s], writes=[vkey])
                out_dma(kvf[s], "kvf%d" % s)
            elif vdst is not None:
                P.op("act", lambda e: e.activation(out=vdst, in_=PS(b)[0:T, 256:512], func=AF.Copy),
                     reads=PSK(b), writes=[vkey])

        def load_wkv(g):
            s = wrot.next()
            key = "wb%d" % s
            P.dma("pool", wb[s][:, :, 0:256], w_in_v[:, :, K0 + g * 256:K0 + (g + 1) * 256], key, writes=[key])
            P.dma("pool", wb[s][:, :, 256:512], w_in_v[:, :, V0 + g * 256:V0 + (g + 1) * 256], key, writes=[key])
            return wb[s], key

        import os
        for g in range(3 if not os.environ.get("KDBG_NOCC") else 0):
            lb = 128 * DIL[g]
            for b0 in range(0, 16, 4):
                P.dma("act", skv_o[g][b0:b0 + 4, 0:lb - 4, :], ck[g][b0:b0 + 4, 4:lb, :], "cc%d_%d" % (g, b0))

        SSD_ON = stage >= 3
        st1a = ExitStack()
        cur[0] = st1a
        if SSD_ON:
            GT = 512
            wz = sbt("wz", [128, 8, 768], BF16)
            wdt = sbt("wdt", [128, 8, 12], BF16)
            P.dma("pool", wz[:, :, :], w_in_v[:, :, Z0:Z0 + 768], "wz", writes=["wz"])
            P.dma("pool", wdt[:, :, :], w_in_v[:, :, DT0:DT0 + 12], "wdt", writes=["wdt"])
            stg = sbt("stg", [128, 14, 3 + GT], BF16)
            carry = sbt("carry", [128, 14, 3], BF16)
            xc = sbt("xc", [128, 14, GT], BF16)
            cvt = [sbt("cvt%d" % i, [128, GT], F32) for i in range(2)]
            cvrot = Rot(range(2))
            hT = sbt("hT", [128, 768], F32)
            hTb = sbt("hTb", [128, 768], BF16)
            dv = sbt("dv", [128, 128], F32)
            cdt = sbt("cdt", [128, 12], F32)
            xw = sbt("xw", [128, 768], BF16)
            xtok = sbt("xtok", [128, 768], BF16)
            Btok = sbt("Btok", [128, 512], BF16)
            sz = sbt("sz", [128, 768], F32)
            t1 = sbt("t1", [128, 768], F32)
            yv = sbt("yv", [128, 768], F32)
            CBm = sbt("CBm", [128, 4, 128], F32)
            Rt2 = [sbt("Rt%d" % i, [128, 3, 128], F32) for i in range(2)]
            Lh2 = [sbt("Lh%d" % i, [128, 3, 128], F32) for i in range(2)]
            Gh2 = [sbt("Gh%d" % i, [128, 3, 128], BF16) for i in range(2)]
            ynb = sbt("ynb", [128, 768], BF16)
            ych = sbt("ych", [128, 6, 128], BF16)
            sst = sbt("sst", [128, 6, 128], F32)
            gcol = lambda g: (g // 2) * 512 + (g % 2) * 192
            hcol = lambda h: gcol(h // 3) + (h % 3) * 64

            def v2(ap2d):
                return ap2d.rearrange("p (a c) -> p a c", a=2)

            def pv2(b, T):
                return PS(b, 2)[0:T, :].rearrange("p (a c) -> p a c", a=2)[:, :, 0:384]

            def ssd_chunk(T, xcf, xck, xnf, xnk, full, out_cols):
                D = lambda a, n=12: dv[0:T, a:a + n]
                if full:
                    for (c0, c1, b) in ((0, 512, 0), (512, 768, 1)):
                        for kc in range(8):
                            P.op("pe", lambda e, kc=kc, c0=c0, c1=c1, b=b: e.matmul(
                                out=PS(b)[0:T, 0:c1 - c0], lhsT=xnf(kc), rhs=wz[:, kc, c0:c1],
                                start=(kc == 0), stop=(kc == 7)), reads=["wz"] + xnk, writes=PSK(b), inc=(kc == 7))
                    P.op("act", lambda e: e.activation(out=sz[0:T, 0:512], in_=PS(0)[0:T, :], func=AF.Silu),
                         reads=PSK(0), writes=["sz"])
                    P.op("act", lambda e: e.activation(out=sz[0:T, 512:768], in_=PS(1)[0:T, 0:256], func=AF.Silu),
                         reads=PSK(1) + ["sz"], writes=["sz"])
                for kc in range(8):
                    P.op("pe", lambda e, kc=kc: e.matmul(out=PS(2)[0:T, 0:12], lhsT=xnf(kc), rhs=wdt[:, kc, :],
                                                         start=(kc == 0), stop=(kc == 7)),
                         reads=["wdt"] + xnk, writes=PSK(2), inc=(kc == 7))
                dvo = lambda fn, rd=(): P.op("dve", fn, reads=["dv"] + list(rd), writes=["dv"])
                aco = lambda fn, rd=(): P.op("act", fn, reads=["dv"] + list(rd), writes=["dv"])
                dvo(lambda e: e.tensor_tensor(out=D(0), in0=PS(2)[0:T, 0:12], in1=dtb[0:T, :], op=ALU.add),
                    PSK(2) + ["dtb"])
                dvo(lambda e: e.tensor_scalar(out=D(12), in0=D(0), scalar1=-1.0, scalar2=None, op0=ALU.mult))
                dvo(lambda e: e.tensor_tensor(out=D(12), in0=D(0), in1=D(12), op=ALU.max))
                aco(lambda e: e.activation(out=D(24), in_=D(12), func=AF.Exp, scale=-1.0))
                dvo(lambda e: e.tensor_scalar(out=D(24), in0=D(24), scalar1=1.0, scalar2=None, op0=ALU.add))
                aco(lambda e: e.activation(out=D(36), in_=D(24), func=AF.Ln))
                dvo(lambda e: e.scalar_tensor_tensor(out=D(48), in0=D(0), scalar=0.0, in1=D(36), op0=ALU.max,
                                                     op1=ALU.add))
                dvo(lambda e: e.tensor_tensor(out=D(60), in0=D(48), in1=abc[0:T, :], op=ALU.mult), ["abc"])
                P.op("pe", lambda e: e.matmul(out=PS(2)[0:T, 16:28], lhsT=triu[0:T, 0:T], rhs=D(60), start=True,
                                              stop=True), reads=["dv", "cm"], writes=PSK(2), inc=False)
                P.op("pe", lambda e: e.matmul(out=PS(2)[:, 32:44], lhsT=onesm[0:T, :], rhs=D(60), start=True,
                                              stop=True), reads=["dv", "cm"], writes=PSK(2))
                dvo(lambda e: e.tensor_copy(out=D(72), in_=PS(2)[0:T, 16:28]), PSK(2))
                P.op("act", lambda e: e.activation(out=cdt[:, :], in_=PS(2)[:, 32:44], func=AF.Exp),
                     reads=PSK(2), writes=["cdt"])
                dvo(lambda e: e.tensor_tensor(out=D(84), in0=PS(2)[0:T, 32:44], in1=D(72), op=ALU.subtract), PSK(2))
                aco(lambda e: e.activation(out=D(96), in_=D(84), func=AF.Exp))
                dvo(lambda e: e.tensor_tensor(out=D(96), in0=D(96), in1=D(48), op=ALU.mult))
                if full:
                    aco(lambda e: e.activation(out=D(108), in_=D(72), func=AF.Exp))
                ptx = PS(3).bitcast(BF16)
                ptb = PS(4).bitcast(BF16)
                for j in range(6):
                    P.op("pe", lambda e, j=j: e.transpose(out=ptx[0:T, j * 128:(j + 1) * 128], in_=xcf(j),
                                                          identity=identb[:, :]),
                         reads=xck + ["identb"], writes=PSK(3), inc=(j == 5))
                for g in range(4):
                    P.op("pe", lambda e, g=g: e.transpose(out=ptb[0:T, g * 128:(g + 1) * 128], in_=xcf(6 + g),
                                                          identity=identb[:, :]),
                         reads=xck + ["identb"], writes=PSK(4), inc=(g == 3))
                P.op("dve", lambda e: e.tensor_tensor(
                    out=xw[0:T, :].rearrange("p (h d) -> p h d", h=12),
                    in0=ptx[0:T, 0:768].rearrange("p (h d) -> p h d", h=12),
                    in1=D(96).unsqueeze(2).broadcast_to([T, 12, 64]), op=ALU.mult),
                    reads=PSK(3) + ["dv"], writes=["xw"])
                if full:
                    P.op("act", lambda e: e.activation(out=xtok[0:T, :], in_=ptx[0:T, 0:768], func=AF.Copy),
                         reads=PSK(3), writes=["xtok"])
                P.op("act", lambda e: e.activation(out=Btok[0:T, :], in_=ptb[0:T, 0:512], func=AF.Copy),
                     reads=PSK(4), writes=["Btok"])
                for g in range(4):
                    P.op("pe", lambda e, g=g: e.matmul(out=PS(6, 2)[:, gcol(g):gcol(g) + 192],
                                                       lhsT=Btok[0:T, g * 128:(g + 1) * 128],
                                                       rhs=xw[0:T, g * 192:(g + 1) * 192], start=True, stop=True),
                         reads=["Btok", "xw"], writes=PSK(6, 2), inc=(g == 3))
                if full:
                    for g in range(4):
                        P.op("pe", lambda e, g=g: e.matmul(out=PS(0, 2)[0:T, gcol(g):gcol(g) + 192],
                                                           lhsT=xcf(10 + g), rhs=hTb[:, g * 192:(g + 1) * 192],
                                                           start=True, stop=True),
                             reads=xck + ["hTb"], writes=PSK(0, 2), inc=(g == 3))
                    P.op("dve", lambda e: e.tensor_tensor(
                        out=v2(t1[0:T, :]).rearrange("p a (h d) -> p a h d", h=6),
                        in0=pv2(0, T).rearrange("p a (h d) -> p a h d", h=6),
                        in1=D(108).rearrange("p (a h) -> p a h", a=2).unsqueeze(3).broadcast_to([T, 2, 6, 64]),
                        op=ALU.mult), reads=PSK(0, 2) + ["dv"], writes=["t1"])
                    for g in range(4):
                        P.op("pe", lambda e, g=g: e.matmul(out=PS(4)[0:T, g * 128:g * 128 + T], lhsT=xcf(6 + g),
                                                           rhs=xcf(10 + g), start=True, stop=True),
                             reads=xck, writes=PSK(4), inc=(g == 3))
                    P.op("dve", lambda e: e.tensor_tensor(
                        out=CBm[0:T, :, 0:T], in0=PS(4)[0:T, :].rearrange("p (g l) -> p g l", g=4)[:, :, 0:T],
                        in1=triu[0:T, 0:T].unsqueeze(1).broadcast_to([T, 4, T]), op=ALU.mult),
                        reads=PSK(4) + ["cm"], writes=["CBm"])
                    for g in range(4):
                        q2 = g % 2
                        Rt, Lh, Gh = Rt2[q2], Lh2[q2], Gh2[q2]
                        rk_, lk_, gk_ = "Rt%d" % q2, "Lh%d" % q2, "Gh%d" % q2
                        sb_ = 5 if q2 == 0 else 3
                        P.op("dve", lambda e, g=g, Rt=Rt: e.tensor_tensor(
                            out=Rt[0:T, :, 0:T], in0=triu[0:T, 0:T].unsqueeze(1).broadcast_to([T, 3, T]),
                            in1=dv[0:T, 60 + 3 * g:63 + 3 * g].unsqueeze(2).broadcast_to([T, 3, T]), op=ALU.mult),
                            reads=["dv", "cm"], writes=[rk_])
                        P.op("pe", lambda e, Rt=Rt, sb_=sb_: e.matmul(
                            out=PS(sb_)[0:T, 0:384].rearrange("p (r l) -> p r l", r=3)[:, :, 0:T], lhsT=slm[0:T, 0:T],
                            rhs=Rt[0:T, :, 0:T], start=True, stop=True), reads=[rk_, "cm"], writes=PSK(sb_))
                        P.op("act", lambda e, Lh=Lh, sb_=sb_: e.activation(
                            out=Lh[0:T, :, 0:T], in_=PS(sb_)[0:T, 0:384].rearrange("p (r l) -> p r l", r=3)[:, :, 0:T],
                            func=AF.Exp), reads=PSK(sb_), writes=[lk_])
                        P.op("dve", lambda e, g=g, Lh=Lh: e.tensor_tensor(
                            out=Lh[0:T, :, 0:T], in0=Lh[0:T, :, 0:T],
                            in1=dv[0:T, 48 + 3 * g:51 + 3 * g].unsqueeze(2).broadcast_to([T, 3, T]), op=ALU.mult),
                            reads=[lk_, "dv"], writes=[lk_])
                        P.op("dve", lambda e, g=g, Lh=Lh, Gh=Gh: e.tensor_tensor(
                            out=Gh[0:T, :, 0:T], in0=Lh[0:T, :, 0:T],
                            in1=CBm[0:T, g, 0:T].unsqueeze(1).broadcast_to([T, 3, T]), op=ALU.mult),
                            reads=[lk_, "CBm"], writes=[gk_])
                        for r in range(3):
                            h = 3 * g + r
                            P.op("pe", lambda e, r=r, h=h, Gh=Gh: e.matmul(
                                out=PS(0, 2)[0:T, hcol(h):hcol(h) + 64], lhsT=Gh[0:T, r, 0:T],
                                rhs=xtok[0:T, h * 64:(h + 1) * 64], start=True, stop=True),
                                reads=[gk_, "xtok"], writes=PSK(0, 2), inc=(r == 2))
                    P.op("dve", lambda e: e.tensor_tensor(out=v2(yv[0:T, :]), in0=pv2(0, T), in1=v2(t1[0:T, :]),
                                                          op=ALU.add), reads=PSK(0, 2) + ["t1"], writes=["yv"])
                    P.op("dve", lambda e: e.tensor_tensor(
                        out=t1[0:T, :].rearrange("p (h d) -> p h d", h=12),
                        in0=xtok[0:T, :].rearrange("p (h d) -> p h d", h=12),
                        in1=dsk[0:T, :].unsqueeze(2).broadcast_to([T, 12, 64]), op=ALU.mult),
                        reads=["xtok", "dsk", "yv"], writes=["t1"])
                    P.op("dve", lambda e: e.tensor_tensor(out=yv[0:T, :], in0=yv[0:T, :], in1=t1[0:T, :], op=ALU.add),
                         reads=["yv", "t1"], writes=["yv"])
                    P.op("dve", lambda e: e.tensor_tensor(out=yv[0:T, :], in0=yv[0:T, :], in1=sz[0:T, :], op=ALU.mult),
                         reads=["yv", "sz"], writes=["yv"])
                    P.op("act", lambda e: e.activation(out=t1[0:T, :], in_=yv[0:T, :], func=AF.Square,
                                                       scale=float(768 ** -0.5), accum_out=D(120, 1)),
                         reads=["yv", "dv"], writes=["t1", "dv"])
                    dvo(lambda e: e.tensor_scalar(out=D(121, 1), in0=D(120, 1), scalar1=EPS, scalar2=None, op0=ALU.add))
                    aco(lambda e: e.activation(out=D(122, 1), in_=D(121, 1), func=AF.Ln))
                    aco(lambda e: e.activation(out=D(123, 1), in_=D(122, 1), func=AF.Exp, scale=-0.5))
                    P.op("dve", lambda e: e.scalar_tensor_tensor(out=ynb[0:T, :], in0=yv[0:T, :], scalar=D(123, 1),
                                                                 in1=gssd[0:T, :], op0=ALU.mult, op1=ALU.mult),
                         reads=["yv", "dv", "gssd"], writes=["ynb"])
                    for j in range(6):
                        P.op("pe", lambda e, j=j: e.transpose(out=ptx[:, j * 128:j * 128 + T],
                                                              in_=ynb[0:T, j * 128:(j + 1) * 128],
                                                              identity=identb[0:T, 0:T]),
                             reads=["ynb", "identb"], writes=PSK(3), inc=(j == 5))
                    P.op("act", lambda e: e.activation(
                        out=ych[:, :, 0:T], in_=ptx[:, 0:768].rearrange("p (j t) -> p j t", j=6)[:, :, 0:T],
                        func=AF.Copy), reads=PSK(3), writes=["ych"])
                    P.dma("sp", mixd[6:12, :, out_cols:out_cols + T].rearrange("j p t -> p j t"), ych[:, :, 0:T],
                          "ycho", reads=["ych"])
                P.op("dve", lambda e: e.tensor_tensor(
                    out=hT[:, :].rearrange("p (h d) -> p h d", h=12), in0=hT[:, :].rearrange("p (h d) -> p h d", h=12),
                    in1=cdt[:, :].unsqueeze(2).broadcast_to([128, 12, 64]), op=ALU.mult),
                    reads=["hT", "cdt"], writes=["hT"])
                P.op("dve", lambda e: e.tensor_tensor(out=v2(hT[:, :]), in0=pv2(6, 128), in1=v2(hT[:, :]), op=ALU.add),
                     reads=PSK(6, 2) + ["hT"], writes=["hT"])
                P.op("act", lambda e: e.activation(out=hTb[:, :], in_=hT[:, :], func=AF.Copy), reads=["hT"],
                     writes=["hTb"])

            def conv_silu(j, ins, outc, cview):
                cs = cvrot.next()
                cv = cview(cvt[cs])
                ck_ = "cvt%d" % cs
                P.op("dve", lambda e: e.tensor_scalar(out=cv, in0=ins[0], scalar1=convw[:, j, 0:1],
                                                      scalar2=convb[:, j:j + 1], op0=ALU.mult, op1=ALU.add),
                     reads=["stg", "convw", "convb"], writes=[ck_])
                for w in range(1, 4):
                    P.op("dve", lambda e, w=w: e.scalar_tensor_tensor(out=cv, in0=ins[w], scalar=convw[:, j, w:w + 1],
                                                                      in1=cv, op0=ALU.mult, op1=ALU.add),
                         reads=["stg", "convw", ck_], writes=[ck_])
                P.op("act", lambda e: e.activation(out=outc, in_=cv, func=AF.Silu), reads=[ck_], writes=["xc"])

            def xbc_proj(tok0, ntok, evac):
                for cg in range(4):
                    ncol = 512 if cg < 3 else 256
                    wt, wkey = load_w(w_in_v, X0 + cg * 512, ncol)
                    for jj in range(ncol // 128):
                        j = cg * 4 + jj
                        b = prot.next()
                        for kc in range(8):
                            P.op("pe", lambda e, kc=kc, b=b, jj=jj, wt=wt: e.matmul(
                                out=PS(b)[:, 0:ntok], lhsT=wt[:, kc, jj * 128:(jj + 1) * 128],
                                rhs=xnT[:, kc, tok0:tok0 + ntok], start=(kc == 0), stop=(kc == 7)),
                                reads=[wkey] + xkeys(tok0, ntok), writes=PSK(b), inc=(kc == 7))
                        evac(j, b)

            def ssd_group(tok0, ntok, full):
                P.op("dve", lambda e: e.tensor_copy(out=stg[:, :, 0:3], in_=carry[:, :, :]), reads=["carry", "xc"],
                     writes=["stg"])
                xbc_proj(tok0, ntok, lambda j, b: P.op("act", lambda e: e.activation(
                    out=stg[:, j, 3:3 + ntok], in_=PS(b)[:, 0:ntok], func=AF.Copy), reads=PSK(b), writes=["stg"]))
                P.op("dve", lambda e: e.tensor_copy(out=carry[:, :, :], in_=stg[:, :, ntok:ntok + 3]), reads=["stg"],
                     writes=["carry"])
                for j in range(14):
                    conv_silu(j, [stg[:, j, w:w + ntok] for w in range(4)], xc[:, j, 0:ntok],
                              lambda t: t[:, 0:ntok])
                for ci in range(ntok // 128):
                    ssd_chunk(128, lambda j, ci=ci: xc[:, j, ci * 128:(ci + 1) * 128], ["xc"],
                              lambda kc, ci=ci: xnT[:, kc, tok0 + ci * 128:tok0 + (ci + 1) * 128],
                              xkeys(tok0 + ci * 128, 128), full, tok0 + ci * 128)

            def state_out(dst):
                for a in range(6):
                    P.op("pe", lambda e, a=a: e.transpose(out=PS(0, 2)[:, a * 128:(a + 1) * 128],
                                                          in_=hT[:, a * 128:(a + 1) * 128], identity=identf),
                         reads=["hT", "cm"], writes=PSK(0, 2), inc=(a == 5))
                P.op("dve", lambda e: e.tensor_copy(out=sst[:, :, :].rearrange("p a n -> p (a n)"),
                                                    in_=PS(0, 2)[:, 0:768]), reads=PSK(0, 2), writes=["sst"])
                P.dma("sp", dst.rearrange("(a p) n -> p a n", p=128), sst[:, :, :], "ssto", reads=["sst"])
        LV = int(os.environ.get("KDBG_LV", "9"))
        if LV >= 1:
            norm_tokens(xp, TO, 0)
        if LV >= 2:
            for g in range(3):
                d = DIL[g]
                wt, wkey = load_wkv(g)
                nb = d
                for r in range(d):
                    blk = (16 // d - 1) * d + r if d < 16 else r
                    kv_block(g, wt, wkey, block_tokens(g, blk), 128, False, Vprev[g][:, r, :],
                             ("Vprev", g, r), None)
        if stage >= 4:
            for g in range(3):
                npv = 128 * DIL[g]
                wt, wkey = load_w(w_in_v, K0 + g * 256, 256)
                for c in range(2):
                    for a0 in range(0, npv, 512):
                        an = min(512, npv - a0)
                        b = prot.next()
                        for kc in range(8):
                            P.op("pe", lambda e, kc=kc, b=b, c=c, a0=a0, an=an, wt=wt, npv=npv: e.matmul(
                                out=PS(b)[:, 0:an], lhsT=wt[:, kc, c * 128:(c + 1) * 128],
                                rhs=xnT[:, kc, TO - npv + a0:TO - npv + a0 + an], start=(kc == 0), stop=(kc == 7)),
                                reads=[wkey] + [("xnT", i) for i in range(17)], writes=PSK(b), inc=(kc == 7))
                        P.op("act", lambda e, b=b, c=c, a0=a0, an=an, g=g: e.activation(
                            out=kTp[g][:, c, a0:a0 + an], in_=PS(b)[:, 0:an], func=AF.Copy),
                            reads=PSK(b), writes=["kTp"])
        if SSD_ON:
            P.op("dve", lambda e: e.memset(hT[:, :], 0.0), writes=["hT"])
            P.op("dve", lambda e: e.memset(hTb[:, :], 0.0), writes=["hTb"])
            P.op("dve", lambda e: e.memset(carry[:, :, :], 0.0), writes=["carry"])
            for gi in range(TO // GT):
                ssd_group(gi * GT, GT, False)
            P.op("dve", lambda e: e.tensor_scalar(out=hT[:, :], in0=hT[:, :], scalar1=hp[:, 0:1], scalar2=None,
                                                  op0=ALU.mult), reads=["hT", "hp"], writes=["hT"])
            P.op("act", lambda e: e.activation(out=hTb[:, :], in_=hT[:, :], func=AF.Copy), reads=["hT"], writes=["hTb"])
            P.op("dve", lambda e: e.tensor_scalar(out=carry[:, :, :].rearrange("p a b -> p (a b)"),
                                                  in0=carry[:, :, :].rearrange("p a b -> p (a b)"),
                                                  scalar1=hp[:, 0:1], scalar2=None, op0=ALU.mult),
                 reads=["carry", "hp"], writes=["carry"])
        SUB = int(os.environ.get("KDBG_SUB", "9"))
        if LV >= 3:
            norm_tokens(xo, TO, 0)
            if SUB >= 2:
                norm_tokens(xs, TS, TO)
        for g in range(3 if (LV >= 3 and SUB >= 1) else 0):
            d = DIL[g]
            lb = 128 * d
            wt, wkey = load_wkv(g)
            for blk in range(16):
                u, r = blk // d, blk % d
                is_out = (u == 16 // d - 1)

                def od(t, tk, g=g, r=r, d=d):
                    if os.environ.get("KDBG_CONT"):
                        P.dma("sp", pkv_o[g][r * 128:(r + 1) * 128, :], t[:, :], "pk" + tk, reads=[tk])
                    else:
                        P.dma("sp", pkv_o[g][r:128 * d:d, :], t[:, :], "pk" + tk, reads=[tk])
                kv_block(g, wt, wkey, block_tokens(g, blk), 128, is_out, Vtok[g][:, blk, :],
                         ("Vtok", g, blk), od if (is_out and LV >= 4) else None)
            def ods(t, tk, g=g, lb=lb):
                for s_ in range(4):
                    P.dma("sp", skv_o[g][:, lb - 4 + s_, :], t[s_:64:4, :], "sk" + tk, reads=[tk])
            if SUB >= 3:
                kv_block(g, wt, wkey, slice(TO, TT), 64, True, Vnew[g][0:64, :], ("Vnew", g), ods if LV >= 5 else None)

        for cg in range(4):
            ncol = 512 if cg < 3 else 256
            wt, wkey = load_w(w_in_v, X0 + cg * 512, ncol)
            for (which, cols, T) in (("p", slice(TO - 3, TO), 3), ("s", slice(TO, TT), 64)):
                b = prot.next()
                for kc in range(8):
                    P.op("pe", lambda e, kc=kc, b=b, cols=cols, T=T, wt=wt, ncol=ncol: e.matmul(
                        out=PS(b)[0:T, 0:ncol], lhsT=xnT[:, kc, cols], rhs=wt[:, kc, 0:ncol],
                        start=(kc == 0), stop=(kc == 7)),
                        reads=[wkey, ("xnT", 15), ("xnT", 16)], writes=PSK(b), inc=(kc == 7))
                ks = kvrot.next()
                P.op("dve", lambda e, b=b, ks=ks, T=T, ncol=ncol: e.tensor_copy(
                    out=kvf[ks][0:T, 0:ncol], in_=PS(b)[0:T, 0:ncol]), reads=PSK(b), writes=["kvf%d" % ks])
                if which == "p":
                    P.dma("sp", pconv_o[:, cg * 512:cg * 512 + ncol], kvf[ks][0:3, 0:ncol], "cvkvf%d" % ks, reads=["kvf%d" % ks])
                else:
                    for s_ in range(1, 4):
                        P.dma("sp", oconv_o[:, s_ - 1, cg * 512:cg * 512 + ncol], kvf[ks][s_:64:4, 0:ncol], "cvkvf%d" % ks,
                              reads=["kvf%d" % ks])
        if SSD_ON:
            for gi in range(TO // GT):
                ssd_group(gi * GT, GT, True)
            state_out(pssm_o)
            stg_s = stg[:, :, 0:112].rearrange("p j (b w) -> p j b w", w=7)
            for j in range(14):
                if j % 6 == 0:
                    ncs = min(768, 1792 - j * 128)
                    P.dma("sp", sz[0:48, 0:ncs], sconv_in[:, j * 128:j * 128 + ncs], "sct", writes=["sz"])
                jl = j % 6
                b = prot.next()
                P.op("pe", lambda e, jl=jl, b=b: e.transpose(out=PS(b)[:, 0:48], in_=sz[0:48, jl * 128:(jl + 1) * 128],
                                                            identity=identf[0:48, 0:48]),
                     reads=["sz", "cm"], writes=PSK(b))
                P.op("dve", lambda e, j=j, b=b: e.tensor_copy(
                    out=stg_s[:, j, :, 0:3], in_=PS(b)[:, 0:48].rearrange("p (b r) -> p b r", r=3)),
                    reads=PSK(b) + ["xc"], writes=["stg"])
            xbc_proj(TO, 64, lambda j, b: P.op("act", lambda e: e.activation(
                out=stg_s[:, j, :, 3:7], in_=PS(b)[:, 0:64].rearrange("p (b s) -> p b s", s=4), func=AF.Copy),
                reads=PSK(b), writes=["stg"]))
            for j in range(14):
                conv_silu(j, [stg_s[:, j, :, w:w + 4] for w in range(4)],
                          xc[:, j, 0:64].rearrange("p (b s) -> p b s", s=4),
                          lambda t: t[:, 0:64].rearrange("p (b s) -> p b s", s=4))
            for b in range(16):
                P.dma("sp", sst[:, :, :], sssm_in[b].rearrange("(a p) n -> p a n", p=128), "ssti", writes=["sst"])
                for a in range(6):
                    P.op("pe", lambda e, a=a: e.transpose(out=PS(0, 2)[:, a * 128:(a + 1) * 128], in_=sst[:, a, :],
                                                          identity=identf), reads=["sst", "cm"], writes=PSK(0, 2),
                         inc=(a == 5))
                P.op("dve", lambda e: e.tensor_copy(out=hT[:, :], in_=PS(0, 2)[:, 0:768]), reads=PSK(0, 2),
                     writes=["hT"])
                P.op("act", lambda e: e.activation(out=hTb[:, :], in_=hT[:, :], func=AF.Copy), reads=["hT"],
                     writes=["hTb"])
                ssd_chunk(4, lambda j, b=b: xc[:, j, 4 * b:4 * b + 4], ["xc"],
                          lambda kc, b=b: xnT[:, kc, TO + 4 * b:TO + 4 * b + 4], [("xnT", 16)], True, TO + 4 * b)
                state_out(ossm_o[b])
        if os.environ.get("KDBG_NOATT"):
            P.op("dve", lambda e: e.memset(junk[:, :], 0.0), writes=["junk"])
            for j in range(6):
                for (a0, a1) in ((0, 1024), (1024, 2048), (2048, TT)):
                    P.dma("sp", mixd[j, :, a0:a1], junk[:, 0:a1 - a0], "zbo", reads=["junk"])
        NOSSD = stage < 3
        if NOSSD:
            zt = sbt("zt", [128, 768], F32)
            P.op("dve", lambda e: e.memset(zt[:], 0.0), writes=["zt"])
            for i in range(6):
                P.dma("sp", pssm_o[i * 128:(i + 1) * 128, :], zt[:, 0:128], "zo", reads=["zt"])
            for b in range(16):
                P.dma("sp", ossm_o[b].rearrange("(a p) n -> p a n", p=128),
                      zt[:, 0:768].rearrange("p (a n) -> p a n", a=6), "zo", reads=["zt"])
        P.barrier()
        P.replay(nc, st1a, "a", st)
        st1a.close()
        if stage >= 4 and not os.environ.get("KDBG_NOATT"):
            P = Prog()
            st1b = ExitStack()
            cur[0] = st1b
            qT = sbt("qTz", [128, 2, 2, TT], BF16)
            kT = sbt("kT", [128, 2, TT], BF16)
            UTg = sbt("UTg", [128, 2, TT], BF16)
            Sm = [sbt("Sm%d" % i, [128, 2, 2, 256], F32) for i in range(3)]
            Pb = [sbt("Pb%d" % i, [128, 2, 2, 256], BF16) for i in range(3)]
            PT = [sbt("PT%d" % i, [128, 8, 128], BF16) for i in range(3)]
            mxt = [sbt("mxt%d" % i, [128, 4], F32) for i in range(3)]
            ngt = [sbt("ngt%d" % i, [128, 4], F32) for i in range(3)]
            dnt = [sbt("dnt%d" % i, [128, 4], F32) for i in range(3)]
            stt = [sbt("stt%d" % i, [128, 4, 2], F32) for i in range(3)]
            ckt = [sbt("ckt%d" % i, [128, 4, 512], BF16) for i in range(2)]
            kTc = [sbt("kTc%d" % i, [128, 4, 2, 128], BF16) for i in range(2)]
            Vnb = [sbt("Vnb%d" % i, [128, 256], BF16) for i in range(2)]
            for hh_ in range(2):
                for c_ in range(2):
                    P.op("pool", lambda e, hh_=hh_, c_=c_: e.memset(qT[:, hh_, c_, :], 0.0), writes=["qT"])
            arot = Rot(range(3))
            crot = Rot(range(2))
            pjrot = Rot([0, 1, 2, 3])

            def attn_block(nq, Wo, q_ap, kp_ap, ko_ap, vp_ap, vo_ap, mask_ap, ut_ap, stat_dst, rkeys):
                W = 128 + Wo
                sl = arot.next()
                bS = (2 * sl, 2 * sl + 1)
                bT, bO = 6, 7
                smk, pbk, ptk, stk = "Sm%d" % sl, "Pb%d" % sl, "PT%d" % sl, "stt%d" % sl
                for c in range(2):
                    for hh in range(2):
                        P.op("pe", lambda e, c=c, hh=hh: e.matmul(
                            out=PS(bS[c])[0:nq, hh * 256:hh * 256 + 128], lhsT=q_ap(c, hh), rhs=kp_ap(c, hh),
                            start=True, stop=True), reads=rkeys, writes=PSK(bS[c]), inc=False)
                        P.op("pe", lambda e, c=c, hh=hh: e.matmul(
                            out=PS(bS[c])[0:nq, hh * 256 + 128:hh * 256 + W], lhsT=q_ap(c, hh), rhs=ko_ap(c, hh),
                            start=True, stop=True), reads=rkeys, writes=PSK(bS[c]), inc=(hh == 1))
                yield
                for c in range(2):
                    Sv = PS(bS[c])[0:nq, :].rearrange("p (h w) -> p h w", h=2)[:, :, 0:W]
                    P.op("dve", lambda e, c=c, Sv=Sv: e.scalar_tensor_tensor(
                        out=Sm[sl][0:nq, c, :, 0:W], in0=Sv, scalar=SCALE,
                        in1=mask_ap.unsqueeze(1).broadcast_to([nq, 2, W]), op0=ALU.mult, op1=ALU.add),
                        reads=PSK(bS[c]) + ["maskA", "maskF", "maskS"], writes=[(smk, c)])
                    P.op("dve", lambda e, c=c: e.tensor_reduce(out=mxt[sl][0:nq, 2 * c:2 * c + 2],
                                                               in_=Sm[sl][0:nq, c, :, 0:W], axis=AX.X, op=ALU.max),
                         reads=[(smk, c)], writes=[(stk, "mx", c)])
                    P.op("dve", lambda e, c=c: e.tensor_scalar(out=ngt[sl][0:nq, 2 * c:2 * c + 2],
                                                               in0=mxt[sl][0:nq, 2 * c:2 * c + 2], scalar1=-1.0,
                                                               scalar2=None, op0=ALU.mult),
                         reads=[(stk, "mx", c)], writes=[(stk, "ng", c)])
                    for hh in range(2):
                        P.op("act", lambda e, c=c, hh=hh: e.activation(
                            out=Pb[sl][0:nq, c, hh, 0:W], in_=Sm[sl][0:nq, c, hh, 0:W], func=AF.Exp,
                            bias=ngt[sl][0:nq, 2 * c + hh:2 * c + hh + 1], scale=1.0,
                            accum_out=dnt[sl][0:nq, 2 * c + hh:2 * c + hh + 1]),
                            reads=[(smk, c), (stk, "ng", c)], writes=[(pbk, c, hh), (stk, "dn", c, hh)])
                yield
                ptv = PS(bT).bitcast(BF16).rearrange("p (a q) -> p a q", a=8)
                for c in range(2):
                    for hh in range(2):
                        a = (c * 2 + hh) * 2
                        P.op("pe", lambda e, c=c, hh=hh, a=a: e.transpose(
                            out=ptv[:, a, 0:nq], in_=Pb[sl][0:nq, c, hh, 0:128], identity=identb[0:nq, 0:nq]),
                            reads=[(pbk, c, hh), "identb"], writes=PSK(bT), inc=False)
                        P.op("pe", lambda e, c=c, hh=hh, a=a: e.transpose(
                            out=ptv[0:Wo, a + 1, 0:nq], in_=Pb[sl][0:nq, c, hh, 128:W], identity=identb[0:nq, 0:nq]),
                            reads=[(pbk, c, hh), "identb"], writes=PSK(bT), inc=(c == 1 and hh == 1))
                yield
                ptv4 = ptv.rearrange("p (x j) q -> p x j q", j=2)
                pt4 = PT[sl][:, :, :].rearrange("p (x j) q -> p x j q", j=2)
                P.op("dve", lambda e: e.tensor_copy(out=pt4[:, :, 0, 0:nq], in_=ptv4[:, :, 0, 0:nq]),
                     reads=PSK(bT), writes=[ptk])
                P.op("dve", lambda e: e.tensor_copy(out=pt4[0:Wo, :, 1, 0:nq], in_=ptv4[0:Wo, :, 1, 0:nq]),
                     reads=PSK(bT) + [ptk], writes=[ptk])
                yield
                pov = PS(bO)[:, :].rearrange("p (a q) -> p a q", a=4)
                for c in range(2):
                    for hh in range(2):
                        a = c * 2 + hh
                        P.op("pe", lambda e, c=c, a=a: e.matmul(out=pov[:, a, 0:nq], lhsT=vp_ap(c),
                                                                rhs=PT[sl][:, 2 * a, 0:nq], start=True, stop=False),
                             reads=rkeys + [ptk], writes=PSK(bO), inc=False)
                        P.op("pe", lambda e, c=c, a=a: e.matmul(out=pov[:, a, 0:nq], lhsT=vo_ap(c),
                                                                rhs=PT[sl][0:Wo, 2 * a + 1, 0:nq], start=False,
                                                                stop=True),
                             reads=rkeys + [ptk], writes=PSK(bO), inc=(a == 3))
                yield
                for c in range(2):
                    for hh in range(2):
                        P.op("act", lambda e, c=c, hh=hh: e.activation(
                            out=ut_ap(c, hh), in_=pov[hh * 64:(hh + 1) * 64, c * 2 + hh, 0:nq], func=AF.Copy),
                            reads=PSK(bO), writes=[("UTg", c, hh)])
                P.op("dve", lambda e: e.tensor_copy(out=stt[sl][0:nq, :, 0], in_=mxt[sl][0:nq, :]),
                     reads=[(stk, "mx", 0), (stk, "mx", 1)], writes=[stk])
                P.op("dve", lambda e: e.tensor_copy(out=stt[sl][0:nq, :, 1], in_=dnt[sl][0:nq, :]),
                     reads=[(stk, "dn", c, hh) for c in range(2) for hh in range(2)] + [stk], writes=[stk])
                P.dma("sp", stat_dst, stt[sl][0:nq, :, :], "stato%d" % sl, reads=[stk])
                yield

            utk = [("UTg", c, hh) for c in range(2) for hh in range(2)]
            KATT = int(os.environ.get("KATT", "9"))
            for g in range(3 if KATT >= 2 else 1):
                d = DIL[g]
                swq = wrot.next()
                wqk = "wb%d" % swq
                P.dma("pool", wb[swq][:, :, 0:256], w_in_v[:, :, Q0 + g * 256:Q0 + (g + 1) * 256], wqk, writes=[wqk])
                P.dma("pool", wb[swq][:, :, 256:512], w_in_v[:, :, K0 + g * 256:K0 + (g + 1) * 256], wqk, writes=[wqk])
                for (dst, dkey, co) in ((qT, "qT", 0), (kT, "kT", 256)):
                    for c in range(2):
                        for (a0, an) in ((0, 512), (512, 512), (1024, 512), (1536, 512), (2048, 64)):
                            b = pjrot.next()
                            for kc in range(8):
                                P.op("pe", lambda e, kc=kc, b=b, c=c, co=co, a0=a0, an=an, swq=swq: e.matmul(
                                    out=PS(b)[:, 0:an], lhsT=wb[swq][:, kc, co + c * 128:co + (c + 1) * 128],
                                    rhs=xnT[:, kc, a0:a0 + an], start=(kc == 0), stop=(kc == 7)),
                                    reads=[wqk], writes=PSK(b), inc=(kc == 7))
                            if dkey == "qT":
                                for hh in range(2):
                                    P.op("act", lambda e, b=b, c=c, hh=hh, a0=a0, an=an: e.activation(
                                        out=qT[hh * 64:(hh + 1) * 64, hh, c, a0:a0 + an],
                                        in_=PS(b)[hh * 64:(hh + 1) * 64, 0:an], func=AF.Copy),
                                        reads=PSK(b), writes=[dkey])
                            else:
                                P.op("act", lambda e, b=b, c=c, dst=dst, a0=a0, an=an: e.activation(
                                    out=dst[:, c, a0:a0 + an], in_=PS(b)[:, 0:an], func=AF.Copy),
                                    reads=PSK(b), writes=[dkey])
                swv = wrot.next()
                wvk = "wb%d" % swv
                P.dma("pool", wb[swv][:, :, 0:256], w_in_v[:, :, V0 + g * 256:V0 + (g + 1) * 256], wvk, writes=[wvk])
                rk = ["qT", "kT", "kTp"]
                gens = []
                for blk in range(16):
                    u, r = blk // d, blk % d
                    tk = block_tokens(g, blk)
                    if u == 0:
                        pi = r
                        ptk_ = slice(r, r + d * 127 + 1, d)
                        kp = (lambda c, hh, ptk_=ptk_, g=g: kTp[g][:, c, ptk_])
                        vp = (lambda c, pi=pi, g=g: Vprev[g][:, pi, c * 128:(c + 1) * 128])
                        mk = maskF[:, :]
                    else:
                        ptok = block_tokens(g, blk - d)
                        kp = (lambda c, hh, ptok=ptok: kT[:, c, ptok])
                        vp = (lambda c, g=g, pb=blk - d: Vtok[g][:, pb, c * 128:(c + 1) * 128])
                        mk = maskA[:, :]
                    gens.append(attn_block(
                        128, 128,
                        (lambda c, hh, tk=tk: qT[:, hh, c, tk]),
                        kp,
                        (lambda c, hh, tk=tk: kT[:, c, tk]),
                        vp,
                        (lambda c, g=g, blk=blk: Vtok[g][:, blk, c * 128:(c + 1) * 128]),
                        mk,
                        (lambda c, hh, tk=tk: UTg[hh * 64:(hh + 1) * 64, c, tk]),
                        stats_d[tk, 4 * g:4 * g + 4, :], rk + utk))
                run_pipelined(gens, 3)
                ns = 1 if d == 1 else 4
                for b in range(16 if KATT >= 3 else 0):
                    cs = crot.next()
                    ckk, kck, vnk = "ckt%d" % cs, "kTc%d" % cs, "Vnb%d" % cs
                    if d == 1:
                        srcv = ck[0][b].rearrange("(m s) c -> m s c", s=1)
                    elif d == 4:
                        srcv = ck[1][b].rearrange("(m s) c -> m s c", s=4)
                    else:
                        srcv = ck[2][b].rearrange("(m s) c -> m s c", s=16)[:, 0:4, :]
                    P.dma("pool", ckt[cs][:, 0:ns, :], srcv, ckk, writes=[ckk])
                    bq = pjrot.next()
                    kv8 = PS(bq).bitcast(BF16).rearrange("p (a q) -> p a q", a=8)
                    for si in range(ns):
                        for c in range(2):
                            P.op("pe", lambda e, si=si, c=c, cs=cs, kv8=kv8: e.transpose(
                                out=kv8[:, si * 2 + c, :], in_=ckt[cs][:, si, c * 128:(c + 1) * 128],
                                identity=identb[:, :]), reads=[ckk, "identb"], writes=PSK(bq),
                                inc=(si == ns - 1 and c == 1))
                    P.op("dve", lambda e, cs=cs, kv8=kv8, ns=ns: e.tensor_copy(
                        out=kTc[cs][:, 0:ns, :, :].rearrange("p s c q -> p (s c) q"), in_=kv8[:, 0:2 * ns, :]),
                        reads=PSK(bq), writes=[kck])
                    bv = pjrot.next()
                    for kc in range(8):
                        P.op("pe", lambda e, kc=kc, b=b, bv=bv, swv=swv: e.matmul(
                            out=PS(bv)[0:4, 0:256], lhsT=xnT[:, kc, TO + 4 * b:TO + 4 * b + 4],
                            rhs=wb[swv][:, kc, 0:256], start=(kc == 0), stop=(kc == 7)),
                            reads=[wvk], writes=PSK(bv), inc=(kc == 7))
                    P.op("act", lambda e, cs=cs, bv=bv: e.activation(out=Vnb[cs][0:4, :], in_=PS(bv)[0:4, 0:256], func=AF.Copy),
                         reads=PSK(bv), writes=[vnk])
                    gens = []
                    for si in range(ns):
                        if d == 1:
                            nq, t0q, mk = 4, TO + 4 * b, maskA[0:4, 0:132]
                        else:
                            nq, t0q, mk = 1, TO + 4 * b + si, maskS[0:1, si, 0:132]
                        gens.append(attn_block(
                            nq, 4,
                            (lambda c, hh, t0q=t0q, nq=nq: qT[:, hh, c, t0q:t0q + nq]),
                            (lambda c, hh, si=si, cs=cs: kTc[cs][:, si, c, :]),
                            (lambda c, hh, b=b: kT[:, c, TO + 4 * b:TO + 4 * b + 4]),
                            (lambda c, si=si, cs=cs: ckt[cs][:, si, 256 + c * 128:256 + (c + 1) * 128]),
                            (lambda c, cs=cs: Vnb[cs][0:4, c * 128:(c + 1) * 128]),
                            mk,
                            (lambda c, hh, t0q=t0q, nq=nq: UTg[hh * 64:(hh + 1) * 64, c, t0q:t0q + nq]),
                            stats_d[t0q:t0q + nq, 4 * g:4 * g + 4, :], rk + utk + [ckk, kck, vnk]))
                    run_pipelined(gens, 3)
                for c in range(2):
                    P.dma("sp", mixd[2 * g + c, :, :], UTg[:, c, :], "uto", reads=utk)
            P.barrier()
            P.replay(nc, st1b, "c", st)
            st1b.close()
        st1.close()

        P = Prog()
        st2 = ExitStack()
        cur[0] = st2
        w_out_v = w_out.rearrange("(k p) n -> p k n", p=128)
        w_gu_v = w_gu.rearrange("(k p) n -> p k n", p=128)
        w_down_v = w_down.rearrange("(k p) n -> p k n", p=128)
        wbig = sbt("wbig", [128, 22, 512], BF16)
        mixT = sbt("mixg", [128, 12, 1088], BF16)
        hbuf = sbt("hbuf", [128, 9, 1024], F32)
        hnT = sbt("hnT", [128, 8, 1088], BF16)
        aT = sbt("aT", [128, 22, 1088], BF16)
        sgt = [sbt("sgt%d" % i, [128, 512], BF16) for i in range(2)]
        yo = [sbt("yo0", [128, 1024], F32)] * 2
        fst = [sbt("fst%d" % i, [128, 4], F32) for i in range(2)]
        stmt = [sbt("stm%d" % i, [128, 12, 2], F32) for i in range(2)]
        Wtt = [sbt("Wt%d" % i, [128, 32], F32) for i in range(2)]
        Wxt = [sbt("Wx%d" % i, [128, 768], BF16) for i in range(2)]
        srot = Rot(range(2))
        prot2 = Rot([0, 1, 2, 3])
        grot = Rot([4, 5, 6, 7])
        groups = [(0, 1024), (1024, 1088)]
        def ffn_group(t0, ntok):
            tiles = [(i // 128, t0 + i, min(128, ntok - i)) for i in range(0, ntok, 128)]
            for (ti, tb, T) in tiles:
                src = xo[tb:tb + T, :] if tb < TO else xs[tb - TO:tb - TO + T, :]
                P.dma("sp", hbuf[0:T, ti, :], src, "hb%d" % ti, writes=[("hbuf", ti)])
            if stage >= 4:
                P.dma("sp", mixT[:, :, 0:ntok], mixd[:, :, t0:t0 + ntok].rearrange("j p t -> p j t"), "mixg",
                      writes=["mixT"])
            else:
                P.op("dve", lambda e: e.memset(mixT[:, :, :], 0.0), writes=["mixT"])
            if stage >= 4 and not os.environ.get("KDBG_NOATT") and int(os.environ.get("KATT", "9")) >= 4:
                for (ti, tb, T) in tiles:
                    ms_ = srot.next()
                    stm, Wt, Wx = stmt[ms_], Wtt[ms_], Wxt[ms_]
                    sk = "stm%d" % ms_
                    P.dma("sp", stm[0:T, :, :], stats_d[tb:tb + T, :, :], sk, writes=[sk])
                    m3 = stm[0:T, :, :].rearrange("p (g h) t -> p g h t", g=3)
                    mo = lambda fn: P.op("dve", fn, reads=[sk], writes=[sk])
                    W3 = Wt[0:T, 0:12].rearrange("p (g h) -> p g h", g=3)
                    E3 = Wt[0:T, 12:24].rearrange("p (g h) -> p g h", g=3)
                    Mx, Dn = Wt[0:T, 24:28], Wt[0:T, 28:32]
                    mo(lambda e, m3=m3, Mx=Mx: e.tensor_tensor(out=Mx, in0=m3[:, 0, :, 0], in1=m3[:, 1, :, 0], op=ALU.max))
                    mo(lambda e, m3=m3, Mx=Mx: e.tensor_tensor(out=Mx, in0=Mx, in1=m3[:, 2, :, 0], op=ALU.max))
                    mo(lambda e, m3=m3, Mx=Mx, E3=E3, T=T: e.tensor_tensor(
                        out=E3, in0=m3[:, :, :, 0], in1=Mx.unsqueeze(1).broadcast_to([T, 3, 4]), op=ALU.subtract))
                    P.op("act", lambda e, E3=E3: e.activation(out=E3, in_=E3, func=AF.Exp), reads=[sk], writes=[sk])
                    mo(lambda e, m3=m3, E3=E3, W3=W3: e.tensor_tensor(out=W3, in0=E3, in1=m3[:, :, :, 1], op=ALU.mult))
                    mo(lambda e, W3=W3, Dn=Dn: e.tensor_tensor(out=Dn, in0=W3[:, 0, :], in1=W3[:, 1, :], op=ALU.add))
                    mo(lambda e, W3=W3, Dn=Dn: e.tensor_tensor(out=Dn, in0=Dn, in1=W3[:, 2, :], op=ALU.add))
                    mo(lambda e, Dn=Dn: e.reciprocal(out=Dn, in_=Dn))
                    mo(lambda e, E3=E3, W3=W3, Dn=Dn, T=T: e.tensor_tensor(
                        out=W3, in0=E3, in1=Dn.unsqueeze(1).broadcast_to([T, 3, 4]), op=ALU.mult))
                    P.op("dve", lambda e, Wt=Wt, Wx=Wx, T=T: e.tensor_copy(
                        out=Wx[0:T, :].rearrange("p (a d) -> p a d", a=12),
                        in_=Wt[0:T, 0:12].unsqueeze(2).broadcast_to([T, 12, 64])), reads=[sk], writes=["Wx%d" % ms_])
                    bm = prot2.next()
                    pw = PS(bm).bitcast(BF16).rearrange("p (a q) -> p a q", a=8)
                    for j in range(6):
                        P.op("pe", lambda e, j=j, pw=pw, Wx=Wx, T=T: e.transpose(
                            out=pw[:, j, 0:T], in_=Wx[0:T, j * 128:(j + 1) * 128], identity=identb[0:T, 0:T]),
                            reads=["Wx%d" % ms_, "identb"], writes=PSK(bm), inc=(j == 5))
                    P.op("dve", lambda e, pw=pw, tb=tb, T=T: e.tensor_tensor(
                        out=mixT[:, 0:6, tb - t0:tb - t0 + T], in0=mixT[:, 0:6, tb - t0:tb - t0 + T],
                        in1=pw[:, 0:6, 0:T], op=ALU.mult), reads=PSK(bm) + ["mixT"], writes=["mixT"])
            for half in range(2):
                P.dma("pool", wbig[:, 0:6, :], w_out_v[:, 0:6, half * 512:(half + 1) * 512], "wbigA", writes=["wbigA"])
                P.dma("pool", wbig[:, 6:12, :], w_out_v[:, 6:12, half * 512:(half + 1) * 512], "wbigB", writes=["wbigB"])
                for (ti, tb, T) in tiles:
                    b = prot2.next()
                    for kc in range(12):
                        P.op("pe", lambda e, kc=kc, b=b, tb=tb, T=T: e.matmul(
                            out=PS(b)[0:T, :], lhsT=mixT[:, kc, tb - t0:tb - t0 + T], rhs=wbig[:, kc, :],
                            start=(kc == 0), stop=(kc == 11)), reads=["wbigA" if kc < 6 else "wbigB", "mixT"], writes=PSK(b), inc=(kc == 11))
                    P.op("dve", lambda e, b=b, ti=ti, T=T, half=half: e.tensor_tensor(
                        out=hbuf[0:T, ti, half * 512:(half + 1) * 512], in0=PS(b)[0:T, :],
                        in1=hbuf[0:T, ti, half * 512:(half + 1) * 512], op=ALU.add),
                        reads=PSK(b) + [("hbuf", ti)], writes=[("hbuf", ti)])
            for (ti, tb, T) in tiles:
                norm_T(hbuf[0:T, ti, :], ("hbuf", ti), T, gffn[:, :], "gffn", hnT[:, :, ti * 128:ti * 128 + T],
                       [("hnT", ti)], prot2.next(), from_dram=False)
            hk = [("hnT", ti) for (ti, _, _) in tiles]
            tranges = [(a0, min(512, ntok - a0)) for a0 in range(0, ntok, 512)]
            for j0 in range(0, 22, 2):
                sw = wrot.next()
                wkey = "wb%d" % sw
                P.dma("pool", wb[sw][:, :, 0:256], w_gu_v[:, :, j0 * 128:(j0 + 2) * 128], wkey, writes=[wkey])
                P.dma("pool", wb[sw][:, :, 256:512], w_gu_v[:, :, 2816 + j0 * 128:2816 + (j0 + 2) * 128], wkey,
                      writes=[wkey])
                for jj in range(2):
                    j = j0 + jj
                    for (a0, an) in tranges:
                        bg, bu = grot.next(), grot.next()
                        for (bb, c0) in ((bg, jj * 128), (bu, 256 + jj * 128)):
                            for kc in range(8):
                                P.op("pe", lambda e, kc=kc, bb=bb, c0=c0, sw=sw, a0=a0, an=an: e.matmul(
                                    out=PS(bb)[:, 0:an], lhsT=wb[sw][:, kc, c0:c0 + 128], rhs=hnT[:, kc, a0:a0 + an],
                                    start=(kc == 0), stop=(kc == 7)), reads=[wkey] + hk, writes=PSK(bb), inc=(kc == 7))
                        ss_ = srot.next()
                        P.op("act", lambda e, bg=bg, ss_=ss_, an=an: e.activation(
                            out=sgt[ss_][:, 0:an], in_=PS(bg)[:, 0:an], func=AF.Silu), reads=PSK(bg),
                            writes=["sgt%d" % ss_])
                        P.op("dve", lambda e, bu=bu, ss_=ss_, j=j, a0=a0, an=an: e.tensor_tensor(
                            out=aT[:, j, a0:a0 + an], in0=sgt[ss_][:, 0:an], in1=PS(bu)[:, 0:an], op=ALU.mult),
                            reads=PSK(bu) + ["sgt%d" % ss_], writes=[("aT", j)])
            ak = [("aT", j) for j in range(22)]
            for half in range(2):
                P.dma("pool", wbig[:, 0:6, :], w_down_v[:, 0:6, half * 512:(half + 1) * 512], "wbigA", writes=["wbigA"])
                P.dma("pool", wbig[:, 6:22, :], w_down_v[:, 6:22, half * 512:(half + 1) * 512], "wbigB", writes=["wbigB"])
                for (ti, tb, T) in tiles:
                    b = prot2.next()
                    for fc in range(22):
                        P.op("pe", lambda e, fc=fc, b=b, ti=ti, T=T: e.matmul(
                            out=PS(b)[0:T, :], lhsT=aT[:, fc, ti * 128:ti * 128 + T], rhs=wbig[:, fc, :],
                            start=(fc == 0), stop=(fc == 21)), reads=["wbigA" if fc < 6 else "wbigB"] + ak, writes=PSK(b), inc=(fc == 21))
                    P.op("dve", lambda e, b=b, ti=ti, T=T, half=half: e.tensor_tensor(
                        out=hbuf[0:T, ti, half * 512:(half + 1) * 512], in0=PS(b)[0:T, :],
                        in1=hbuf[0:T, ti, half * 512:(half + 1) * 512], op=ALU.add),
                        reads=PSK(b) + [("hbuf", ti)], writes=[("hbuf", ti)])
            for (ti, tb, T) in tiles:
                s_ = srot.next()
                fs, fk = fst[s_], "fst%d" % s_
                P.op("act", lambda e, ti=ti, T=T, fs=fs: e.activation(
                    out=junk[0:T, :], in_=hbuf[0:T, ti, :], func=AF.Square, scale=1.0 / 32, accum_out=fs[0:T, 0:1]),
                    reads=[("hbuf", ti)], writes=["junk", fk])
                P.op("dve", lambda e, T=T, fs=fs: e.tensor_scalar(out=fs[0:T, 1:2], in0=fs[0:T, 0:1], scalar1=EPS,
                                                                  scalar2=None, op0=ALU.add), reads=[fk], writes=[fk])
                P.op("act", lambda e, T=T, fs=fs: e.activation(out=fs[0:T, 2:3], in_=fs[0:T, 1:2], func=AF.Ln),
                     reads=[fk], writes=[fk])
                P.op("act", lambda e, T=T, fs=fs: e.activation(out=fs[0:T, 3:4], in_=fs[0:T, 2:3], func=AF.Exp,
                                                               scale=-0.5), reads=[fk], writes=[fk])
                P.op("dve", lambda e, ti=ti, T=T, fs=fs, s_=s_: e.scalar_tensor_tensor(
                    out=yo[s_][0:T, :], in0=hbuf[0:T, ti, :], scalar=fs[0:T, 3:4], in1=gfin[0:T, :],
                    op0=ALU.mult, op1=ALU.mult), reads=[("hbuf", ti), fk, "gfin"], writes=["yo0"])
                P.dma("sp", y_o[tb:tb + T, :], yo[s_][0:T, :], "yout0", reads=["yo0"])

        for (t0_, ntok_) in groups:
            ffn_group(t0_, ntok_)
        P.barrier()
        P.replay(nc, st2, "b", st)
        st2.close()
    return nc


_NC_CACHE = {}


def _consts():
    ident = np.eye(128, dtype=np.float32)
    i = np.arange(128)
    triu = (i[:, None] <= i[None, :]).astype(np.float32)
    sl = (i[:, None] > i[None, :]).astype(np.float32)
    ones = np.ones((128, 128), np.float32)
    cm = np.concatenate([ident, triu, sl, ones], axis=1)
    q = np.arange(128)[:, None]
    k = np.arange(256)[None, :]
    valid = (k >= q) & (k <= q + 128)
    maskA = np.where(valid, 0.0, NEG).astype(np.float32)
    maskS = np.full((4, 256), NEG, np.float32)
    for s in range(4):
        maskS[s, 0:128] = 0.0
        maskS[s, 128 + s] = 0.0
    return cm, maskA, maskS.reshape(1, 1024)


def kernel(x_prompt, x_sample, cache_kv_d1, cache_kv_d4, cache_kv_d16, state_conv, state_ssm,
           norm_mix, w_in, conv_w, conv_b, dt_bias, a_log, d_skip, norm_ssd, w_out, norm_ffn,
           w_gate_up, w_down, norm_final, _stage=99):
    f = lambda a: np.ascontiguousarray(np.asarray(a, dtype=np.float32))
    if _stage not in _NC_CACHE:
        _NC_CACHE[_stage] = build(_stage)
    nc = _NC_CACHE[_stage]
    cm, maskA, maskS = _consts()
    bc = lambda v, n: f(np.broadcast_to(np.asarray(v, np.float32).reshape(1, -1), (128, n)))
    col = lambda v, k: f(np.asarray(v, np.float32).reshape(k, 128).T)
    cw = np.asarray(conv_w, np.float32)[0]
    convw = f(cw.reshape(4, 14, 128).transpose(2, 1, 0).reshape(128, 56))
    shared = {
        "w_in": f(w_in[0]), "w_out": f(w_out[0]), "w_gu": f(w_gate_up[0]), "w_down": f(w_down[0]),
        "gmix": col(norm_mix[0], 8), "gffn": col(norm_ffn[0], 8), "gfin": bc(norm_final, 1024),
        "gssd": bc(norm_ssd[0], 768), "convw": convw, "convb": col(conv_b[0], 14),
        "dtb": bc(dt_bias[0], 12), "alog": bc(a_log[0], 12), "dsk": bc(d_skip[0], 12),
        "maskA": maskA, "maskS": maskS, "cmats": cm,
    }
    xpn = np.asarray(x_prompt, np.float32)
    xsn = np.asarray(x_sample, np.float32)
    caches = [np.asarray(cache_kv_d1, np.float32)[0], np.asarray(cache_kv_d4, np.float32)[0],
              np.asarray(cache_kv_d16, np.float32)[0]]
    sc = np.asarray(state_conv, np.float32)[0]
    ss = np.asarray(state_ssm, np.float32)[0]
    in_maps = []
    for c in range(8):
        b, half = c // 2, c % 2
        m = dict(shared)
        m["xo"] = f(xpn[b, half * TO:(half + 1) * TO])
        m["xp"] = f(xpn[b, 0:TO]) if half == 1 else np.zeros((TO, 1024), np.float32)
        m["xs"] = f(xsn[16 * c:16 * c + 16].reshape(64, 1024))
        for nm, ca, lb in zip(("ck1", "ck4", "ck16"), caches, (128, 512, 2048)):
            m[nm] = f(ca[16 * c:16 * c + 16].reshape(16, lb, 512))
        m["sconv"] = f(sc[16 * c:16 * c + 16].reshape(48, 1792))
        m["sssm"] = f(ss[16 * c:16 * c + 16].reshape(16, 768, 128))
        mf = maskA.copy()
        if half == 0:
            mf[:, 0:128] = NEG
        m["maskF"] = mf
        m["hp"] = np.full((128, 1), float(half), np.float32)
        in_maps.append(m)
    res = run_bass_kernel_spmd(nc, in_maps, core_ids=list(range(8))).results
    y_prompt = np.zeros((4, 4096, 1024), np.float32)
    y_sample = np.zeros((128, 4, 1024), np.float32)
    for c in range(8):
        b, half = c // 2, c % 2
        y_prompt[b, half * TO:(half + 1) * TO] = res[c]["y"][0:TO]
        y_sample[16 * c:16 * c + 16] = res[c]["y"][TO:TT].reshape(16, 4, 1024)
    odd = [1, 3, 5, 7]
    p_kv = [np.stack([res[c][n] for c in odd]).reshape(1, 4, lb, 2, 4, 64)
            for n, lb in (("pkv1", 128), ("pkv4", 512), ("pkv16", 2048))]
    p_conv = np.stack([res[c]["pconv"] for c in odd]).reshape(1, 4, 3, 1792)
    p_ssm = np.stack([res[c]["pssm"] for c in odd]).reshape(1, 4, 12, 64, 128)
    s_kv = [np.concatenate([res[c][n] for c in range(8)]).reshape(1, 128, lb, 2, 4, 64)
            for n, lb in (("skv1", 128), ("skv4", 512), ("skv16", 2048))]
    s_conv = np.concatenate([res[c]["oconv"] for c in range(8)]).reshape(1, 128, 3, 1792)
    s_ssm = np.concatenate([res[c]["ossm"] for c in range(8)]).reshape(1, 128, 12, 64, 128)
    return (y_prompt, y_sample, p_kv[0], p_kv[1], p_kv[2], p_conv, p_ssm,
            s_kv[0], s_kv[1], s_kv[2], s_conv, s_ssm)
```

```python
import numpy as np
from contextlib import ExitStack
import concourse.bass as bass
import concourse.mybir as mybir
from concourse.bass_utils import run_bass_kernel_spmd

F32 = mybir.dt.float32
BF16 = mybir.dt.bfloat16
AF = mybir.ActivationFunctionType
ALU = mybir.AluOpType
AX = mybir.AxisListType

ENGS = ("pe", "act", "dve", "pool", "sp")
NEG = -1.0e30
DIL = (1, 4, 16)
Q0, K0, V0, Z0, X0, DT0 = 0, 768, 1536, 2304, 3072, 4864
TO, TS = 2048, 64
TT = TO + TS
SCALE = 0.125
EPS = 1e-5


class Prog:
    def __init__(self):
        self.ops = {e: [] for e in ENGS}
        self.cnt = {e: 0 for e in ENGS}
        self.waited = {e: {} for e in ENGS}
        self.dcnt = {}
        self.lastw = {}
        self.readers = {}

    def _emit(self, eng, fn, tok, hasinc, reads, writes, isdma):
        deps = {}

        def add(t):
            if t is not None and deps.get(t[0], 0) < t[1]:
                deps[t[0]] = t[1]
        for r in reads:
            add(self.lastw.get(r))
            if isinstance(r, str) and r.startswith("ps"):
                for s, v in self.readers.get(r, {}).items():
                    add((s, v))
        for w in writes:
            add(self.lastw.get(w))
            for s, v in self.readers.get(w, {}).items():
                add((s, v))
        waits = []
        own = "E" + eng
        for s, v in deps.items():
            if s == own and eng == "pe" and not isdma:
                continue
            if self.waited[eng].get(s, 0) >= v:
                continue
            self.waited[eng][s] = v
            waits.append((s, v))
        self.ops[eng].append((waits, fn, tok if hasinc else None, isdma))
        for r in reads:
            d = self.readers.setdefault(r, {})
            if d.get(tok[0], 0) < tok[1]:
                d[tok[0]] = tok[1]
        for w in writes:
            self.lastw[w] = tok
            self.readers[w] = {}
        return tok

    def op(self, eng, fn, reads=(), writes=(), inc=True):
        if inc:
            self.cnt[eng] += 1
            tok = ("E" + eng, self.cnt[eng])
        else:
            tok = ("E" + eng, self.cnt[eng] + 1)
        return self._emit(eng, fn, tok, inc, reads, writes, False)

    def dma(self, eng, out, in_, stream, reads=(), writes=()):
        self.dcnt[stream] = self.dcnt.get(stream, 0) + 16
        tok = ("D" + stream, self.dcnt[stream])
        return self._emit(eng, lambda e: e.dma_start(out=out, in_=in_), tok, True, reads, writes, True)

    def barrier(self):
        allw = [("E" + e, self.cnt[e]) for e in ENGS if self.cnt[e] > 0]
        allw += [("D" + s, v) for s, v in self.dcnt.items()]
        for eng in ENGS:
            waits = []
            for s, v in allw:
                if self.waited[eng].get(s, 0) >= v:
                    continue
                self.waited[eng][s] = v
                waits.append((s, v))
            self.ops[eng].append((waits, None, None, False))

    def replay(self, nc, stack, pfx="a", semstack=None):
        sems = {}
        for n in ["E" + e for e in ENGS] + ["D" + s for s in self.dcnt]:
            sems[n] = (semstack or stack).enter_context(nc.semaphore(pfx + n))
        block = stack.enter_context(nc.Block())
        handles = {"pe": block.tensor, "act": block.scalar, "dve": block.vector,
                   "pool": block.gpsimd, "sp": block.sync}
        for eng in ENGS:
            ops = self.ops[eng]

            def body(e, ops=ops):
                for (waits, fn, tok, isdma) in ops:
                    for (s, v) in waits:
                        e.wait_ge(sems[s], v)
                    if fn is None:
                        continue
                    ins = fn(e)
                    if tok is not None:
                        ins.then_inc(sems[tok[0]], 16 if isdma else 1)
            handles[eng](body)


class Rot:
    def __init__(self, items):
        self.items = list(items)
        self.i = 0

    def next(self):
        v = self.items[self.i % len(self.items)]
        self.i += 1
        return v


def run_pipelined(gens, depth):
    active = []
    gens = list(gens)
    gi = 0
    while gi < len(gens) or active:
        if gi < len(gens) and len(active) < depth:
            active.append(gens[gi])
            gi += 1
        nxt = []
        for g in active:
            try:
                next(g)
                nxt.append(g)
            except StopIteration:
                pass
        active = nxt


def build(stage=99):
    nc = bass.Bass("TRN2", target_bir_lowering=False)
    dram = lambda n, s, k="ExternalInput": nc.dram_tensor(n, s, F32, kind=k).ap()
    xo = dram("xo", [TO, 1024])
    xp = dram("xp", [TO, 1024])
    xs = dram("xs", [TS, 1024])
    ck = [dram("ck1", [16, 128, 512]), dram("ck4", [16, 512, 512]), dram("ck16", [16, 2048, 512])]
    sconv_in = dram("sconv", [48, 1792])
    sssm_in = dram("sssm", [16, 768, 128])
    w_in = dram("w_in", [1024, 4876])
    w_out = dram("w_out", [1536, 1024])
    w_gu = dram("w_gu", [1024, 5632])
    w_down = dram("w_down", [2816, 1024])
    gmix_d = dram("gmix", [128, 8])
    gffn_d = dram("gffn", [128, 8])
    gfin_d = dram("gfin", [128, 1024])
    gssd_d = dram("gssd", [128, 768])
    convw_d = dram("convw", [128, 14 * 4])
    convb_d = dram("convb", [128, 14])
    dtb_d = dram("dtb", [128, 12])
    alog_d = dram("alog", [128, 12])
    dsk_d = dram("dsk", [128, 12])
    maskA_d = dram("maskA", [128, 256])
    maskF_d = dram("maskF", [128, 256])
    maskS_d = dram("maskS", [1, 4 * 256])
    hp_d = dram("hp", [128, 1])
    cmats_d = dram("cmats", [128, 4 * 128])
    y_o = dram("y", [TT, 1024], "ExternalOutput")
    pkv_o = [dram("pkv1", [128, 512], "ExternalOutput"), dram("pkv4", [512, 512], "ExternalOutput"),
             dram("pkv16", [2048, 512], "ExternalOutput")]
    pconv_o = dram("pconv", [3, 1792], "ExternalOutput")
    pssm_o = dram("pssm", [768, 128], "ExternalOutput")
    skv_o = [dram("skv1", [16, 128, 512], "ExternalOutput"), dram("skv4", [16, 512, 512], "ExternalOutput"),
             dram("skv16", [16, 2048, 512], "ExternalOutput")]
    oconv_o = dram("oconv", [16, 3, 1792], "ExternalOutput")
    ossm_o = dram("ossm", [16, 768, 128], "ExternalOutput")
    stats_d = dram("stats", [TT, 12, 2], "Internal")
    mixd = nc.dram_tensor("mixd", [12, 128, TT], BF16, kind="Internal").ap()
    w_in_v = w_in.rearrange("(k p) n -> p k n", p=128)

    with ExitStack() as st:
        cur = [st]
        sbt = lambda n, s, d: cur[0].enter_context(nc.sbuf_tensor("sb_" + n, s, d))
        psum = st.enter_context(nc.psum_tensor("psum", [128, 4096], F32))
        P = Prog()

        def PS(b, nb=1):
            return psum[:, b * 512:(b + nb) * 512]

        def PSK(b, nb=1):
            return ["ps%d" % i for i in range(b, b + nb)]

        cm = sbt("cm", [128, 4, 128], F32)
        identb = sbt("identb", [128, 128], BF16)
        gmix = sbt("gmix", [128, 8], F32)
        gffn = sbt("gffn", [128, 8], F32)
        gfin = sbt("gfin", [128, 1024], F32)
        gssd = sbt("gssd", [128, 768], F32)
        convw = sbt("convw", [128, 14, 4], F32)
        convb = sbt("convb", [128, 14], F32)
        dtb = sbt("dtb", [128, 12], F32)
        abc = sbt("abc", [128, 12], F32)
        dsk = sbt("dsk", [128, 12], F32)
        maskA = sbt("maskA", [128, 256], F32)
        maskF = sbt("maskF", [128, 256], F32)
        maskS = sbt("maskS", [128, 4, 256], F32)
        hp = sbt("hp", [128, 1], F32)
        cl = [("cm", cm[:].rearrange("p a b -> p (a b)"), cmats_d), ("gmix", gmix[:], gmix_d),
              ("gffn", gffn[:], gffn_d), ("gfin", gfin[:], gfin_d), ("gssd", gssd[:], gssd_d),
              ("convw", convw[:].rearrange("p a b -> p (a b)"), convw_d), ("convb", convb[:], convb_d),
              ("dtb", dtb[:], dtb_d), ("abc", abc[:], alog_d), ("dsk", dsk[:], dsk_d),
              ("maskA", maskA[:], maskA_d), ("maskF", maskF[:], maskF_d),
              ("maskS", maskS[0:1].rearrange("p a b -> p (a b)"), maskS_d), ("hp", hp[:], hp_d)]
        for (k, t, d) in cl:
            P.dma("sp", t, d[:, :], "c_" + k, writes=[k])
        identf = cm[:, 0, :]
        triu = cm[:, 1, :]
        slm = cm[:, 2, :]
        onesm = cm[:, 3, :]
        P.op("dve", lambda e: e.tensor_copy(out=identb[:], in_=identf), reads=["cm"], writes=["identb"])
        P.op("act", lambda e: e.activation(out=abc[:], in_=abc[:], func=AF.Exp), reads=["abc"], writes=["abc"])
        P.op("dve", lambda e: e.tensor_scalar(out=abc[:], in0=abc[:], scalar1=-1.0, scalar2=None, op0=ALU.mult),
             reads=["abc"], writes=["abc"])

        xin = [sbt("xin%d" % i, [128, 1024], F32) for i in range(2)]
        xbb = [sbt("xbb%d" % i, [128, 1024], BF16) for i in range(2)]
        junk = sbt("junk", [128, 1024], BF16)
        nstat = [sbt("nstat%d" % i, [128, 4], F32) for i in range(2)]
        wb = [sbt("wb%d" % i, [128, 8, 512], BF16) for i in range(2)]
        wrot = Rot(range(2))
        nrot = Rot(range(2))
        st1 = ExitStack()
        cur[0] = st1
        xnT = sbt("xnT", [128, 8, TT], BF16)
        kvf = [sbt("kvf%d" % i, [128, 512], F32) for i in range(2)]
        kvrot = Rot(range(2))

        def load_w(wv, c0, ncols, nk=8):
            s = wrot.next()
            key = "wb%d" % s
            P.dma("pool", wb[s][:, 0:nk, 0:ncols], wv[:, 0:nk, c0:c0 + ncols], key, writes=[key])
            return wb[s], key

        def xkeys(t0, tn):
            return [("xnT", i) for i in range(t0 // 128, (t0 + tn - 1) // 128 + 1)]

        def norm_T(src, src_key, T, gcol, gkey, dst, dst_keys, psb, from_dram=True):
            s = nrot.next()
            if from_dram:
                xt = xin[s]
                P.dma("sp", xt[0:T, :], src, "xin%d" % s, writes=["xin%d" % s])
                xa, xk = xt[0:T, :], "xin%d" % s
            else:
                xa, xk = src, src_key
            ns = nstat[s]
            nk = "nstat%d" % s
            P.op("act", lambda e: e.activation(out=junk[0:T, :], in_=xa, func=AF.Square, scale=1.0 / 32,
                                               accum_out=ns[0:T, 0:1]), reads=[xk], writes=["junk", nk])
            P.op("dve", lambda e: e.tensor_scalar(out=ns[0:T, 1:2], in0=ns[0:T, 0:1], scalar1=EPS, scalar2=None,
                                                  op0=ALU.add), reads=[nk], writes=[nk])
            P.op("act", lambda e: e.activation(out=ns[0:T, 2:3], in_=ns[0:T, 1:2], func=AF.Ln), reads=[nk], writes=[nk])
            P.op("act", lambda e: e.activation(out=ns[0:T, 3:4], in_=ns[0:T, 2:3], func=AF.Exp, scale=-0.5),
                 reads=[nk], writes=[nk])
            xb = xbb[s]
            bk = "xbb%d" % s
            P.op("dve", lambda e: e.tensor_scalar(out=xb[0:T, :], in0=xa, scalar1=ns[0:T, 3:4], scalar2=None,
                                                  op0=ALU.mult), reads=[xk, nk], writes=[bk])
            pt = PS(psb).bitcast(BF16).rearrange("p (k t) -> p k t", k=8)
            for k in range(8):
                P.op("pe", lambda e, k=k: e.transpose(out=pt[:, k, 0:T], in_=xb[0:T, k * 128:(k + 1) * 128],
                                                      identity=identb[0:T, 0:T]),
                     reads=[bk, "identb"], writes=PSK(psb), inc=(k == 7))
            P.op("dve", lambda e: e.tensor_tensor(out=dst, in0=pt[:, :, 0:T],
                                                  in1=gcol.unsqueeze(2).broadcast_to([128, 8, T]), op=ALU.mult),
                 reads=PSK(psb) + [gkey], writes=dst_keys)

        prot = Rot([0, 1, 2, 3])

        def norm_tokens(srcd, ntok, base):
            for i in range(0, ntok, 128):
                T = min(128, ntok - i)
                norm_T(srcd[i:i + T, :], None, T, gmix[:, :], "gmix", xnT[:, :, base + i:base + i + T],
                       [("xnT", (base + i) // 128)], prot.next())

        def tokslice(start, d, n=128):
            return slice(start, start + d * (n - 1) + 1, d)

        Vtok = [sbt("Vtok%d" % g, [128, 16, 256], BF16) for g in range(3)]
        Vprev = [sbt("Vprev0", [128, 1, 256], BF16), sbt("Vprev1", [128, 4, 256], BF16),
                 sbt("Vprev2", [128, 16, 256], BF16)]
        Vnew = [sbt("Vnew%d" % g, [128, 256], BF16) for g in range(3)]
        kTp = [sbt("kTp0", [128, 2, 128], BF16), sbt("kTp1", [128, 2, 512], BF16), sbt("kTp2", [128, 2, 2048], BF16)]

        def block_tokens(g, blk):
            d = DIL[g]
            u, r = blk // d, blk % d
            return tokslice(u * 128 * d + r, d)

        def kv_block(g, wt, wkey, cols, T, want_k, vdst, vkey, out_dma):
            b = prot.next()
            c0 = 0 if want_k else 256
            for kc in range(8):
                P.op("pe", lambda e, kc=kc: e.matmul(out=PS(b)[0:T, c0:512], lhsT=xnT[:, kc, cols],
                                                     rhs=wt[:, kc, c0:512], start=(kc == 0), stop=(kc == 7)),
                     reads=[wkey] + [("xnT", i) for i in range(17)], writes=PSK(b), inc=(kc == 7))
            if out_dma is not None:
                s = kvrot.next()
                P.op("dve", lambda e: e.tensor_copy(out=kvf[s][0:T, :], in_=PS(b)[0:T, :]),
                     reads=PSK(b), writes=["kvf%d" % s])
                if vdst is not None:
                    P.op("act", lambda e: e.activation(out=vdst, in_=kvf[s][0:T, 256:512], func=AF.Copy),
                         reads=["kvf%d" % s], writes=[vkey])
                out_dma(kvf[s], "kvf%d" % s)
            elif vdst is not None:
                P.op("act", lambda e: e.activation(out=vdst, in_=PS(b)[0:T, 256:512], func=AF.Copy),
                     reads=PSK(b), writes=[vkey])

        def load_wkv(g):
            s = wrot.next()
            key = "wb%d" % s
            P.dma("pool", wb[s][:, :, 0:256], w_in_v[:, :, K0 + g * 256:K0 + (g + 1) * 256], key, writes=[key])
            P.dma("pool", wb[s][:, :, 256:512], w_in_v[:, :, V0 + g * 256:V0 + (g + 1) * 256], key, writes=[key])
            return wb[s], key

        import os
        for g in range(3 if not os.environ.get("KDBG_NOCC") else 0):
            lb = 128 * DIL[g]
            for b0 in range(0, 16, 4):
                P.dma("act", skv_o[g][b0:b0 + 4, 0:lb - 4, :], ck[g][b0:b0 + 4, 4:lb, :], "cc%d_%d" % (g, b0))

        SSD_ON = stage >= 3
        st1a = ExitStack()
        cur[0] = st1a
        if SSD_ON:
            GT = 512
            wz = sbt("wz", [128, 8, 768], BF16)
            wdt = sbt("wdt", [128, 8, 12], BF16)
            P.dma("pool", wz[:, :, :], w_in_v[:, :, Z0:Z0 + 768], "wz", writes=["wz"])
            P.dma("pool", wdt[:, :, :], w_in_v[:, :, DT0:DT0 + 12], "wdt", writes=["wdt"])
            stg = sbt("stg", [128, 14, 3 + GT], BF16)
            carry = sbt("carry", [128, 14, 3], BF16)
            xc = sbt("xc", [128, 14, GT], BF16)
            cvt = [sbt("cvt%d" % i, [128, GT], F32) for i in range(2)]
            cvrot = Rot(range(2))
            hT = sbt("hT", [128, 768], F32)
            hTb = sbt("hTb", [128, 768], BF16)
            dv = sbt("dv", [128, 128], F32)
            cdt = sbt("cdt", [128, 12], F32)
            xw = sbt("xw", [128, 768], BF16)
            xtok = sbt("xtok", [128, 768], BF16)
            Btok = sbt("Btok", [128, 512], BF16)
            sz = sbt("sz", [128, 768], F32)
            t1 = sbt("t1", [128, 768], F32)
            yv = sbt("yv", [128, 768], F32)
            CBm = sbt("CBm", [128, 4, 128], F32)
            Rt2 = [sbt("Rt%d" % i, [128, 3, 128], F32) for i in range(2)]
            Lh2 = [sbt("Lh%d" % i, [128, 3, 128], F32) for i in range(2)]
            Gh2 = [sbt("Gh%d" % i, [128, 3, 128], BF16) for i in range(2)]
            ynb = sbt("ynb", [128, 768], BF16)
            ych = sbt("ych", [128, 6, 128], BF16)
            sst = sbt("sst", [128, 6, 128], F32)
            gcol = lambda g: (g // 2) * 512 + (g % 2) * 192
            hcol = lambda h: gcol(h // 3) + (h % 3) * 64

            def v2(ap2d):
                return ap2d.rearrange("p (a c) -> p a c", a=2)

            def pv2(b, T):
                return PS(b, 2)[0:T, :].rearrange("p (a c) -> p a c", a=2)[:, :, 0:384]

            def ssd_chunk(T, xcf, xck, xnf, xnk, full, out_cols):
                D = lambda a, n=12: dv[0:T, a:a + n]
                if full:
                    for (c0, c1, b) in ((0, 512, 0), (512, 768, 1)):
                        for kc in range(8):
                            P.op("pe", lambda e, kc=kc, c0=c0, c1=c1, b=b: e.matmul(
                                out=PS(b)[0:T, 0:c1 - c0], lhsT=xnf(kc), rhs=wz[:, kc, c0:c1],
                                start=(kc == 0), stop=(kc == 7)), reads=["wz"] + xnk, writes=PSK(b), inc=(kc == 7))
                    P.op("act", lambda e: e.activation(out=sz[0:T, 0:512], in_=PS(0)[0:T, :], func=AF.Silu),
                         reads=PSK(0), writes=["sz"])
                    P.op("act", lambda e: e.activation(out=sz[0:T, 512:768], in_=PS(1)[0:T, 0:256], func=AF.Silu),
                         reads=PSK(1) + ["sz"], writes=["sz"])
                for kc in range(8):
                    P.op("pe", lambda e, kc=kc: e.matmul(out=PS(2)[0:T, 0:12], lhsT=xnf(kc), rhs=wdt[:, kc, :],
                                                         start=(kc == 0), stop=(kc == 7)),
                         reads=["wdt"] + xnk, writes=PSK(2), inc=(kc == 7))
                dvo = lambda fn, rd=(): P.op("dve", fn, reads=["dv"] + list(rd), writes=["dv"])
                aco = lambda fn, rd=(): P.op("act", fn, reads=["dv"] + list(rd), writes=["dv"])
                dvo(lambda e: e.tensor_tensor(out=D(0), in0=PS(2)[0:T, 0:12], in1=dtb[0:T, :], op=ALU.add),
                    PSK(2) + ["dtb"])
                dvo(lambda e: e.tensor_scalar(out=D(12), in0=D(0), scalar1=-1.0, scalar2=None, op0=ALU.mult))
                dvo(lambda e: e.tensor_tensor(out=D(12), in0=D(0), in1=D(12), op=ALU.max))
                aco(lambda e: e.activation(out=D(24), in_=D(12), func=AF.Exp, scale=-1.0))
                dvo(lambda e: e.tensor_scalar(out=D(24), in0=D(24), scalar1=1.0, scalar2=None, op0=ALU.add))
                aco(lambda e: e.activation(out=D(36), in_=D(24), func=AF.Ln))
                dvo(lambda e: e.scalar_tensor_tensor(out=D(48), in0=D(0), scalar=0.0, in1=D(36), op0=ALU.max,
                                                     op1=ALU.add))
                dvo(lambda e: e.tensor_tensor(out=D(60), in0=D(48), in1=abc[0:T, :], op=ALU.mult), ["abc"])
                P.op("pe", lambda e: e.matmul(out=PS(2)[0:T, 16:28], lhsT=triu[0:T, 0:T], rhs=D(60), start=True,
                                              stop=True), reads=["dv", "cm"], writes=PSK(2), inc=False)
                P.op("pe", lambda e: e.matmul(out=PS(2)[:, 32:44], lhsT=onesm[0:T, :], rhs=D(60), start=True,
                                              stop=True), reads=["dv", "cm"], writes=PSK(2))
                dvo(lambda e: e.tensor_copy(out=D(72), in_=PS(2)[0:T, 16:28]), PSK(2))
                P.op("act", lambda e: e.activation(out=cdt[:, :], in_=PS(2)[:, 32:44], func=AF.Exp),
                     reads=PSK(2), writes=["cdt"])
                dvo(lambda e: e.tensor_tensor(out=D(84), in0=PS(2)[0:T, 32:44], in1=D(72), op=ALU.subtract), PSK(2))
                aco(lambda e: e.activation(out=D(96), in_=D(84), func=AF.Exp))
                dvo(lambda e: e.tensor_tensor(out=D(96), in0=D(96), in1=D(48), op=ALU.mult))
                if full:
                    aco(lambda e: e.activation(out=D(108), in_=D(72), func=AF.Exp))
                ptx = PS(3).bitcast(BF16)
                ptb = PS(4).bitcast(BF16)
                for j in range(6):
                    P.op("pe", lambda e, j=j: e.transpose(out=ptx[0:T, j * 128:(j + 1) * 128], in_=xcf(j),
                                                          identity=identb[:, :]),
                         reads=xck + ["identb"], writes=PSK(3), inc=(j == 5))
                for g in range(4):
                    P.op("pe", lambda e, g=g: e.transpose(out=ptb[0:T, g * 128:(g + 1) * 128], in_=xcf(6 + g),
                                                          identity=identb[:, :]),
                         reads=xck + ["identb"], writes=PSK(4), inc=(g == 3))
                P.op("dve", lambda e: e.tensor_tensor(
                    out=xw[0:T, :].rearrange("p (h d) -> p h d", h=12),
                    in0=ptx[0:T, 0:768].rearrange("p (h d) -> p h d", h=12),
                    in1=D(96).unsqueeze(2).broadcast_to([T, 12, 64]), op=ALU.mult),
                    reads=PSK(3) + ["dv"], writes=["xw"])
                if full:
                    P.op("act", lambda e: e.activation(out=xtok[0:T, :], in_=ptx[0:T, 0:768], func=AF.Copy),
                         reads=PSK(3), writes=["xtok"])
                P.op("act", lambda e: e.activation(out=Btok[0:T, :], in_=ptb[0:T, 0:512], func=AF.Copy),
                     reads=PSK(4), writes=["Btok"])
                for g in range(4):
                    P.op("pe", lambda e, g=g: e.matmul(out=PS(6, 2)[:, gcol(g):gcol(g) + 192],
                                                       lhsT=Btok[0:T, g * 128:(g + 1) * 128],
                                                       rhs=xw[0:T, g * 192:(g + 1) * 192], start=True, stop=True),
                         reads=["Btok", "xw"], writes=PSK(6, 2), inc=(g == 3))
                if full:
                    for g in range(4):
                        P.op("pe", lambda e, g=g: e.matmul(out=PS(0, 2)[0:T, gcol(g):gcol(g) + 192],
                                                           lhsT=xcf(10 + g), rhs=hTb[:, g * 192:(g + 1) * 192],
                                                           start=True, stop=True),
                             reads=xck + ["hTb"], writes=PSK(0, 2), inc=(g == 3))
                    P.op("dve", lambda e: e.tensor_tensor(
                        out=v2(t1[0:T, :]).rearrange("p a (h d) -> p a h d", h=6),
                        in0=pv2(0, T).rearrange("p a (h d) -> p a h d", h=6),
                        in1=D(108).rearrange("p (a h) -> p a h", a=2).unsqueeze(3).broadcast_to([T, 2, 6, 64]),
                        op=ALU.mult), reads=PSK(0, 2) + ["dv"], writes=["t1"])
                    for g in range(4):
                        P.op("pe", lambda e, g=g: e.matmul(out=PS(4)[0:T, g * 128:g * 128 + T], lhsT=xcf(6 + g),
                                                           rhs=xcf(10 + g), start=True, stop=True),
                             reads=xck, writes=PSK(4), inc=(g == 3))
                    P.op("dve", lambda e: e.tensor_tensor(
                        out=CBm[0:T, :, 0:T], in0=PS(4)[0:T, :].rearrange("p (g l) -> p g l", g=4)[:, :, 0:T],
                        in1=triu[0:T, 0:T].unsqueeze(1).broadcast_to([T, 4, T]), op=ALU.mult),
                        reads=PSK(4) + ["cm"], writes=["CBm"])
                    for g in range(4):
                        q2 = g % 2
                        Rt, Lh, Gh = Rt2[q2], Lh2[q2], Gh2[q2]
                        rk_, lk_, gk_ = "Rt%d" % q2, "Lh%d" % q2, "Gh%d" % q2
                        sb_ = 5 if q2 == 0 else 3
                        c3 = lambda t: t[0:T, :, :].rearrange("p r l -> p (r l)")[:, 0:3 * T].rearrange(
                            "p (r l) -> p r l", r=3)
                        Rv, Lv, Gv = c3(Rt), c3(Lh), c3(Gh)
                        Sv3 = PS(sb_)[0:T, 0:3 * T].rearrange("p (r l) -> p r l", r=3)
                        P.op("dve", lambda e, g=g, Rv=Rv: e.tensor_tensor(
                            out=Rv, in0=triu[0:T, 0:T].unsqueeze(1).broadcast_to([T, 3, T]),
                            in1=dv[0:T, 60 + 3 * g:63 + 3 * g].unsqueeze(2).broadcast_to([T, 3, T]), op=ALU.mult),
                            reads=["dv", "cm"], writes=[rk_])
                        P.op("pe", lambda e, Rv=Rv, Sv3=Sv3: e.matmul(
                            out=Sv3, lhsT=slm[0:T, 0:T], rhs=Rv, start=True, stop=True),
                            reads=[rk_, "cm"], writes=PSK(sb_))
                        P.op("act", lambda e, Lv=Lv, Sv3=Sv3: e.activation(out=Lv, in_=Sv3, func=AF.Exp),
                             reads=PSK(sb_), writes=[lk_])
                        P.op("dve", lambda e, g=g, Lv=Lv: e.tensor_tensor(
                            out=Lv, in0=Lv,
                            in1=dv[0:T, 48 + 3 * g:51 + 3 * g].unsqueeze(2).broadcast_to([T, 3, T]), op=ALU.mult),
                            reads=[lk_, "dv"], writes=[lk_])
                        P.op("dve", lambda e, g=g, Lv=Lv, Gv=Gv: e.tensor_tensor(
                            out=Gv, in0=Lv,
                            in1=CBm[0:T, g, 0:T].unsqueeze(1).broadcast_to([T, 3, T]), op=ALU.mult),
                            reads=[lk_, "CBm"], writes=[gk_])
                        for r in range(3):
                            h = 3 * g + r
                            P.op("pe", lambda e, r=r, h=h, Gv=Gv: e.matmul(
                                out=PS(0, 2)[0:T, hcol(h):hcol(h) + 64], lhsT=Gv[:, r, :],
                                rhs=xtok[0:T, h * 64:(h + 1) * 64], start=True, stop=True),
                                reads=[gk_, "xtok"], writes=PSK(0, 2), inc=(r == 2))
                    P.op("dve", lambda e: e.tensor_tensor(out=v2(yv[0:T, :]), in0=pv2(0, T), in1=v2(t1[0:T, :]),
                                                          op=ALU.add), reads=PSK(0, 2) + ["t1"], writes=["yv"])
                    P.op("dve", lambda e: e.tensor_tensor(
                        out=t1[0:T, :].rearrange("p (h d) -> p h d", h=12),
                        in0=xtok[0:T, :].rearrange("p (h d) -> p h d", h=12),
                        in1=dsk[0:T, :].unsqueeze(2).broadcast_to([T, 12, 64]), op=ALU.mult),
                        reads=["xtok", "dsk", "yv"], writes=["t1"])
                    P.op("dve", lambda e: e.tensor_tensor(out=yv[0:T, :], in0=yv[0:T, :], in1=t1[0:T, :], op=ALU.add),
                         reads=["yv", "t1"], writes=["yv"])
                    P.op("dve", lambda e: e.tensor_tensor(out=yv[0:T, :], in0=yv[0:T, :], in1=sz[0:T, :], op=ALU.mult),
                         reads=["yv", "sz"], writes=["yv"])
                    P.op("act", lambda e: e.activation(out=t1[0:T, :], in_=yv[0:T, :], func=AF.Square,
                                                       scale=float(768 ** -0.5), accum_out=D(120, 1)),
                         reads=["yv", "dv"], writes=["t1", "dv"])
                    dvo(lambda e: e.tensor_scalar(out=D(121, 1), in0=D(120, 1), scalar1=EPS, scalar2=None, op0=ALU.add))
                    aco(lambda e: e.activation(out=D(122, 1), in_=D(121, 1), func=AF.Ln))
                    aco(lambda e: e.activation(out=D(123, 1), in_=D(122, 1), func=AF.Exp, scale=-0.5))
                    P.op("dve", lambda e: e.scalar_tensor_tensor(out=ynb[0:T, :], in0=yv[0:T, :], scalar=D(123, 1),
                                                                 in1=gssd[0:T, :], op0=ALU.mult, op1=ALU.mult),
                         reads=["yv", "dv", "gssd"], writes=["ynb"])
                    for j in range(6):
                        P.op("pe", lambda e, j=j: e.transpose(out=ptx[:, j * 128:j * 128 + T],
                                                              in_=ynb[0:T, j * 128:(j + 1) * 128],
                                                              identity=identb[0:T, 0:T]),
                             reads=["ynb", "identb"], writes=PSK(3), inc=(j == 5))
                    P.op("act", lambda e: e.activation(
                        out=ych[:, :, 0:T], in_=ptx[:, 0:768].rearrange("p (j t) -> p j t", j=6)[:, :, 0:T],
                        func=AF.Copy), reads=PSK(3), writes=["ych"])
                    P.dma("sp", mixd[6:12, :, out_cols:out_cols + T].rearrange("j p t -> p j t"), ych[:, :, 0:T],
                          "ycho", reads=["ych"])
                P.op("dve", lambda e: e.tensor_tensor(
                    out=hT[:, :].rearrange("p (h d) -> p h d", h=12), in0=hT[:, :].rearrange("p (h d) -> p h d", h=12),
                    in1=cdt[:, :].unsqueeze(2).broadcast_to([128, 12, 64]), op=ALU.mult),
                    reads=["hT", "cdt"], writes=["hT"])
                P.op("dve", lambda e: e.tensor_tensor(out=v2(hT[:, :]), in0=pv2(6, 128), in1=v2(hT[:, :]), op=ALU.add),
                     reads=PSK(6, 2) + ["hT"], writes=["hT"])
                P.op("act", lambda e: e.activation(out=hTb[:, :], in_=hT[:, :], func=AF.Copy), reads=["hT"],
                     writes=["hTb"])

            def conv_silu(j, ins, outc, cview):
                cs = cvrot.next()
                cv = cview(cvt[cs])
                ck_ = "cvt%d" % cs
                P.op("dve", lambda e: e.tensor_scalar(out=cv, in0=ins[0], scalar1=convw[:, j, 0:1],
                                                      scalar2=convb[:, j:j + 1], op0=ALU.mult, op1=ALU.add),
                     reads=["stg", "convw", "convb"], writes=[ck_])
                for w in range(1, 4):
                    P.op("dve", lambda e, w=w: e.scalar_tensor_tensor(out=cv, in0=ins[w], scalar=convw[:, j, w:w + 1],
                                                                      in1=cv, op0=ALU.mult, op1=ALU.add),
                         reads=["stg", "convw", ck_], writes=[ck_])
                P.op("act", lambda e: e.activation(out=outc, in_=cv, func=AF.Silu), reads=[ck_], writes=["xc"])

            def xbc_proj(tok0, ntok, evac):
                for cg in range(4):
                    ncol = 512 if cg < 3 else 256
                    wt, wkey = load_w(w_in_v, X0 + cg * 512, ncol)
                    for jj in range(ncol // 128):
                        j = cg * 4 + jj
                        b = prot.next()
                        for kc in range(8):
                            P.op("pe", lambda e, kc=kc, b=b, jj=jj, wt=wt: e.matmul(
                                out=PS(b)[:, 0:ntok], lhsT=wt[:, kc, jj * 128:(jj + 1) * 128],
                                rhs=xnT[:, kc, tok0:tok0 + ntok], start=(kc == 0), stop=(kc == 7)),
                                reads=[wkey] + xkeys(tok0, ntok), writes=PSK(b), inc=(kc == 7))
                        evac(j, b)

            def ssd_group(tok0, ntok, full):
                P.op("dve", lambda e: e.tensor_copy(out=stg[:, :, 0:3], in_=carry[:, :, :]), reads=["carry", "xc"],
                     writes=["stg"])
                xbc_proj(tok0, ntok, lambda j, b: P.op("act", lambda e: e.activation(
                    out=stg[:, j, 3:3 + ntok], in_=PS(b)[:, 0:ntok], func=AF.Copy), reads=PSK(b), writes=["stg"]))
                P.op("dve", lambda e: e.tensor_copy(out=carry[:, :, :], in_=stg[:, :, ntok:ntok + 3]), reads=["stg"],
                     writes=["carry"])
                for j in range(14):
                    conv_silu(j, [stg[:, j, w:w + ntok] for w in range(4)], xc[:, j, 0:ntok],
                              lambda t: t[:, 0:ntok])
                for ci in range(ntok // 128):
                    ssd_chunk(128, lambda j, ci=ci: xc[:, j, ci * 128:(ci + 1) * 128], ["xc"],
                              lambda kc, ci=ci: xnT[:, kc, tok0 + ci * 128:tok0 + (ci + 1) * 128],
                              xkeys(tok0 + ci * 128, 128), full, tok0 + ci * 128)

            def state_out(dst):
                for a in range(6):
                    P.op("pe", lambda e, a=a: e.transpose(out=PS(0, 2)[:, a * 128:(a + 1) * 128],
                                                          in_=hT[:, a * 128:(a + 1) * 128], identity=identf),
                         reads=["hT", "cm"], writes=PSK(0, 2), inc=(a == 5))
                P.op("dve", lambda e: e.tensor_copy(out=sst[:, :, :].rearrange("p a n -> p (a n)"),
                                                    in_=PS(0, 2)[:, 0:768]), reads=PSK(0, 2), writes=["sst"])
                P.dma("sp", dst.rearrange("(a p) n -> p a n", p=128), sst[:, :, :], "ssto", reads=["sst"])
        LV = int(os.environ.get("KDBG_LV", "9"))
        if LV >= 1:
            norm_tokens(xp, TO, 0)
        if LV >= 2:
            for g in range(3):
                d = DIL[g]
                wt, wkey = load_wkv(g)
                nb = d
                for r in range(d):
                    blk = (16 // d - 1) * d + r if d < 16 else r
                    kv_block(g, wt, wkey, block_tokens(g, blk), 128, False, Vprev[g][:, r, :],
                             ("Vprev", g, r), None)
        if stage >= 4:
            for g in range(3):
                npv = 128 * DIL[g]
                wt, wkey = load_w(w_in_v, K0 + g * 256, 256)
                for c in range(2):
                    for a0 in range(0, npv, 512):
                        an = min(512, npv - a0)
                        b = prot.next()
                        for kc in range(8):
                            P.op("pe", lambda e, kc=kc, b=b, c=c, a0=a0, an=an, wt=wt, npv=npv: e.matmul(
                                out=PS(b)[:, 0:an], lhsT=wt[:, kc, c * 128:(c + 1) * 128],
                                rhs=xnT[:, kc, TO - npv + a0:TO - npv + a0 + an], start=(kc == 0), stop=(kc == 7)),
                                reads=[wkey] + [("xnT", i) for i in range(17)], writes=PSK(b), inc=(kc == 7))
                        P.op("act", lambda e, b=b, c=c, a0=a0, an=an, g=g: e.activation(
                            out=kTp[g][:, c, a0:a0 + an], in_=PS(b)[:, 0:an], func=AF.Copy),
                            reads=PSK(b), writes=["kTp"])
        if SSD_ON:
            P.op("dve", lambda e: e.memset(hT[:, :], 0.0), writes=["hT"])
            P.op("dve", lambda e: e.memset(hTb[:, :], 0.0), writes=["hTb"])
            P.op("dve", lambda e: e.memset(carry[:, :, :], 0.0), writes=["carry"])
            for gi in range(TO // GT):
                ssd_group(gi * GT, GT, False)
            P.op("dve", lambda e: e.tensor_scalar(out=hT[:, :], in0=hT[:, :], scalar1=hp[:, 0:1], scalar2=None,
                                                  op0=ALU.mult), reads=["hT", "hp"], writes=["hT"])
            P.op("act", lambda e: e.activation(out=hTb[:, :], in_=hT[:, :], func=AF.Copy), reads=["hT"], writes=["hTb"])
            P.op("dve", lambda e: e.tensor_scalar(out=carry[:, :, :].rearrange("p a b -> p (a b)"),
                                                  in0=carry[:, :, :].rearrange("p a b -> p (a b)"),
                                                  scalar1=hp[:, 0:1], scalar2=None, op0=ALU.mult),
                 reads=["carry", "hp"], writes=["carry"])
        SUB = int(os.environ.get("KDBG_SUB", "9"))
        if LV >= 3:
            norm_tokens(xo, TO, 0)
            if SUB >= 2:
                norm_tokens(xs, TS, TO)
        for g in range(3 if (LV >= 3 and SUB >= 1) else 0):
            d = DIL[g]
            lb = 128 * d
            wt, wkey = load_wkv(g)
            for blk in range(16):
                u, r = blk // d, blk % d
                is_out = (u == 16 // d - 1)

                def od(t, tk, g=g, r=r, d=d):
                    if os.environ.get("KDBG_CONT"):
                        P.dma("sp", pkv_o[g][r * 128:(r + 1) * 128, :], t[:, :], "pk" + tk, reads=[tk])
                    else:
                        P.dma("sp", pkv_o[g][r:128 * d:d, :], t[:, :], "pk" + tk, reads=[tk])
                kv_block(g, wt, wkey, block_tokens(g, blk), 128, is_out, Vtok[g][:, blk, :],
                         ("Vtok", g, blk), od if (is_out and LV >= 4) else None)
            def ods(t, tk, g=g, lb=lb):
                for s_ in range(4):
                    P.dma("sp", skv_o[g][:, lb - 4 + s_, :], t[s_:64:4, :], "sk" + tk, reads=[tk])
            if SUB >= 3:
                kv_block(g, wt, wkey, slice(TO, TT), 64, True, Vnew[g][0:64, :], ("Vnew", g), ods if LV >= 5 else None)

        for cg in range(4):
            ncol = 512 if cg < 3 else 256
            wt, wkey = load_w(w_in_v, X0 + cg * 512, ncol)
            for (which, cols, T) in (("p", slice(TO - 3, TO), 3), ("s", slice(TO, TT), 64)):
                b = prot.next()
                for kc in range(8):
                    P.op("pe", lambda e, kc=kc, b=b, cols=cols, T=T, wt=wt, ncol=ncol: e.matmul(
                        out=PS(b)[0:T, 0:ncol], lhsT=xnT[:, kc, cols], rhs=wt[:, kc, 0:ncol],
                        start=(kc == 0), stop=(kc == 7)),
                        reads=[wkey, ("xnT", 15), ("xnT", 16)], writes=PSK(b), inc=(kc == 7))
                ks = kvrot.next()
                P.op("dve", lambda e, b=b, ks=ks, T=T, ncol=ncol: e.tensor_copy(
                    out=kvf[ks][0:T, 0:ncol], in_=PS(b)[0:T, 0:ncol]), reads=PSK(b), writes=["kvf%d" % ks])
                if which == "p":
                    P.dma("sp", pconv_o[:, cg * 512:cg * 512 + ncol], kvf[ks][0:3, 0:ncol], "cvkvf%d" % ks, reads=["kvf%d" % ks])
                else:
                    for s_ in range(1, 4):
                        P.dma("sp", oconv_o[:, s_ - 1, cg * 512:cg * 512 + ncol], kvf[ks][s_:64:4, 0:ncol], "cvkvf%d" % ks,
                              reads=["kvf%d" % ks])
        if SSD_ON:
            for gi in range(TO // GT):
                ssd_group(gi * GT, GT, True)
            state_out(pssm_o)
            stg_s = stg[:, :, 0:112].rearrange("p j (b w) -> p j b w", w=7)
            for j in range(14):
                if j % 6 == 0:
                    ncs = min(768, 1792 - j * 128)
                    P.dma("sp", sz[0:48, 0:ncs], sconv_in[:, j * 128:j * 128 + ncs], "sct", writes=["sz"])
                jl = j % 6
                b = prot.next()
                P.op("pe", lambda e, jl=jl, b=b: e.transpose(out=PS(b)[:, 0:48], in_=sz[0:48, jl * 128:(jl + 1) * 128],
                                                            identity=identf[0:48, 0:48]),
                     reads=["sz", "cm"], writes=PSK(b))
                P.op("dve", lambda e, j=j, b=b: e.tensor_copy(
                    out=stg_s[:, j, :, 0:3], in_=PS(b)[:, 0:48].rearrange("p (b r) -> p b r", r=3)),
                    reads=PSK(b) + ["xc"], writes=["stg"])
            xbc_proj(TO, 64, lambda j, b: P.op("act", lambda e: e.activation(
                out=stg_s[:, j, :, 3:7], in_=PS(b)[:, 0:64].rearrange("p (b s) -> p b s", s=4), func=AF.Copy),
                reads=PSK(b), writes=["stg"]))
            for j in range(14):
                conv_silu(j, [stg_s[:, j, :, w:w + 4] for w in range(4)],
                          xc[:, j, 0:64].rearrange("p (b s) -> p b s", s=4),
                          lambda t: t[:, 0:64].rearrange("p (b s) -> p b s", s=4))
            for b in range(16):
                P.dma("sp", sst[:, :, :], sssm_in[b].rearrange("(a p) n -> p a n", p=128), "ssti", writes=["sst"])
                for a in range(6):
                    P.op("pe", lambda e, a=a: e.transpose(out=PS(0, 2)[:, a * 128:(a + 1) * 128], in_=sst[:, a, :],
                                                          identity=identf), reads=["sst", "cm"], writes=PSK(0, 2),
                         inc=(a == 5))
                P.op("dve", lambda e: e.tensor_copy(out=hT[:, :], in_=PS(0, 2)[:, 0:768]), reads=PSK(0, 2),
                     writes=["hT"])
                P.op("act", lambda e: e.activation(out=hTb[:, :], in_=hT[:, :], func=AF.Copy), reads=["hT"],
                     writes=["hTb"])
                ssd_chunk(4, lambda j, b=b: xc[:, j, 4 * b:4 * b + 4], ["xc"],
                          lambda kc, b=b: xnT[:, kc, TO + 4 * b:TO + 4 * b + 4], [("xnT", 16)], True, TO + 4 * b)
                state_out(ossm_o[b])
        if os.environ.get("KDBG_NOATT"):
            P.op("dve", lambda e: e.memset(junk[:, :], 0.0), writes=["junk"])
            for j in range(6):
                for (a0, a1) in ((0, 1024), (1024, 2048), (2048, TT)):
                    P.dma("sp", mixd[j, :, a0:a1], junk[:, 0:a1 - a0], "zbo", reads=["junk"])
        NOSSD = stage < 3
        if NOSSD:
            zt = sbt("zt", [128, 768], F32)
            P.op("dve", lambda e: e.memset(zt[:], 0.0), writes=["zt"])
            for i in range(6):
                P.dma("sp", pssm_o[i * 128:(i + 1) * 128, :], zt[:, 0:128], "zo", reads=["zt"])
            for b in range(16):
                P.dma("sp", ossm_o[b].rearrange("(a p) n -> p a n", p=128),
                      zt[:, 0:768].rearrange("p (a n) -> p a n", a=6), "zo", reads=["zt"])
        P.barrier()
        P.replay(nc, st1a, "a", st)
        st1a.close()
        if stage >= 4 and not os.environ.get("KDBG_NOATT"):
            P = Prog()
            st1b = ExitStack()
            cur[0] = st1b
            qT = sbt("qTz", [128, 2, 2, TT], BF16)
            kT = sbt("kT", [128, 2, TT], BF16)
            UTg = sbt("UTg", [128, 2, TT], BF16)
            Sm = [sbt("Sm%d" % i, [128, 2, 2, 256], F32) for i in range(3)]
            Pb = [sbt("Pb%d" % i, [128, 2, 2, 256], BF16) for i in range(3)]
            PT = [sbt("PT%d" % i, [128, 8, 128], BF16) for i in range(3)]
            mxt = [sbt("mxt%d" % i, [128, 4], F32) for i in range(3)]
            ngt = [sbt("ngt%d" % i, [128, 4], F32) for i in range(3)]
            dnt = [sbt("dnt%d" % i, [128, 4], F32) for i in range(3)]
            stt = [sbt("stt%d" % i, [128, 4, 2], F32) for i in range(3)]
            ckt = [sbt("ckt%d" % i, [128, 4, 512], BF16) for i in range(2)]
            kTc = [sbt("kTc%d" % i, [128, 4, 2, 128], BF16) for i in range(2)]
            Vnb = [sbt("Vnb%d" % i, [128, 256], BF16) for i in range(2)]
            for hh_ in range(2):
                for c_ in range(2):
                    P.op("pool", lambda e, hh_=hh_, c_=c_: e.memset(qT[:, hh_, c_, :], 0.0), writes=["qT"])
            arot = Rot(range(3))
            crot = Rot(range(2))
            pjrot = Rot([0, 1, 2, 3])

            def attn_block(nq, Wo, q_ap, kp_ap, ko_ap, vp_ap, vo_ap, mask_ap, ut_ap, stat_dst, rkeys):
                W = 128 + Wo
                sl = arot.next()
                bS = (2 * sl, 2 * sl + 1)
                bT, bO = 6, 7
                smk, pbk, ptk, stk = "Sm%d" % sl, "Pb%d" % sl, "PT%d" % sl, "stt%d" % sl
                for c in range(2):
                    for hh in range(2):
                        P.op("pe", lambda e, c=c, hh=hh: e.matmul(
                            out=PS(bS[c])[0:nq, hh * 256:hh * 256 + 128], lhsT=q_ap(c, hh), rhs=kp_ap(c, hh),
                            start=True, stop=True), reads=rkeys, writes=PSK(bS[c]), inc=False)
                        P.op("pe", lambda e, c=c, hh=hh: e.matmul(
                            out=PS(bS[c])[0:nq, hh * 256 + 128:hh * 256 + W], lhsT=q_ap(c, hh), rhs=ko_ap(c, hh),
                            start=True, stop=True), reads=rkeys, writes=PSK(bS[c]), inc=(hh == 1))
                yield
                for c in range(2):
                    Sv = PS(bS[c])[0:nq, :].rearrange("p (h w) -> p h w", h=2)[:, :, 0:W]
                    P.op("dve", lambda e, c=c, Sv=Sv: e.scalar_tensor_tensor(
                        out=Sm[sl][0:nq, c, :, 0:W], in0=Sv, scalar=SCALE,
                        in1=mask_ap.unsqueeze(1).broadcast_to([nq, 2, W]), op0=ALU.mult, op1=ALU.add),
                        reads=PSK(bS[c]) + ["maskA", "maskF", "maskS"], writes=[(smk, c)])
                    P.op("dve", lambda e, c=c: e.tensor_reduce(out=mxt[sl][0:nq, 2 * c:2 * c + 2],
                                                               in_=Sm[sl][0:nq, c, :, 0:W], axis=AX.X, op=ALU.max),
                         reads=[(smk, c)], writes=[(stk, "mx", c)])
                    P.op("dve", lambda e, c=c: e.tensor_scalar(out=ngt[sl][0:nq, 2 * c:2 * c + 2],
                                                               in0=mxt[sl][0:nq, 2 * c:2 * c + 2], scalar1=-1.0,
                                                               scalar2=None, op0=ALU.mult),
                         reads=[(stk, "mx", c)], writes=[(stk, "ng", c)])
                    for hh in range(2):
                        P.op("act", lambda e, c=c, hh=hh: e.activation(
                            out=Pb[sl][0:nq, c, hh, 0:W], in_=Sm[sl][0:nq, c, hh, 0:W], func=AF.Exp,
                            bias=ngt[sl][0:nq, 2 * c + hh:2 * c + hh + 1], scale=1.0,
                            accum_out=dnt[sl][0:nq, 2 * c + hh:2 * c + hh + 1]),
                            reads=[(smk, c), (stk, "ng", c)], writes=[(pbk, c, hh), (stk, "dn", c, hh)])
                yield
                ptv = PS(bT).bitcast(BF16).rearrange("p (a q) -> p a q", a=8)
                for c in range(2):
                    for hh in range(2):
                        a = (c * 2 + hh) * 2
                        P.op("pe", lambda e, c=c, hh=hh, a=a: e.transpose(
                            out=ptv[:, a, 0:nq], in_=Pb[sl][0:nq, c, hh, 0:128], identity=identb[0:nq, 0:nq]),
                            reads=[(pbk, c, hh), "identb"], writes=PSK(bT), inc=False)
                        P.op("pe", lambda e, c=c, hh=hh, a=a: e.transpose(
                            out=ptv[0:Wo, a + 1, 0:nq], in_=Pb[sl][0:nq, c, hh, 128:W], identity=identb[0:nq, 0:nq]),
                            reads=[(pbk, c, hh), "identb"], writes=PSK(bT), inc=(c == 1 and hh == 1))
                yield
                ptv4 = ptv.rearrange("p (x j) q -> p x j q", j=2)
                pt4 = PT[sl][:, :, :].rearrange("p (x j) q -> p x j q", j=2)
                P.op("dve", lambda e: e.tensor_copy(out=pt4[:, :, 0, 0:nq], in_=ptv4[:, :, 0, 0:nq]),
                     reads=PSK(bT), writes=[ptk])
                P.op("dve", lambda e: e.tensor_copy(out=pt4[0:Wo, :, 1, 0:nq], in_=ptv4[0:Wo, :, 1, 0:nq]),
                     reads=PSK(bT) + [ptk], writes=[ptk])
                yield
                pov = PS(bO)[:, :].rearrange("p (a q) -> p a q", a=4)
                for c in range(2):
                    for hh in range(2):
                        a = c * 2 + hh
                        P.op("pe", lambda e, c=c, a=a: e.matmul(out=pov[:, a, 0:nq], lhsT=vp_ap(c),
                                                                rhs=PT[sl][:, 2 * a, 0:nq], start=True, stop=False),
                             reads=rkeys + [ptk], writes=PSK(bO), inc=False)
                        P.op("pe", lambda e, c=c, a=a: e.matmul(out=pov[:, a, 0:nq], lhsT=vo_ap(c),
                                                                rhs=PT[sl][0:Wo, 2 * a + 1, 0:nq], start=False,
                                                                stop=True),
                             reads=rkeys + [ptk], writes=PSK(bO), inc=(a == 3))
                yield
                for c in range(2):
                    for hh in range(2):
                        P.op("act", lambda e, c=c, hh=hh: e.activation(
                            out=ut_ap(c, hh), in_=pov[hh * 64:(hh + 1) * 64, c * 2 + hh, 0:nq], func=AF.Copy),
                            reads=PSK(bO), writes=[("UTg", c, hh)])
                P.op("dve", lambda e: e.tensor_copy(out=stt[sl][0:nq, :, 0], in_=mxt[sl][0:nq, :]),
                     reads=[(stk, "mx", 0), (stk, "mx", 1)], writes=[stk])
                P.op("dve", lambda e: e.tensor_copy(out=stt[sl][0:nq, :, 1], in_=dnt[sl][0:nq, :]),
                     reads=[(stk, "dn", c, hh) for c in range(2) for hh in range(2)] + [stk], writes=[stk])
                P.dma("sp", stat_dst, stt[sl][0:nq, :, :], "stato%d" % sl, reads=[stk])
                yield

            utk = [("UTg", c, hh) for c in range(2) for hh in range(2)]
            KATT = int(os.environ.get("KATT", "9"))
            for g in range(3 if KATT >= 2 else 1):
                d = DIL[g]
                swq = wrot.next()
                wqk = "wb%d" % swq
                P.dma("pool", wb[swq][:, :, 0:256], w_in_v[:, :, Q0 + g * 256:Q0 + (g + 1) * 256], wqk, writes=[wqk])
                P.dma("pool", wb[swq][:, :, 256:512], w_in_v[:, :, K0 + g * 256:K0 + (g + 1) * 256], wqk, writes=[wqk])
                for (dst, dkey, co) in ((qT, "qT", 0), (kT, "kT", 256)):
                    for c in range(2):
                        for (a0, an) in ((0, 512), (512, 512), (1024, 512), (1536, 512), (2048, 64)):
                            b = pjrot.next()
                            for kc in range(8):
                                P.op("pe", lambda e, kc=kc, b=b, c=c, co=co, a0=a0, an=an, swq=swq: e.matmul(
                                    out=PS(b)[:, 0:an], lhsT=wb[swq][:, kc, co + c * 128:co + (c + 1) * 128],
                                    rhs=xnT[:, kc, a0:a0 + an], start=(kc == 0), stop=(kc == 7)),
                                    reads=[wqk], writes=PSK(b), inc=(kc == 7))
                            if dkey == "qT":
                                for hh in range(2):
                                    P.op("act", lambda e, b=b, c=c, hh=hh, a0=a0, an=an: e.activation(
                                        out=qT[hh * 64:(hh + 1) * 64, hh, c, a0:a0 + an],
                                        in_=PS(b)[hh * 64:(hh + 1) * 64, 0:an], func=AF.Copy),
                                        reads=PSK(b), writes=[dkey])
                            else:
                                P.op("act", lambda e, b=b, c=c, dst=dst, a0=a0, an=an: e.activation(
                                    out=dst[:, c, a0:a0 + an], in_=PS(b)[:, 0:an], func=AF.Copy),
                                    reads=PSK(b), writes=[dkey])
                swv = wrot.next()
                wvk = "wb%d" % swv
                P.dma("pool", wb[swv][:, :, 0:256], w_in_v[:, :, V0 + g * 256:V0 + (g + 1) * 256], wvk, writes=[wvk])
                rk = ["qT", "kT", "kTp"]
                gens = []
                for blk in range(16):
                    u, r = blk // d, blk % d
                    tk = block_tokens(g, blk)
                    if u == 0:
                        pi = r
                        ptk_ = slice(r, r + d * 127 + 1, d)
                        kp = (lambda c, hh, ptk_=ptk_, g=g: kTp[g][:, c, ptk_])
                        vp = (lambda c, pi=pi, g=g: Vprev[g][:, pi, c * 128:(c + 1) * 128])
                        mk = maskF[:, :]
                    else:
                        ptok = block_tokens(g, blk - d)
                        kp = (lambda c, hh, ptok=ptok: kT[:, c, ptok])
                        vp = (lambda c, g=g, pb=blk - d: Vtok[g][:, pb, c * 128:(c + 1) * 128])
                        mk = maskA[:, :]
                    gens.append(attn_block(
                        128, 128,
                        (lambda c, hh, tk=tk: qT[:, hh, c, tk]),
                        kp,
                        (lambda c, hh, tk=tk: kT[:, c, tk]),
                        vp,
                        (lambda c, g=g, blk=blk: Vtok[g][:, blk, c * 128:(c + 1) * 128]),
                        mk,
                        (lambda c, hh, tk=tk: UTg[hh * 64:(hh + 1) * 64, c, tk]),
                        stats_d[tk, 4 * g:4 * g + 4, :], rk + utk))
                run_pipelined(gens, 3)
                ns = 1 if d == 1 else 4
                for b in range(16 if KATT >= 3 else 0):
                    cs = crot.next()
                    ckk, kck, vnk = "ckt%d" % cs, "kTc%d" % cs, "Vnb%d" % cs
                    if d == 1:
                        srcv = ck[0][b].rearrange("(m s) c -> m s c", s=1)
                    elif d == 4:
                        srcv = ck[1][b].rearrange("(m s) c -> m s c", s=4)
                    else:
                        srcv = ck[2][b].rearrange("(m s) c -> m s c", s=16)[:, 0:4, :]
                    P.dma("pool", ckt[cs][:, 0:ns, :], srcv, ckk, writes=[ckk])
                    bq = pjrot.next()
                    kv8 = PS(bq).bitcast(BF16).rearrange("p (a q) -> p a q", a=8)
                    for si in range(ns):
                        for c in range(2):
                            P.op("pe", lambda e, si=si, c=c, cs=cs, kv8=kv8: e.transpose(
                                out=kv8[:, si * 2 + c, :], in_=ckt[cs][:, si, c * 128:(c + 1) * 128],
                                identity=identb[:, :]), reads=[ckk, "identb"], writes=PSK(bq),
                                inc=(si == ns - 1 and c == 1))
                    P.op("dve", lambda e, cs=cs, kv8=kv8, ns=ns: e.tensor_copy(
                        out=kTc[cs][:, 0:ns, :, :].rearrange("p s c q -> p (s c) q"), in_=kv8[:, 0:2 * ns, :]),
                        reads=PSK(bq), writes=[kck])
                    bv = pjrot.next()
                    for kc in range(8):
                        P.op("pe", lambda e, kc=kc, b=b, bv=bv, swv=swv: e.matmul(
                            out=PS(bv)[0:4, 0:256], lhsT=xnT[:, kc, TO + 4 * b:TO + 4 * b + 4],
                            rhs=wb[swv][:, kc, 0:256], start=(kc == 0), stop=(kc == 7)),
                            reads=[wvk], writes=PSK(bv), inc=(kc == 7))
                    P.op("act", lambda e, cs=cs, bv=bv: e.activation(out=Vnb[cs][0:4, :], in_=PS(bv)[0:4, 0:256], func=AF.Copy),
                         reads=PSK(bv), writes=[vnk])
                    gens = []
                    for si in range(ns):
                        if d == 1:
                            nq, t0q, mk = 4, TO + 4 * b, maskA[0:4, 0:132]
                        else:
                            nq, t0q, mk = 1, TO + 4 * b + si, maskS[0:1, si, 0:132]
                        gens.append(attn_block(
                            nq, 4,
                            (lambda c, hh, t0q=t0q, nq=nq: qT[:, hh, c, t0q:t0q + nq]),
                            (lambda c, hh, si=si, cs=cs: kTc[cs][:, si, c, :]),
                            (lambda c, hh, b=b: kT[:, c, TO + 4 * b:TO + 4 * b + 4]),
                            (lambda c, si=si, cs=cs: ckt[cs][:, si, 256 + c * 128:256 + (c + 1) * 128]),
                            (lambda c, cs=cs: Vnb[cs][0:4, c * 128:(c + 1) * 128]),
                            mk,
                            (lambda c, hh, t0q=t0q, nq=nq: UTg[hh * 64:(hh + 1) * 64, c, t0q:t0q + nq]),
                            stats_d[t0q:t0q + nq, 4 * g:4 * g + 4, :], rk + utk + [ckk, kck, vnk]))
                    run_pipelined(gens, 3)
                for c in range(2):
                    P.dma("sp", mixd[2 * g + c, :, :], UTg[:, c, :], "uto", reads=utk)
            P.barrier()
            P.replay(nc, st1b, "c", st)
            st1b.close()
        st1.close()

        P = Prog()
        st2 = ExitStack()
        cur[0] = st2
        w_out_v = w_out.rearrange("(k p) n -> p k n", p=128)
        w_gu_v = w_gu.rearrange("(k p) n -> p k n", p=128)
        w_down_v = w_down.rearrange("(k p) n -> p k n", p=128)
        wbig = sbt("wbig", [128, 22, 512], BF16)
        mixT = sbt("mixg", [128, 12, 1088], BF16)
        hbuf = sbt("hbuf", [128, 9, 1024], F32)
        hnT = sbt("hnT", [128, 8, 1088], BF16)
        aT = sbt("aT", [128, 22, 1088], BF16)
        sgt = [sbt("sgt%d" % i, [128, 512], BF16) for i in range(2)]
        yo = [sbt("yo0", [128, 1024], F32)] * 2
        fst = [sbt("fst%d" % i, [128, 4], F32) for i in range(2)]
        stmt = [sbt("stm%d" % i, [128, 12, 2], F32) for i in range(2)]
        Wtt = [sbt("Wt%d" % i, [128, 32], F32) for i in range(2)]
        Wxt = [sbt("Wx%d" % i, [128, 768], BF16) for i in range(2)]
        srot = Rot(range(2))
        prot2 = Rot([0, 1, 2, 3])
        grot = Rot([4, 5, 6, 7])
        groups = [(0, 1024), (1024, 1088)]
        def ffn_group(t0, ntok):
            tiles = [(i // 128, t0 + i, min(128, ntok - i)) for i in range(0, ntok, 128)]
            for (ti, tb, T) in tiles:
                src = xo[tb:tb + T, :] if tb < TO else xs[tb - TO:tb - TO + T, :]
                P.dma("sp", hbuf[0:T, ti, :], src, "hb%d" % ti, writes=[("hbuf", ti)])
            if stage >= 4:
                P.dma("sp", mixT[:, :, 0:ntok], mixd[:, :, t0:t0 + ntok].rearrange("j p t -> p j t"), "mixg",
                      writes=["mixT"])
            else:
                P.op("dve", lambda e: e.memset(mixT[:, :, :], 0.0), writes=["mixT"])
            if stage >= 4 and not os.environ.get("KDBG_NOATT") and int(os.environ.get("KATT", "9")) >= 4:
                for (ti, tb, T) in tiles:
                    ms_ = srot.next()
                    stm, Wt, Wx = stmt[ms_], Wtt[ms_], Wxt[ms_]
                    sk = "stm%d" % ms_
                    P.dma("sp", stm[0:T, :, :], stats_d[tb:tb + T, :, :], sk, writes=[sk])
                    m3 = stm[0:T, :, :].rearrange("p (g h) t -> p g h t", g=3)
                    mo = lambda fn: P.op("dve", fn, reads=[sk], writes=[sk])
                    W3 = Wt[0:T, 0:12].rearrange("p (g h) -> p g h", g=3)
                    E3 = Wt[0:T, 12:24].rearrange("p (g h) -> p g h", g=3)
                    Mx, Dn = Wt[0:T, 24:28], Wt[0:T, 28:32]
                    mo(lambda e, m3=m3, Mx=Mx: e.tensor_tensor(out=Mx, in0=m3[:, 0, :, 0], in1=m3[:, 1, :, 0], op=ALU.max))
                    mo(lambda e, m3=m3, Mx=Mx: e.tensor_tensor(out=Mx, in0=Mx, in1=m3[:, 2, :, 0], op=ALU.max))
                    mo(lambda e, m3=m3, Mx=Mx, E3=E3, T=T: e.tensor_tensor(
                        out=E3, in0=m3[:, :, :, 0], in1=Mx.unsqueeze(1).broadcast_to([T, 3, 4]), op=ALU.subtract))
                    P.op("act", lambda e, E3=E3: e.activation(out=E3, in_=E3, func=AF.Exp), reads=[sk], writes=[sk])
                    mo(lambda e, m3=m3, E3=E3, W3=W3: e.tensor_tensor(out=W3, in0=E3, in1=m3[:, :, :, 1], op=ALU.mult))
                    mo(lambda e, W3=W3, Dn=Dn: e.tensor_tensor(out=Dn, in0=W3[:, 0, :], in1=W3[:, 1, :], op=ALU.add))
                    mo(lambda e, W3=W3, Dn=Dn: e.tensor_tensor(out=Dn, in0=Dn, in1=W3[:, 2, :], op=ALU.add))
                    mo(lambda e, Dn=Dn: e.reciprocal(out=Dn, in_=Dn))
                    mo(lambda e, E3=E3, W3=W3, Dn=Dn, T=T: e.tensor_tensor(
                        out=W3, in0=E3, in1=Dn.unsqueeze(1).broadcast_to([T, 3, 4]), op=ALU.mult))
                    P.op("dve", lambda e, Wt=Wt, Wx=Wx, T=T: e.tensor_copy(
                        out=Wx[0:T, :].rearrange("p (a d) -> p a d", a=12),
                        in_=Wt[0:T, 0:12].unsqueeze(2).broadcast_to([T, 12, 64])), reads=[sk], writes=["Wx%d" % ms_])
                    bm = prot2.next()
                    pw = PS(bm).bitcast(BF16).rearrange("p (a q) -> p a q", a=8)
                    for j in range(6):
                        P.op("pe", lambda e, j=j, pw=pw, Wx=Wx, T=T: e.transpose(
                            out=pw[:, j, 0:T], in_=Wx[0:T, j * 128:(j + 1) * 128], identity=identb[0:T, 0:T]),
                            reads=["Wx%d" % ms_, "identb"], writes=PSK(bm), inc=(j == 5))
                    P.op("dve", lambda e, pw=pw, tb=tb, T=T: e.tensor_tensor(
                        out=mixT[:, 0:6, tb - t0:tb - t0 + T], in0=mixT[:, 0:6, tb - t0:tb - t0 + T],
                        in1=pw[:, 0:6, 0:T], op=ALU.mult), reads=PSK(bm) + ["mixT"], writes=["mixT"])
            for half in range(2):
                P.dma("pool", wbig[:, 0:6, :], w_out_v[:, 0:6, half * 512:(half + 1) * 512], "wbigA", writes=["wbigA"])
                P.dma("pool", wbig[:, 6:12, :], w_out_v[:, 6:12, half * 512:(half + 1) * 512], "wbigB", writes=["wbigB"])
                for (ti, tb, T) in tiles:
                    b = prot2.next()
                    for kc in range(12):
                        P.op("pe", lambda e, kc=kc, b=b, tb=tb, T=T: e.matmul(
                            out=PS(b)[0:T, :], lhsT=mixT[:, kc, tb - t0:tb - t0 + T], rhs=wbig[:, kc, :],
                            start=(kc == 0), stop=(kc == 11)), reads=["wbigA" if kc < 6 else "wbigB", "mixT"], writes=PSK(b), inc=(kc == 11))
                    P.op("dve", lambda e, b=b, ti=ti, T=T, half=half: e.tensor_tensor(
                        out=hbuf[0:T, ti, half * 512:(half + 1) * 512], in0=PS(b)[0:T, :],
                        in1=hbuf[0:T, ti, half * 512:(half + 1) * 512], op=ALU.add),
                        reads=PSK(b) + [("hbuf", ti)], writes=[("hbuf", ti)])
            for (ti, tb, T) in tiles:
                norm_T(hbuf[0:T, ti, :], ("hbuf", ti), T, gffn[:, :], "gffn", hnT[:, :, ti * 128:ti * 128 + T],
                       [("hnT", ti)], prot2.next(), from_dram=False)
            hk = [("hnT", ti) for (ti, _, _) in tiles]
            tranges = [(a0, min(512, ntok - a0)) for a0 in range(0, ntok, 512)]
            for j0 in range(0, 22, 2):
                sw = wrot.next()
                wkey = "wb%d" % sw
                P.dma("pool", wb[sw][:, :, 0:256], w_gu_v[:, :, j0 * 128:(j0 + 2) * 128], wkey, writes=[wkey])
                P.dma("pool", wb[sw][:, :, 256:512], w_gu_v[:, :, 2816 + j0 * 128:2816 + (j0 + 2) * 128], wkey,
                      writes=[wkey])
                for jj in range(2):
                    j = j0 + jj
                    for (a0, an) in tranges:
                        bg, bu = grot.next(), grot.next()
                        for (bb, c0) in ((bg, jj * 128), (bu, 256 + jj * 128)):
                            for kc in range(8):
                                P.op("pe", lambda e, kc=kc, bb=bb, c0=c0, sw=sw, a0=a0, an=an: e.matmul(
                                    out=PS(bb)[:, 0:an], lhsT=wb[sw][:, kc, c0:c0 + 128], rhs=hnT[:, kc, a0:a0 + an],
                                    start=(kc == 0), stop=(kc == 7)), reads=[wkey] + hk, writes=PSK(bb), inc=(kc == 7))
                        ss_ = srot.next()
                        P.op("act", lambda e, bg=bg, ss_=ss_, an=an: e.activation(
                            out=sgt[ss_][:, 0:an], in_=PS(bg)[:, 0:an], func=AF.Silu), reads=PSK(bg),
                            writes=["sgt%d" % ss_])
                        P.op("dve", lambda e, bu=bu, ss_=ss_, j=j, a0=a0, an=an: e.tensor_tensor(
                            out=aT[:, j, a0:a0 + an], in0=sgt[ss_][:, 0:an], in1=PS(bu)[:, 0:an], op=ALU.mult),
                            reads=PSK(bu) + ["sgt%d" % ss_], writes=[("aT", j)])
            ak = [("aT", j) for j in range(22)]
            for half in range(2):
                P.dma("pool", wbig[:, 0:6, :], w_down_v[:, 0:6, half * 512:(half + 1) * 512], "wbigA", writes=["wbigA"])
                P.dma("pool", wbig[:, 6:22, :], w_down_v[:, 6:22, half * 512:(half + 1) * 512], "wbigB", writes=["wbigB"])
                for (ti, tb, T) in tiles:
                    b = prot2.next()
                    for fc in range(22):
                        P.op("pe", lambda e, fc=fc, b=b, ti=ti, T=T: e.matmul(
                            out=PS(b)[0:T, :], lhsT=aT[:, fc, ti * 128:ti * 128 + T], rhs=wbig[:, fc, :],
                            start=(fc == 0), stop=(fc == 21)), reads=["wbigA" if fc < 6 else "wbigB"] + ak, writes=PSK(b), inc=(fc == 21))
                    P.op("dve", lambda e, b=b, ti=ti, T=T, half=half: e.tensor_tensor(
                        out=hbuf[0:T, ti, half * 512:(half + 1) * 512], in0=PS(b)[0:T, :],
                        in1=hbuf[0:T, ti, half * 512:(half + 1) * 512], op=ALU.add),
                        reads=PSK(b) + [("hbuf", ti)], writes=[("hbuf", ti)])
            for (ti, tb, T) in tiles:
                s_ = srot.next()
                fs, fk = fst[s_], "fst%d" % s_
                P.op("act", lambda e, ti=ti, T=T, fs=fs: e.activation(
                    out=junk[0:T, :], in_=hbuf[0:T, ti, :], func=AF.Square, scale=1.0 / 32, accum_out=fs[0:T, 0:1]),
                    reads=[("hbuf", ti)], writes=["junk", fk])
                P.op("dve", lambda e, T=T, fs=fs: e.tensor_scalar(out=fs[0:T, 1:2], in0=fs[0:T, 0:1], scalar1=EPS,
                                                                  scalar2=None, op0=ALU.add), reads=[fk], writes=[fk])
                P.op("act", lambda e, T=T, fs=fs: e.activation(out=fs[0:T, 2:3], in_=fs[0:T, 1:2], func=AF.Ln),
                     reads=[fk], writes=[fk])
                P.op("act", lambda e, T=T, fs=fs: e.activation(out=fs[0:T, 3:4], in_=fs[0:T, 2:3], func=AF.Exp,
                                                               scale=-0.5), reads=[fk], writes=[fk])
                P.op("dve", lambda e, ti=ti, T=T, fs=fs, s_=s_: e.scalar_tensor_tensor(
                    out=yo[s_][0:T, :], in0=hbuf[0:T, ti, :], scalar=fs[0:T, 3:4], in1=gfin[0:T, :],
                    op0=ALU.mult, op1=ALU.mult), reads=[("hbuf", ti), fk, "gfin"], writes=["yo0"])
                P.dma("sp", y_o[tb:tb + T, :], yo[s_][0:T, :], "yout0", reads=["yo0"])

        for (t0_, ntok_) in groups:
            ffn_group(t0_, ntok_)
        P.barrier()
        P.replay(nc, st2, "b", st)
        st2.close()
    return nc


_NC_CACHE = {}


def _consts():
    ident = np.eye(128, dtype=np.float32)
    i = np.arange(128)
    triu = (i[:, None] <= i[None, :]).astype(np.float32)
    sl = (i[:, None] > i[None, :]).astype(np.float32)
    ones = np.ones((128, 128), np.float32)
    cm = np.concatenate([ident, triu, sl, ones], axis=1)
    q = np.arange(128)[:, None]
    k = np.arange(256)[None, :]
    valid = (k >= q) & (k <= q + 128)
    maskA = np.where(valid, 0.0, NEG).astype(np.float32)
    maskS = np.full((4, 256), NEG, np.float32)
    for s in range(4):
        maskS[s, 0:128] = 0.0
        maskS[s, 128 + s] = 0.0
    return cm, maskA, maskS.reshape(1, 1024)


def kernel(x_prompt, x_sample, cache_kv_d1, cache_kv_d4, cache_kv_d16, state_conv, state_ssm,
           norm_mix, w_in, conv_w, conv_b, dt_bias, a_log, d_skip, norm_ssd, w_out, norm_ffn,
           w_gate_up, w_down, norm_final, _stage=99):
    f = lambda a: np.ascontiguousarray(np.asarray(a, dtype=np.float32))
    if _stage not in _NC_CACHE:
        _NC_CACHE[_stage] = build(_stage)
    nc = _NC_CACHE[_stage]
    cm, maskA, maskS = _consts()
    bc = lambda v, n: f(np.broadcast_to(np.asarray(v, np.float32).reshape(1, -1), (128, n)))
    col = lambda v, k: f(np.asarray(v, np.float32).reshape(k, 128).T)
    cw = np.asarray(conv_w, np.float32)[0]
    convw = f(cw.reshape(4, 14, 128).transpose(2, 1, 0).reshape(128, 56))
    shared = {
        "w_in": f(w_in[0]), "w_out": f(w_out[0]), "w_gu": f(w_gate_up[0]), "w_down": f(w_down[0]),
        "gmix": col(norm_mix[0], 8), "gffn": col(norm_ffn[0], 8), "gfin": bc(norm_final, 1024),
        "gssd": bc(norm_ssd[0], 768), "convw": convw, "convb": col(conv_b[0], 14),
        "dtb": bc(dt_bias[0], 12), "alog": bc(a_log[0], 12), "dsk": bc(d_skip[0], 12),
        "maskA": maskA, "maskS": maskS, "cmats": cm,
    }
    xpn = np.asarray(x_prompt, np.float32)
    xsn = np.asarray(x_sample, np.float32)
    caches = [np.asarray(cache_kv_d1, np.float32)[0], np.asarray(cache_kv_d4, np.float32)[0],
              np.asarray(cache_kv_d16, np.float32)[0]]
    sc = np.asarray(state_conv, np.float32)[0]
    ss = np.asarray(state_ssm, np.float32)[0]
    in_maps = []
    for c in range(8):
        b, half = c // 2, c % 2
        m = dict(shared)
        m["xo"] = f(xpn[b, half * TO:(half + 1) * TO])
        m["xp"] = f(xpn[b, 0:TO]) if half == 1 else np.zeros((TO, 1024), np.float32)
        m["xs"] = f(xsn[16 * c:16 * c + 16].reshape(64, 1024))
        for nm, ca, lb in zip(("ck1", "ck4", "ck16"), caches, (128, 512, 2048)):
            m[nm] = f(ca[16 * c:16 * c + 16].reshape(16, lb, 512))
        m["sconv"] = f(sc[16 * c:16 * c + 16].reshape(48, 1792))
        m["sssm"] = f(ss[16 * c:16 * c + 16].reshape(16, 768, 128))
        mf = maskA.copy()
        if half == 0:
            mf[:, 0:128] = NEG
        m["maskF"] = mf
        m["hp"] = np.full((128, 1), float(half), np.float32)
        in_maps.append(m)
    res = run_bass_kernel_spmd(nc, in_maps, core_ids=list(range(8))).results
    y_prompt = np.zeros((4, 4096, 1024), np.float32)
    y_sample = np.zeros((128, 4, 1024), np.float32)
    for c in range(8):
        b, half = c // 2, c % 2
        y_prompt[b, half * TO:(half + 1) * TO] = res[c]["y"][0:TO]
        y_sample[16 * c:16 * c + 16] = res[c]["y"][TO:TT].reshape(16, 4, 1024)
    odd = [1, 3, 5, 7]
    p_kv = [np.stack([res[c][n] for c in odd]).reshape(1, 4, lb, 2, 4, 64)
            for n, lb in (("pkv1", 128), ("pkv4", 512), ("pkv16", 2048))]
    p_conv = np.stack([res[c]["pconv"] for c in odd]).reshape(1, 4, 3, 1792)
    p_ssm = np.stack([res[c]["pssm"] for c in odd]).reshape(1, 4, 12, 64, 128)
    s_kv = [np.concatenate([res[c][n] for c in range(8)]).reshape(1, 128, lb, 2, 4, 64)
            for n, lb in (("skv1", 128), ("skv4", 512), ("skv16", 2048))]
    s_conv = np.concatenate([res[c]["oconv"] for c in range(8)]).reshape(1, 128, 3, 1792)
    s_ssm = np.concatenate([res[c]["ossm"] for c in range(8)]).reshape(1, 128, 12, 64, 128)
    return (y_prompt, y_sample, p_kv[0], p_kv[1], p_kv[2], p_conv, p_ssm,
            s_kv[0], s_kv[1], s_kv[2], s_conv, s_ssm)
```
